# Optimizing a Trainium2 kernel written in Bass

```python
import jax
import jax.numpy as jnp
from jax import lax
import numpy as np

D_MODEL = 2048
BATCH = 8
SEQ = 2048
DEPTH = 4

HEAD_DIM = 128
ROPE_THETA = 10000.0
NORM_EPS = 1e-6
D_FF = 5632
HALF_STEP = 0.5
N_EVEN = (DEPTH + 1) // 2
N_ODD = DEPTH // 2

A_HEADS = 12
A_WIDTH = A_HEADS * HEAD_DIM
A_PATTERNS = ((128, 1), (512, 4), (2048, 16))
A_BLOCK = 128
B_WINDOWS = (2, 4, 8, 16)
B_GROUPS = 4
B_GROUP_DIM = 128
B_WIDTH = B_GROUPS * B_GROUP_DIM
EVEN_IN = 3 * A_WIDTH + B_WIDTH
EVEN_MIX = A_WIDTH + B_WIDTH

C_HEADS = 12
C_KV_HEADS = 2
C_WIDTH = C_HEADS * HEAD_DIM
C_KV_WIDTH = C_KV_HEADS * HEAD_DIM
C_BRANCHES = 3
CMP_BLOCK = 32
CMP_STRIDE = 16
CMP_HIDDEN = 256
SLC_BLOCK = 64
SLC_TOPN = 8
WIN_SIZE = 512
WIN_BLOCK = 128
SLC_QUERY_CHUNK = 64
FORCED_SCORE = 1e9
D_WIDTH = 512
CONV_WIDTH = 3
ODD_IN = C_WIDTH + 6 * C_KV_WIDTH + C_BRANCHES * C_HEADS + 3 * D_WIDTH
ODD_MIX = C_WIDTH + D_WIDTH

kernel_name = 'hybrid_dilated_pool_nsa_shortconv_macaron'


def rms_norm(x, w):
    xf = x.astype(jnp.float32)
    y = xf * lax.rsqrt(jnp.mean(xf * xf, axis=-1, keepdims=True) + NORM_EPS)
    return (y * w.astype(jnp.float32)).astype(x.dtype)


def swiglu(x, w_gate, w_up, w_down):
    return (jax.nn.silu(x @ w_gate) * (x @ w_up)) @ w_down


def to_heads(t, n_heads):
    b, s, _ = t.shape
    return t.reshape(b, s, n_heads, HEAD_DIM).transpose(0, 2, 1, 3)


def from_heads(t):
    b, h, s, d = t.shape
    return t.transpose(0, 2, 1, 3).reshape(b, s, h * d)


def rope_tables(positions):
    inv_freq = 1.0 / (ROPE_THETA ** (jnp.arange(0, HEAD_DIM, 2, dtype=jnp.float32) / HEAD_DIM))
    ang = positions.astype(jnp.float32)[..., None] * inv_freq
    return jnp.cos(ang), jnp.sin(ang)


def apply_rope(t, cos, sin):
    t1, t2 = jnp.split(t.astype(jnp.float32), 2, axis=-1)
    c, s = cos[:, None], sin[:, None]
    return jnp.concatenate([t1 * c - t2 * s, t1 * s + t2 * c], axis=-1).astype(t.dtype)


def banded_causal_attention(q, k, v, n_back, blk):
    L, hd = q.shape[-2], q.shape[-1]
    nb = -(-L // blk)
    lp = nb * blk
    n_prev = -(-n_back // blk)
    qp = jnp.pad(q, [(0, 0)] * (q.ndim - 2) + [(0, lp - L), (0, 0)])
    kv_pad = [(0, 0)] * (k.ndim - 2) + [(n_prev * blk, lp - L), (0, 0)]
    kp = jnp.pad(k, kv_pad).reshape(k.shape[:-2] + (nb + n_prev, blk, hd))
    vp = jnp.pad(v, kv_pad).reshape(v.shape[:-2] + (nb + n_prev, blk, hd))
    qb = qp.reshape(q.shape[:-2] + (nb, blk, hd))
    kb = jnp.concatenate([kp[..., j:j + nb, :, :] for j in range(n_prev + 1)], axis=-2)
    vb = jnp.concatenate([vp[..., j:j + nb, :, :] for j in range(n_prev + 1)], axis=-2)
    span = (n_prev + 1) * blk
    qi = jnp.arange(blk)[:, None]
    kj = jnp.arange(span)[None, :]
    dist = qi - kj + n_prev * blk
    key_pos = (jnp.arange(nb)[:, None, None] - n_prev) * blk + kj[None]
    mask = (dist >= 0) & (dist <= n_back) & (key_pos >= 0)
    s = jnp.einsum('...hnqd,...nkd->...hnqk', qb, kb, preferred_element_type=jnp.float32) * (hd ** -0.5)
    s = jnp.where(mask, s, -jnp.inf)
    m = jnp.max(s, axis=-1, keepdims=True)
    p = jnp.exp(s - m)
    den = jnp.sum(p, axis=-1, keepdims=True)
    o = jnp.einsum('...hnqk,...nkd->...hnqd', p, vb.astype(jnp.float32)) / den
    lse = (m + jnp.log(den))[..., 0]
    o = o.reshape(o.shape[:-3] + (lp, hd))[..., :L, :]
    lse = lse.reshape(lse.shape[:-2] + (lp,))[..., :L]
    return o, lse


def dilated_attention(q, k, v):
    b, h, s, hd = q.shape
    outs, lses = [], []
    for window, dil in A_PATTERNS:
        L = s // dil
        def strided(t):
            return t.reshape(b, h, L, dil, hd).transpose(0, 1, 3, 2, 4)
        o, lse = banded_causal_attention(strided(q)[..., None, :, :], strided(k), strided(v), window // dil, A_BLOCK)
        outs.append(o[:, :, :, 0].transpose(0, 1, 3, 2, 4).reshape(b, h, s, hd))
        lses.append(lse[:, :, :, 0].transpose(0, 1, 3, 2).reshape(b, h, s))
    weights = jax.nn.softmax(jnp.stack(lses, axis=0), axis=0)
    return jnp.sum(weights[..., None] * jnp.stack(outs, axis=0), axis=0)


def multiscale_pool(u, pool_w, pool_scale):
    b, s, _ = u.shape
    uf = u.astype(jnp.float32).reshape(b, s, B_GROUPS, B_GROUP_DIM)
    cs = jnp.cumsum(uf, axis=1)
    pos1 = jnp.arange(1, s + 1, dtype=jnp.float32)
    groups = []
    for g, w in enumerate(B_WINDOWS):
        c = cs[:, :, g]
        prev = jnp.pad(c, ((0, 0), (w, 0), (0, 0)))[:, :s]
        cnt = jnp.minimum(pos1, float(w))
        groups.append((c - prev) / cnt[None, :, None] - uf[:, :, g])
    pooled = jnp.stack(groups, axis=2)
    mixed = jnp.einsum('bsgc,gcd->bsgd', pooled, pool_w.astype(jnp.float32))
    return mixed.reshape(b, s, B_WIDTH) * pool_scale.astype(jnp.float32)


def even_mixer(hn, cos, sin, w_in, w_out, pool_w, pool_scale):
    proj = hn @ w_in
    q, k, v, u = jnp.split(proj, [A_WIDTH, 2 * A_WIDTH, 3 * A_WIDTH], axis=-1)
    q = apply_rope(to_heads(q, A_HEADS), cos, sin)
    k = apply_rope(to_heads(k, A_HEADS), cos, sin)
    o_a = dilated_attention(q, k, to_heads(v, A_HEADS))
    o_b = multiscale_pool(u, pool_w, pool_scale)
    mixed = jnp.concatenate([from_heads(o_a), o_b], axis=-1).astype(hn.dtype)
    return mixed @ w_out


def compress_blocks(t, pe, w1, w2):
    b, g, s, hd = t.shape
    r = t.reshape(b, g, s // CMP_STRIDE, CMP_STRIDE, hd)
    n_sub = CMP_BLOCK // CMP_STRIDE
    n_blk = s // CMP_STRIDE - n_sub + 1
    blocks = jnp.concatenate([r[:, :, j:j + n_blk] for j in range(n_sub)], axis=3) + pe
    flat = blocks.reshape(b, g, n_blk, CMP_BLOCK * hd)
    return jax.nn.gelu(flat @ w1) @ w2


def nsa_attention(q, kc, vc, ks, vs, kw, vw, gate_logits, cos, sin, pe_k, w1_k, w2_k, pe_v, w1_v, w2_v):
    b, h, s, hd = q.shape
    g = C_KV_HEADS
    hg = h // g
    scale = hd ** -0.5
    t = jnp.arange(s)

    k_cmp = compress_blocks(kc, pe_k, w1_k, w2_k)
    v_cmp = compress_blocks(vc, pe_v, w1_v, w2_v)
    n_cmp = k_cmp.shape[2]
    qg = q.reshape(b, g, hg, s, hd)
    sc = jnp.einsum('bghsd,bgnd->bghsn', qg, k_cmp, preferred_element_type=jnp.float32) * scale
    cmp_start = jnp.arange(n_cmp) * CMP_STRIDE
    cmask = (cmp_start[None, :] + CMP_BLOCK - 1) <= t[:, None]
    sc = jnp.where(cmask, sc, -jnp.inf)
    m = jnp.max(sc, axis=-1, keepdims=True)
    m = jnp.where(jnp.isfinite(m), m, 0.0)
    p_cmp = jnp.exp(sc - m)
    den = jnp.sum(p_cmp, axis=-1, keepdims=True)
    p_cmp = p_cmp / jnp.maximum(den, 1.0)
    o_cmp = jnp.einsum('bghsn,bgnd->bghsd', p_cmp, v_cmp.astype(jnp.float32))

    n_slc = s // SLC_BLOCK
    slc_start = jnp.arange(n_slc) * SLC_BLOCK
    cover = ((cmp_start[:, None] < slc_start[None, :] + SLC_BLOCK)
             & (cmp_start[:, None] + CMP_BLOCK > slc_start[None, :])).astype(jnp.float32)
    imp = jnp.einsum('bghsn,nj->bgsj', p_cmp, cover)
    cur = t // SLC_BLOCK
    jj = jnp.arange(n_slc)
    visible = jj[None, :] <= cur[:, None]
    forced = visible & ((jj[None, :] == 0) | (jj[None, :] >= cur[:, None] - 1))
    score = jnp.where(forced, FORCED_SCORE, jnp.where(visible, imp, -FORCED_SCORE))
    top_n = min(SLC_TOPN, n_slc)
    _, sel = lax.top_k(score, top_n)

    q_rot = apply_rope(q, cos, sin).reshape(b, g, hg, s, hd)
    ks_blocks = apply_rope(ks, cos, sin).reshape(b, g, n_slc, SLC_BLOCK, hd)
    vs_blocks = vs.reshape(b, g, n_slc, SLC_BLOCK, hd)
    gather = jax.vmap(jax.vmap(lambda blocks, idx: blocks[idx]))

    def sel_chunk(args):
        qc, ic, tc = args
        kg = gather(ks_blocks, ic)
        vg = gather(vs_blocks, ic)
        s_ = jnp.einsum('bghcd,bgcnld->bghcnl', qc, kg, preferred_element_type=jnp.float32) * scale
        key_pos = ic[..., None] * SLC_BLOCK + jnp.arange(SLC_BLOCK)
        smask = key_pos <= tc[None, None, :, None, None]
        s_ = jnp.where(smask[:, :, None], s_, -jnp.inf)
        pr = jax.nn.softmax(s_.reshape(s_.shape[:4] + (-1,)), axis=-1).reshape(s_.shape)
        return jnp.einsum('bghcnl,bgcnld->bghcd', pr, vg.astype(jnp.float32))

    n_chunk = s // SLC_QUERY_CHUNK
    q_chunks = jnp.moveaxis(q_rot.reshape(b, g, hg, n_chunk, SLC_QUERY_CHUNK, hd), 3, 0)
    i_chunks = jnp.moveaxis(sel.reshape(b, g, n_chunk, SLC_QUERY_CHUNK, top_n), 2, 0)
    t_chunks = t.reshape(n_chunk, SLC_QUERY_CHUNK)
    o_slc = lax.map(sel_chunk, (q_chunks, i_chunks, t_chunks))
    o_slc = jnp.moveaxis(o_slc, 0, 3).reshape(b, g, hg, s, hd)

    o_win, _ = banded_causal_attention(q_rot, apply_rope(kw, cos, sin), vw, WIN_SIZE - 1, WIN_BLOCK)

    gates = jax.nn.sigmoid(gate_logits.astype(jnp.float32)).reshape(b, s, h, C_BRANCHES)
    gates = gates.transpose(0, 2, 1, 3).reshape(b, g, hg, s, C_BRANCHES)
    o = gates[..., 0:1] * o_cmp + gates[..., 1:2] * o_slc + gates[..., 2:3] * o_win
    return o.reshape(b, h, s, hd)


def short_conv_mixer(u_in, c_gate, b_gate, conv_w):
    s = u_in.shape[1]
    u = c_gate * u_in
    up = jnp.pad(u, ((0, 0), (CONV_WIDTH - 1, 0), (0, 0)))
    conv = up[:, 0:s] * conv_w[0]
    for j in range(1, CONV_WIDTH):
        conv = conv + up[:, j:j + s] * conv_w[j]
    return b_gate * conv


def odd_mixer(hn, cos, sin, w_in, w_out, pe_k, w1_k, w2_k, pe_v, w1_v, w2_v, conv_w):
    proj = hn @ w_in
    cuts = np.cumsum([C_WIDTH] + [C_KV_WIDTH] * 6 + [C_BRANCHES * C_HEADS, D_WIDTH, D_WIDTH]).tolist()
    q, kc, vc, ks, vs, kw, vw, gl, u_in, c_gate, b_gate = jnp.split(proj, cuts, axis=-1)
    kvh = lambda t: to_heads(t, C_KV_HEADS)
    o_c = nsa_attention(to_heads(q, C_HEADS), kvh(kc), kvh(vc), kvh(ks), kvh(vs), kvh(kw), kvh(vw), gl,
                        cos, sin, pe_k, w1_k, w2_k, pe_v, w1_v, w2_v)
    y_d = short_conv_mixer(u_in, c_gate, b_gate, conv_w)
    mixed = jnp.concatenate([from_heads(o_c).astype(hn.dtype), y_d.astype(hn.dtype)], axis=-1)
    return mixed @ w_out


def setup_inputs(seed: int = 0) -> dict:
    key = jax.random.key(seed)
    k = jax.random.split(key, 19)

    def normal(kk, shape, fan_in):
        return jax.random.normal(kk, shape, jnp.float32) * (float(fan_in) ** -0.5)

    x = jax.random.normal(k[0], (BATCH, SEQ, D_MODEL), jnp.float32)
    offsets = jax.random.randint(k[1], (BATCH, 1), 0, 1024, dtype=jnp.int32)
    positions = offsets + jnp.arange(SEQ, dtype=jnp.int32)[None, :]
    norm_w = 1.0 + 0.02 * jax.random.normal(k[2], (DEPTH, 6, D_MODEL), jnp.float32)
    ffn_w_gate = normal(k[3], (DEPTH, 2, D_MODEL, D_FF), D_MODEL)
    ffn_w_up = normal(k[4], (DEPTH, 2, D_MODEL, D_FF), D_MODEL)
    ffn_w_down = normal(k[5], (DEPTH, 2, D_FF, D_MODEL), D_FF)
    ev_w_in = normal(k[6], (N_EVEN, D_MODEL, EVEN_IN), D_MODEL)
    ev_w_out = normal(k[7], (N_EVEN, EVEN_MIX, D_MODEL), EVEN_MIX)
    pool_w = normal(k[8], (N_EVEN, B_GROUPS, B_GROUP_DIM, B_GROUP_DIM), B_GROUP_DIM)
    pool_scale = 1.0 + 0.1 * jax.random.normal(k[9], (N_EVEN, B_WIDTH), jnp.float32)
    od_w_in = normal(k[10], (N_ODD, D_MODEL, ODD_IN), D_MODEL)
    od_w_out = normal(k[11], (N_ODD, ODD_MIX, D_MODEL), ODD_MIX)
    cmp_pe_k = 0.1 * jax.random.normal(k[12], (N_ODD, CMP_BLOCK, HEAD_DIM), jnp.float32)
    cmp_w1_k = normal(k[13], (N_ODD, CMP_BLOCK * HEAD_DIM, CMP_HIDDEN), CMP_BLOCK * HEAD_DIM)
    cmp_w2_k = normal(k[14], (N_ODD, CMP_HIDDEN, HEAD_DIM), CMP_HIDDEN)
    cmp_pe_v = 0.1 * jax.random.normal(k[15], (N_ODD, CMP_BLOCK, HEAD_DIM), jnp.float32)
    cmp_w1_v = normal(k[16], (N_ODD, CMP_BLOCK * HEAD_DIM, CMP_HIDDEN), CMP_BLOCK * HEAD_DIM)
    cmp_w2_v = normal(k[17], (N_ODD, CMP_HIDDEN, HEAD_DIM), CMP_HIDDEN)
    conv_w = normal(k[18], (N_ODD, CONV_WIDTH, D_WIDTH), CONV_WIDTH)
    return {'x': x, 'positions': positions, 'norm_w': norm_w,
            'ffn_w_gate': ffn_w_gate, 'ffn_w_up': ffn_w_up, 'ffn_w_down': ffn_w_down,
            'ev_w_in': ev_w_in, 'ev_w_out': ev_w_out, 'pool_w': pool_w, 'pool_scale': pool_scale,
            'od_w_in': od_w_in, 'od_w_out': od_w_out,
            'cmp_pe_k': cmp_pe_k, 'cmp_w1_k': cmp_w1_k, 'cmp_w2_k': cmp_w2_k,
            'cmp_pe_v': cmp_pe_v, 'cmp_w1_v': cmp_w1_v, 'cmp_w2_v': cmp_w2_v,
            'conv_w': conv_w}


def reference(x, positions, norm_w, ffn_w_gate, ffn_w_up, ffn_w_down, ev_w_in, ev_w_out, pool_w, pool_scale,
              od_w_in, od_w_out, cmp_pe_k, cmp_w1_k, cmp_w2_k, cmp_pe_v, cmp_w1_v, cmp_w2_v, conv_w):
    cos, sin = rope_tables(positions)
    h = x
    for layer in range(DEPTH):
        nw = norm_w[layer]
        f = swiglu(rms_norm(h, nw[0]), ffn_w_gate[layer, 0], ffn_w_up[layer, 0], ffn_w_down[layer, 0])
        h = h + HALF_STEP * rms_norm(f, nw[1])
        hn = rms_norm(h, nw[2])
        i = layer // 2
        if layer % 2 == 0:
            m = even_mixer(hn, cos, sin, ev_w_in[i], ev_w_out[i], pool_w[i], pool_scale[i])
        else:
            m = odd_mixer(hn, cos, sin, od_w_in[i], od_w_out[i], cmp_pe_k[i], cmp_w1_k[i], cmp_w2_k[i],
                          cmp_pe_v[i], cmp_w1_v[i], cmp_w2_v[i], conv_w[i])
        h = h + rms_norm(m, nw[3])
        f = swiglu(rms_norm(h, nw[4]), ffn_w_gate[layer, 1], ffn_w_up[layer, 1], ffn_w_down[layer, 1])
        h = h + HALF_STEP * rms_norm(f, nw[5])
    return h
```

```python
import numpy as np
import ml_dtypes
from contextlib import ExitStack
import concourse.bass as bass
import concourse.mybir as mybir
from concourse.bass_utils import run_bass_kernel_spmd

F32 = mybir.dt.float32
BF16 = mybir.dt.bfloat16
I32 = mybir.dt.int32
AF = mybir.ActivationFunctionType
ALU = mybir.AluOpType

S_ = 2048
D_ = 2048
FF = 5632
NFC = FF // 128
EPS = 1e-6
SCALE = 128 ** -0.5
TWO_PI = 6.283185307179586
PI = 3.141592653589793


class Sem:
    __slots__ = ("h", "issued", "dma")

    def __init__(self, h, dma):
        self.h = h
        self.issued = 0
        self.dma = dma


class Buf:
    __slots__ = ("name", "w", "r", "dsem", "excl")

    def __init__(self, name="", excl=False):
        self.name = name
        self.w = None
        self.r = {}
        self.dsem = None
        self.excl = excl


def PB():
    return Buf("psum", True)


class Eng:
    def __init__(self, name, h, sem):
        self.name = name
        self.h = h
        self.sem = sem
        self.known = {}
        self.pending = False


class Sched:
    def __init__(self, nc, es, n_dsem=90):
        self.nc = nc
        mk = lambda nm, dma: Sem(es.enter_context(nc.semaphore(nm)), dma)
        self.engs = {
            "pe": Eng("pe", nc.tensor, mk("s_pe", False)),
            "act": Eng("act", nc.scalar, mk("s_act", False)),
            "dve": Eng("dve", nc.vector, mk("s_dve", False)),
            "pool": Eng("pool", nc.gpsimd, mk("s_pool", False)),
            "sp": Eng("sp", nc.sync, mk("s_sp", False)),
        }
        self.dsems = [mk("s_d%d" % i, True) for i in range(n_dsem)]
        self.dnext = 0
        self.n_ins = 0
        self.n_wait = 0

    def _need(self, need, tok):
        if tok is None:
            return
        sem, val = tok
        if sem.dma:
            val = sem.issued
        if need.get(sem, 0) < val:
            need[sem] = val

    def _sync(self, E, reads, writes, skip_w_sem=None):
        need = {}
        for b in reads:
            self._need(need, b.w)
            if b.excl:
                for sem, val in b.r.items():
                    if sem is not E.sem:
                        self._need(need, (sem, val))
        for b in writes:
            if not (skip_w_sem is not None and b.w is not None and b.w[0] is skip_w_sem):
                self._need(need, b.w)
            for sem, val in b.r.items():
                self._need(need, (sem, val))
        for sem, val in need.items():
            if sem is E.sem and E.name == "pe":
                continue
            if E.known.get(sem, 0) < val:
                E.h.wait_ge(sem.h, val)
                E.known[sem] = val
                self.n_wait += 1

    def op(self, eng, fn, reads=(), writes=(), inc=True):
        E = self.engs[eng]
        self._sync(E, reads, writes)
        ins = fn(E.h)
        self.n_ins += 1
        val = E.sem.issued + 1
        if inc:
            ins.then_inc(E.sem.h, 1)
            E.sem.issued = val
            E.pending = False
        else:
            E.pending = True
        for b in reads:
            if b.r.get(E.sem, 0) < val:
                b.r[E.sem] = val
        for b in writes:
            b.w = (E.sem, val)
            b.r = {}
        return ins

    def dma(self, q, out, in_, src, dst, **kw):
        E = self.engs[q]
        if dst.dsem is None:
            dst.dsem = self.dsems[self.dnext % len(self.dsems)]
            self.dnext += 1
        ds = dst.dsem
        self._sync(E, [src] if src is not None else [], [dst], skip_w_sem=ds)
        ins = E.h.dma_start(out=out, in_=in_, **kw)
        ins.then_inc(ds.h, 16)
        ds.issued += 16
        self.n_ins += 1
        dst.w = (ds, ds.issued)
        dst.r = {}
        if src is not None:
            src.r[ds] = ds.issued
        return ins

    def barrier(self):
        sp = self.engs["sp"]
        for E in self.engs.values():
            assert not E.pending, E.name
        allsems = [E.sem for E in self.engs.values() if E is not sp] + self.dsems
        for sem in allsems:
            if sp.known.get(sem, 0) < sem.issued:
                sp.h.wait_ge(sem.h, sem.issued)
                sp.known[sem] = sem.issued
        sp.h.nop().then_inc(sp.sem.h, 1)
        sp.sem.issued += 1
        for E in self.engs.values():
            if E is sp:
                continue
            E.h.wait_ge(sp.sem.h, sp.sem.issued)
            for sem in allsems + [sp.sem]:
                E.known[sem] = sem.issued
        sp.known[sp.sem] = sp.sem.issued


def host_consts():
    c = {}
    c["ident"] = np.eye(128, dtype=np.float32).astype(ml_dtypes.bfloat16)
    inv_freq = (1.0 / (10000.0 ** (np.arange(0, 128, 2, dtype=np.float32) / np.float32(128)))).astype(np.float32)
    c["invfreq"] = np.concatenate([inv_freq, inv_freq])[:, None].astype(np.float32)
    c["sinsign"] = np.concatenate([-np.ones(64), np.ones(64)])[:, None].astype(np.float32)
    kk = np.arange(128)[:, None]; qq = np.arange(128)[None, :]
    m_diag = (kk <= qq); m_prev = (kk >= qq); m_far = (kk > qq)
    bf = ml_dtypes.bfloat16
    c["mk_ev"] = np.concatenate([np.where(m_diag, 0.0, -30000.0), np.where(m_prev, 0.0, -30000.0)], axis=1).astype(np.float32).astype(bf)
    c["ones_bf"] = np.ones((128, 128), np.float32).astype(bf)
    invc = np.zeros((4, 16), np.float32)
    for g, w in enumerate((2, 4, 8, 16)):
        invc[g] = 1.0 / np.minimum(np.arange(1, 17), w)
    c["invc"] = np.broadcast_to(invc.reshape(1, 64), (128, 64)).copy()
    NEG = -30000.0
    b_diag = np.where(m_diag, 0.0, NEG); b_far = np.where(m_far, 0.0, NEG)
    full = np.zeros((128, 128)); none = np.full((128, 128), NEG)
    wb = np.zeros((128, 8, 4, 128), np.float32)
    for ai, a in enumerate(range(-4, 4)):
        for b in range(4):
            rel = b - a
            wb[:, ai, b, :] = none if (rel < 0 or rel > 4) else (b_diag if rel == 0 else (b_far if rel == 4 else full))
    c["wbias"] = wb.reshape(128, 8, 512).astype(bf)
    n_ = np.arange(128)[:, None]; t_ = np.arange(2048)[None, :]
    c["cbias"] = np.where(t_ >= 16 * n_ + 31, 0.0, NEG).astype(np.float32).astype(bf)
    j_ = np.arange(32)[None, :]
    c["cover"] = ((16 * n_ < 64 * j_ + 64) & (16 * n_ + 32 > 64 * j_)).astype(np.float32)
    t3 = (np.arange(16)[None, :, None] * 128 + np.arange(128)[:, None, None])
    cur = t3 // 64; j3 = np.arange(32)[None, None, :]
    vis = j3 <= cur; forced = vis & ((j3 == 0) | (j3 >= cur - 1))
    c["nf01"] = (vis & ~forced).astype(np.float32).reshape(128, 512)
    c["addc"] = np.where(forced, 1e9, np.where(vis, 0.0, -1e9)).astype(np.float32).reshape(128, 512)
    e_all = np.zeros((32, 16, 128), np.float32)
    for kb in range(16):
        e_all[2 * kb, kb, 0:64] = 1.0; e_all[2 * kb + 1, kb, 64:128] = 1.0
    c["e_all"] = e_all.astype(bf)
    return c


class K:
    def __init__(self, cfg):
        self.cfg = cfg
        nc = self.nc = bass.Bass("TRN2", target_bir_lowering=False)
        self.es = ExitStack()
        self.S = Sched(nc, self.es)
        dt = lambda name, shape, dtype=F32, kind="ExternalInput": nc.dram_tensor(name, shape, dtype, kind=kind).ap()
        self.x = dt("x", [S_, D_])
        self.pos = dt("pos", [1, S_], I32)
        self.norm_w = dt("norm_w", [24, D_])
        self.wg = dt("ffn_w_gate", [8, D_, FF])
        self.wu = dt("ffn_w_up", [8, D_, FF])
        self.wd = dt("ffn_w_down", [8, FF, D_])
        self.ident_d = dt("ident", [128, 128], BF16)
        self.invfreq_d = dt("invfreq", [128, 1])
        self.sinsign_d = dt("sinsign", [128, 1])
        self.ev_w_in = dt("ev_w_in", [2, D_, 5120])
        self.ev_w_out = dt("ev_w_out", [2, D_, D_])
        self.pool_w = dt("pool_w", [2, 4, 128, 128])
        self.pool_scale_t = dt("pool_scale_t", [2, 128, 4])
        self.mk_ev_d = dt("mk_ev", [128, 256], BF16)
        self.ones_d = dt("ones_bf", [128, 128], BF16)
        self.invc_d = dt("invc", [128, 64])
        self.rope_d = dt("rope_scr", [2, 128, S_], F32, kind="ExternalOutput")
        self.mixT_d = dt("mixT_scr", [16, 128, S_], BF16, kind="ExternalOutput")
        self.od_w_in = dt("od_w_in", [2, D_, 4644])
        self.od_w_out = dt("od_w_out", [2, D_, D_])
        self.cmp_w1 = [dt("cmp_w1_k", [2, 4096, 256]), dt("cmp_w1_v", [2, 4096, 256])]
        self.cmp_w2 = [dt("cmp_w2_k", [2, 256, 128]), dt("cmp_w2_v", [2, 256, 128])]
        self.cmp_peT = [dt("cmp_peT_k", [2, 128, 32]), dt("cmp_peT_v", [2, 128, 32])]
        self.conv_wT = dt("conv_wT", [2, 128, 12])
        self.wbias_d = dt("wbias", [128, 8, 512], BF16)
        self.cbias_d = dt("cbias", [128, S_], BF16)
        self.cover_d = dt("cover", [128, 32])
        self.nf01_d = dt("nf01", [128, 512])
        self.addc_d = dt("addc", [128, 512])
        self.e_all_d = dt("e_all", [32, 16, 128], BF16)
        self.gl_d = dt("gl_scr", [36, S_], F32, kind="ExternalOutput")
        self.qscr_d = dt("q_scr", [12, 2, 128, S_], BF16, kind="ExternalOutput")
        self.ocmp_d = dt("ocmp_scr", [12, 128, S_], F32, kind="ExternalOutput")
        self.b_gl = Buf("gl"); self.b_qscr = [Buf("qscr%d" % i) for i in range(12)]; self.b_ocmp = [Buf("ocmp%d" % i) for i in range(12)]
        self.b_rope = Buf("rope")
        self.b_mixT = [Buf("mixT%d" % i) for i in range(16)]
        self.y = dt("y", [S_, D_], F32, kind="ExternalOutput")
        self.ybuf = [Buf("y%d" % i) for i in range(16)]
        self.xbuf = Buf("x")

    def sb(self, st, name, shape, dtype):
        self._uid = getattr(self, "_uid", 0) + 1
        return st.enter_context(self.nc.sbuf_tensor("%s_%d" % (name, self._uid), shape, dtype))

    def pp(self, st, name, shape, dtype):
        self._uid = getattr(self, "_uid", 0) + 1
        return st.enter_context(self.nc.psum_tensor("%s_%d" % (name, self._uid), shape, dtype))

    def setup(self):
        nc, S, es = self.nc, self.S, self.es
        self.ident = self.sb(es, "ident_s", [128, 128], BF16)
        self.b_ident = Buf("ident")
        S.dma("sp", self.ident[:], self.ident_d, None, self.b_ident)
        self.ones = self.sb(es, "ones_s", [128, 128], BF16); self.b_ones = Buf("ones")
        S.dma("sp", self.ones[:], self.ones_d, None, self.b_ones)
        with ExitStack() as st:
            pi = self.sb(st, "rp_pi", [128, S_], I32); b_pi = Buf()
            pf = self.sb(st, "rp_pf", [128, S_], F32); b_pf = Buf()
            ang = self.sb(st, "rp_ang", [128, S_], F32); b_ang = Buf()
            kf = self.sb(st, "rp_kf", [128, S_], F32); b_kf = Buf()
            res = self.sb(st, "rp_res", [128, S_], F32); b_res = Buf()
            ivf = self.sb(st, "rp_ivf", [128, 2], F32); b_ivf = Buf()
            S.dma("sp", ivf[:, 0:1], self.invfreq_d, None, b_ivf)
            S.dma("sp", ivf[:, 1:2], self.sinsign_d, None, b_ivf)
            S.dma("sp", pi[:], self.pos.partition_broadcast(128), None, b_pi)
            S.op("dve", lambda e: e.tensor_copy(pf[:], pi[:]), [b_pi], [b_pf])
            for which in range(2):
                if which == 0:
                    S.op("dve", lambda e: e.tensor_scalar(ang[:], pf[:], ivf[:, 0:1], PI / 2, op0=ALU.mult, op1=ALU.add), [b_pf, b_ivf], [b_ang])
                else:
                    S.op("dve", lambda e: e.tensor_scalar(ang[:], pf[:], ivf[:, 0:1], None, op0=ALU.mult), [b_pf, b_ivf], [b_ang])
                S.op("dve", lambda e: e.tensor_scalar(pi[:], ang[:], 1.0 / TWO_PI, None, op0=ALU.mult), [b_ang], [b_pi])
                S.op("dve", lambda e: e.tensor_copy(kf[:], pi[:]), [b_pi], [b_kf])
                S.op("dve", lambda e: e.scalar_tensor_tensor(out=ang[:], in0=kf[:], scalar=-TWO_PI, in1=ang[:], op0=ALU.mult, op1=ALU.add), [b_kf, b_ang], [b_ang])
                S.op("dve", lambda e: e.tensor_scalar(kf[:], ang[:], PI, TWO_PI, op0=ALU.is_gt, op1=ALU.mult), [b_ang], [b_kf])
                S.op("dve", lambda e: e.tensor_sub(ang[:], ang[:], kf[:]), [b_ang, b_kf], [b_ang])
                S.op("dve", lambda e: e.tensor_scalar(kf[:], ang[:], -PI, -TWO_PI, op0=ALU.is_lt, op1=ALU.mult), [b_ang], [b_kf])
                S.op("dve", lambda e: e.tensor_add(ang[:], ang[:], kf[:]), [b_ang, b_kf], [b_ang])
                S.op("act", lambda e: e.activation(out=res[:], in_=ang[:], func=AF.Sin), [b_ang], [b_res])
                if which == 1:
                    S.op("dve", lambda e: e.tensor_scalar(res[:], res[:], ivf[:, 1:2], None, op0=ALU.mult), [b_res, b_ivf], [b_res])
                S.dma("sp", self.rope_d[which], res[:], b_res, self.b_rope)
            S.barrier()

    def pre_norm(self, st, src, src_bufs, nw_idx, tts, hnT, b_hnT):
        nc, S = self.nc, self.S
        wb = self.sb(st, "pn_wb", [128, D_], F32); b_wb = Buf()
        S.dma("pool", wb[:], self.norm_w[nw_idx:nw_idx + 1, :].partition_broadcast(128), None, b_wb)
        NB_ = 3
        hb = [self.sb(st, "pn_h%d" % i, [128, D_], F32) for i in range(NB_)]
        b_hb = [Buf() for _ in range(NB_)]
        hn = [self.sb(st, "pn_hn%d" % i, [128, D_], BF16) for i in range(NB_)]
        b_hn = [Buf() for _ in range(NB_)]
        ssl = [self.sb(st, "pn_ss%d" % i, [128, 4], F32) for i in range(NB_)]; b_ssl = [Buf() for _ in range(NB_)]
        pst = [self.pp(st, "pn_ps%d" % i, [128, 8, 128], BF16) for i in range(2)]
        b_pst = [PB(), PB()]
        def stage_a(i, tt):
            h_t, bh = hb[i % NB_], b_hb[i % NB_]
            hn_t, bn = hn[i % NB_], b_hn[i % NB_]
            ss, b_ss = ssl[i % NB_], b_ssl[i % NB_]
            S.dma("sp" if i % 2 == 0 else "pool", h_t[:], src(tt), src_bufs(tt), bh)
            S.op("dve", lambda e: e.memset(ss[:, 0:1], 0.0), [], [b_ss])
            S.op("act", lambda e: e.activation(out=hn_t[:], in_=h_t[:], func=AF.Square, accum_out=ss[:, 0:1]), [bh], [bn, b_ss])
            S.op("act", lambda e: e.activation(out=ss[:, 1:2], in_=ss[:, 0:1], func=AF.Sqrt, scale=1.0 / D_, bias=self.eps_t[:, 0:1]), [b_ss], [b_ss])
            S.op("dve", lambda e: e.reciprocal(ss[:, 2:3], ss[:, 1:2]), [b_ss], [b_ss])
            S.op("dve", lambda e: e.scalar_tensor_tensor(out=hn_t[:], in0=h_t[:], scalar=ss[:, 2:3], in1=wb[:], op0=ALU.mult, op1=ALU.mult), [bh, b_ss, b_wb], [bn])

        def stage_b(i, tt):
            hn_t, bn = hn[i % NB_], b_hn[i % NB_]
            for half in range(2):
                for j in range(8):
                    c0 = (half * 8 + j) * 128
                    S.op("pe", lambda e: e.transpose(pst[half][:, j, :], hn_t[:, c0:c0 + 128], self.ident[:]), [bn, self.b_ident], [b_pst[half]], inc=(j == 7))
                if half == 0:
                    S.op("act", lambda e: e.activation(out=hnT[:, half * 8:(half + 1) * 8, i * 128:(i + 1) * 128], in_=pst[half][:], func=AF.Copy), [b_pst[half]], [b_hnT])
                else:
                    S.op("dve", lambda e: e.tensor_copy(hnT[:, half * 8:(half + 1) * 8, i * 128:(i + 1) * 128], pst[half][:]), [b_pst[half]], [b_hnT])

        n = len(tts)
        for i in range(n + 1):
            if i < n:
                stage_a(i, tts[i])
            if i >= 1:
                stage_b(i - 1, tts[i - 1])

    def post_tile(self, tt, f_t, b_f, wb, b_wb, coef, h_t, b_h, h_src, h_src_buf, junk, b_junk, ss, b_ss, ldq="sp"):
        S = self.S
        S.dma(ldq, h_t[:], h_src, h_src_buf, b_h)
        S.op("dve", lambda e: e.memset(ss[:, 0:1], 0.0), [], [b_ss])
        S.op("act", lambda e: e.activation(out=junk[:], in_=f_t[:], func=AF.Square, accum_out=ss[:, 0:1]), [b_f], [b_junk, b_ss])
        S.op("act", lambda e: e.activation(out=ss[:, 1:2], in_=ss[:, 0:1], func=AF.Sqrt, scale=1.0 / D_, bias=self.eps_t[:, 0:1]), [b_ss], [b_ss])
        S.op("dve", lambda e: e.reciprocal(ss[:, 2:3], ss[:, 1:2]), [b_ss], [b_ss])
        if coef != 1.0:
            S.op("dve", lambda e: e.tensor_scalar(ss[:, 2:3], ss[:, 2:3], coef, None, op0=ALU.mult), [b_ss], [b_ss])
        S.op("dve", lambda e: e.scalar_tensor_tensor(out=f_t[:], in0=f_t[:], scalar=ss[:, 2:3], in1=wb[:], op0=ALU.mult, op1=ALU.mult), [b_f, b_ss, b_wb], [b_f])
        S.op("dve", lambda e: e.tensor_add(f_t[:], f_t[:], h_t[:]), [b_f, b_h], [b_f])
        S.dma("sp", self.y[tt * 128:(tt + 1) * 128, :], f_t[:], b_f, self.ybuf[tt])

    def stream_src(self, first):
        if first:
            return (lambda tt: self.x[tt * 128:(tt + 1) * 128, :]), (lambda tt: None)
        return (lambda tt: self.y[tt * 128:(tt + 1) * 128, :]), (lambda tt: self.ybuf[tt])

    def ffn(self, fi, nw_pre, nw_post, first):
        nc, S = self.nc, self.S
        G = 1024
        NT = G // 128
        src, src_bufs = self.stream_src(first)
        wg_v = self.wg[fi].rearrange("(kc p) f -> p kc f", p=128)
        wu_v = self.wu[fi].rearrange("(kc p) f -> p kc f", p=128)
        wd_v = self.wd[fi].rearrange("(fc p) d -> p fc d", p=128)
        BW = 256
        with ExitStack() as st:
            aT = self.sb(st, "f_aT", [128, NFC, G], BF16); b_aT = Buf()
            wpost = self.sb(st, "f_wpost", [128, D_], F32); b_wpost = Buf()
            S.dma("sp", wpost[:], self.norm_w[nw_post:nw_post + 1, :].partition_broadcast(128), None, b_wpost)
            for g in range(S_ // G):
                tts = list(range(g * NT, (g + 1) * NT))
                with ExitStack() as sa:
                    hnT = self.sb(sa, "f_hnT", [128, 16, G], BF16); b_hnT = Buf()
                    with ExitStack() as st2:
                        self.pre_norm(st2, src, src_bufs, nw_pre, tts, hnT, b_hnT)
                        S.barrier()
                    wgb = [self.sb(sa, "f_wg%d" % i, [128, 16, BW], BF16) for i in range(2)]
                    wub = [self.sb(sa, "f_wu%d" % i, [128, 16, BW], BF16) for i in range(2)]
                    b_wgb = [Buf(), Buf()]; b_wub = [Buf(), Buf()]
                    sg = [self.sb(sa, "f_sg%d" % i, [128, 512], F32) for i in range(4)]
                    b_sg = [Buf() for _ in range(4)]
                    ps = [self.pp(sa, "f_ps%d" % i, [128, 512], F32) for i in range(8)]
                    b_ps = [PB() for _ in range(8)]
                    cnt = 0
                    for blk in range(FF // BW):
                        sl = blk % 2
                        S.dma("pool", wgb[sl][:], wg_v[:, :, blk * BW:(blk + 1) * BW], None, b_wgb[sl])
                        S.dma("pool", wub[sl][:], wu_v[:, :, blk * BW:(blk + 1) * BW], None, b_wub[sl])
                        for c in range(BW // 128):
                            fc = blk * (BW // 128) + c
                            for tb in range(G // 512):
                                pi_ = (cnt % 4) * 2
                                pg, pu, bpg, bpu = ps[pi_], ps[pi_ + 1], b_ps[pi_], b_ps[pi_ + 1]
                                for kc in range(16):
                                    S.op("pe", lambda e: e.matmul(pg[:], wgb[sl][:, kc, c * 128:(c + 1) * 128], hnT[:, kc, tb * 512:(tb + 1) * 512], start=(kc == 0), stop=(kc == 15)),
                                         [b_wgb[sl], b_hnT], [bpg], inc=(kc == 15))
                                for kc in range(16):
                                    S.op("pe", lambda e: e.matmul(pu[:], wub[sl][:, kc, c * 128:(c + 1) * 128], hnT[:, kc, tb * 512:(tb + 1) * 512], start=(kc == 0), stop=(kc == 15)),
                                         [b_wub[sl], b_hnT], [bpu], inc=(kc == 15))
                                sgt, bsg = sg[cnt % 4], b_sg[cnt % 4]
                                S.op("act", lambda e: e.activation(out=sgt[:], in_=pg[:], func=AF.Silu), [bpg], [bsg])
                                S.op("dve", lambda e: e.tensor_tensor(out=aT[:, fc, tb * 512:(tb + 1) * 512], in0=sgt[:], in1=pu[:], op=ALU.mult), [bsg, bpu], [b_aT])
                                cnt += 1
                    S.barrier()
                with ExitStack() as sb_:
                    NWD = 6
                    wdb = [self.sb(sb_, "f_wd%d" % i, [128, 512], BF16) for i in range(NWD)]
                    b_wdb = [Buf() for _ in range(NWD)]
                    fsb = [self.sb(sb_, "f_fsb%d" % i, [128, D_], F32) for i in range(NT)]
                    b_fsb = [Buf() for _ in range(NT)]
                    hp = [self.sb(sb_, "f_hp%d" % i, [128, D_], F32) for i in range(3)]
                    b_hp = [Buf() for _ in range(3)]
                    junk = [self.sb(sb_, "f_junk%d" % i, [128, D_], BF16) for i in range(2)]; b_junk = [Buf(), Buf()]
                    ss = [self.sb(sb_, "f_ss%d" % i, [128, 4], F32) for i in range(2)]; b_ss = [Buf(), Buf()]
                    ps = [self.pp(sb_, "f_pb%d" % i, [128, 512], F32) for i in range(8)]
                    b_ps = [PB() for _ in range(8)]
                    wcnt = 0
                    for dq in range(4):
                        for fc in range(NFC):
                            sl = wcnt % NWD; wcnt += 1
                            S.dma("pool", wdb[sl][:], wd_v[:, fc, dq * 512:(dq + 1) * 512], None, b_wdb[sl])
                            for t8 in range(NT):
                                S.op("pe", lambda e: e.matmul(ps[t8][:], aT[:, fc, t8 * 128:(t8 + 1) * 128], wdb[sl][:], start=(fc == 0), stop=(fc == NFC - 1)),
                                     [b_aT, b_wdb[sl]], [b_ps[t8]], inc=(t8 == NT - 1))
                        for t8 in range(NT):
                            if t8 % 2 == 0:
                                S.op("act", lambda e: e.activation(out=fsb[t8][:, dq * 512:(dq + 1) * 512], in_=ps[t8][:], func=AF.Copy), [b_ps[t8]], [b_fsb[t8]])
                            else:
                                S.op("dve", lambda e: e.tensor_copy(fsb[t8][:, dq * 512:(dq + 1) * 512], ps[t8][:]), [b_ps[t8]], [b_fsb[t8]])
                    for t8 in range(NT):
                        tt = g * NT + t8
                        self.post_tile(tt, fsb[t8], b_fsb[t8], wpost, b_wpost, 0.5, hp[t8 % 3], b_hp[t8 % 3], src(tt), src_bufs(tt), junk[t8 % 2], b_junk[t8 % 2], ss[t8 % 2], b_ss[t8 % 2], ldq="pool")
                    S.barrier()

    def build(self):
        nc, S, es = self.nc, self.S, self.es
        cfg = self.cfg
        with es:
            self.eps_t = self.sb(es, "eps_t", [128, 1], F32)
            b_eps = Buf()
            S.op("dve", lambda e: e.memset(self.eps_t[:], EPS), [], [b_eps])
            self.setup()
            S.barrier()
            first = True
            n_sub = cfg.get("n_sub", 12)
            sub = 0
            for layer in range(4):
                for which in range(3):
                    if sub >= n_sub:
                        break
                    if which == 0:
                        self.ffn(layer * 2 + 0, layer * 6 + 0, layer * 6 + 1, first)
                    elif which == 1:
                        self.mixer(layer, first)
                    else:
                        self.ffn(layer * 2 + 1, layer * 6 + 4, layer * 6 + 5, first)
                    first = False
                    sub += 1
            S.barrier()
        return nc

    def mixer(self, layer, first):
        with ExitStack() as mst:
            self._mst = mst
            self._wo = None
            if layer % 2 == 0:
                self.even_mixer(layer, first)
            else:
                self.odd_mixer(layer, first)
            if self.cfg.get("ev_stage", 9) >= 5 and self.cfg.get("od_stage", 9) >= 3:
                self.out_proj(layer, first)
            self._wo = None

    def proj_slab(self, wslab, b_w, ncol, hnT, b_hnT, pj, b_pj, evac):
        S = self.S
        for half in range(2):
            for bk in range(2):
                c0 = half * 1024 + bk * 512
                for kc in range(16):
                    S.op("pe", lambda e: e.matmul(pj[bk][0:ncol, :], wslab[:, kc, 0:ncol], hnT[:, kc, c0:c0 + 512], start=(kc == 0), stop=(kc == 15)),
                         [b_w, b_hnT], [b_pj[bk]], inc=(kc == 15))
                evac(bk, c0)

    def even_mixer(self, layer, first):
        nc, S = self.nc, self.S
        i = layer // 2
        src, src_bufs = self.stream_src(first)
        w_in_v = self.ev_w_in[i].rearrange("(kc p) c -> p kc c", p=128)
        with ExitStack() as st:
            hnT = self.sb(st, "m_hnT", [128, 16, S_], BF16); b_hnT = Buf()
            with ExitStack() as st2:
                self.pre_norm(st2, src, src_bufs, layer * 6 + 2, list(range(16)), hnT, b_hnT)
                S.barrier()
            cosF = self.sb(st, "m_cos", [128, S_], F32); sinF = self.sb(st, "m_sin", [128, S_], F32); b_cs = Buf()
            S.dma("sp", cosF[:], self.rope_d[0], self.b_rope, b_cs)
            S.dma("sp", sinF[:], self.rope_d[1], self.b_rope, b_cs)
            mk = self.sb(st, "m_mk", [128, 256], BF16); b_mk = Buf()
            S.dma("sp", mk[:], self.mk_ev_d, None, b_mk)
            wsl = [[self.sb(st, "m_w%d_%d" % (j, k), [128, 16, 128], BF16) for k in range(2)] for j in range(3)]
            b_wsl = [[Buf(), Buf()] for j in range(3)]
            qT = self.sb(st, "m_qT", [128, S_], BF16); kT = self.sb(st, "m_kT", [128, S_], BF16); vT = self.sb(st, "m_vT", [128, S_], BF16)
            b_qT, b_kT, b_vT = Buf(), Buf(), Buf()
            vS = self.sb(st, "m_vS", [128, 48, 128], BF16); b_vS = Buf()
            acc = self.sb(st, "m_acc", [128, S_], F32); dacc = self.sb(st, "m_dacc", [128, S_], F32); b_acc, b_dacc = Buf(), Buf()
            qsw = [self.sb(st, "m_qsw%d" % k, [128, 512], F32) for k in range(2)]; b_qsw = [Buf(), Buf()]
            t1 = [self.sb(st, "m_t1%d" % k, [128, 512], F32) for k in range(2)]; b_t1 = [Buf(), Buf()]
            et = [self.sb(st, "m_et%d" % k, [128, 256], F32) for k in range(3)]; b_et = [Buf() for _ in range(3)]
            pt = [self.sb(st, "m_pt%d" % k, [128, 256], BF16) for k in range(3)]; b_pt = [Buf() for _ in range(3)]
            osb = [self.sb(st, "m_osb%d" % k, [128, S_], BF16) for k in range(2)]; b_osb = [Buf(), Buf()]
            pj = [self.pp(st, "m_pj%d" % k, [128, 512], F32) for k in range(2)]; b_pj = [PB(), PB()]
            vtp = self.pp(st, "m_vtp", [128, 8, 128], BF16); b_vtp = PB()
            sc = [self.pp(st, "m_sc%d" % k, [128, 512], F32) for k in range(3)]; b_sc = [PB() for _ in range(3)]
            ud = [self.pp(st, "m_ud%d" % k, [128, 512], F32) for k in range(2)]; b_ud = [PB(), PB()]
            rcnt = [0]

            def rope_evac(dst, b_dst):
                def f(bk, c0):
                    k = rcnt[0] % 2; rcnt[0] += 1
                    S.op("act", lambda e: e.activation(out=qsw[k][0:64, :], in_=pj[bk][64:128, :], func=AF.Copy), [b_pj[bk]], [b_qsw[k], b_pj[bk]])
                    S.op("act", lambda e: e.activation(out=qsw[k][64:128, :], in_=pj[bk][0:64, :], func=AF.Copy), [b_pj[bk]], [b_qsw[k], b_pj[bk]])
                    S.op("dve", lambda e: e.tensor_tensor(out=t1[k][:], in0=pj[bk][:], in1=cosF[:, c0:c0 + 512], op=ALU.mult), [b_pj[bk], b_cs], [b_t1[k], b_pj[bk]])
                    S.op("dve", lambda e: e.tensor_tensor(out=qsw[k][:], in0=qsw[k][:], in1=sinF[:, c0:c0 + 512], op=ALU.mult), [b_qsw[k], b_cs], [b_qsw[k]])
                    S.op("dve", lambda e: e.tensor_tensor(out=dst[:, c0:c0 + 512], in0=t1[k][:], in1=qsw[k][:], op=ALU.add), [b_t1[k], b_qsw[k]], [b_dst])
                return f

            def copy_evac(dst, b_dst):
                def f(bk, c0):
                    S.op("act", lambda e: e.activation(out=dst[:, c0:c0 + 512], in_=pj[bk][:], func=AF.Copy), [b_pj[bk]], [b_dst])
                return f

            blocks = []
            for pi_, d in enumerate((1, 4, 16)):
                nb = 16 // d
                for r in range(d):
                    for n in range(nb):
                        blocks.append((pi_, d, n * 128 * d + r, (n - 1) * 128 * d + r if n > 0 else None))
            blk_index = {(b[1], b[2]): bi for bi, b in enumerate(blocks)}
            sl_of = lambda start, d: slice(start, start + 127 * d + 1, d) if d > 1 else slice(start, start + 128)

            bcnt = 0
            from collections import deque
            epipe = deque(); ELA = 2
            stage = self.cfg.get("ev_stage", 9)
            for h in range(12 if stage >= 3 else (1 if stage >= 1 else 0)):
                k2 = h % 2
                sub_ = self.cfg.get("ev_sub", 9)
                tb_left = list(range(0, 48, 8))

                def emit_tb():
                    if not tb_left:
                        return
                    b0 = tb_left.pop(0)
                    for jj in range(8):
                        _, d, start, _ = blocks[b0 + jj]
                        S.op("pe", lambda e: e.transpose(vtp[:, jj, :], vT[:, sl_of(start, d)], self.ident[:]), [b_vT, self.b_ident], [b_vtp], inc=(jj == 7))
                    S.op("act", lambda e: e.activation(out=vS[:, b0:b0 + 8, :], in_=vtp[:], func=AF.Copy), [b_vtp], [b_vS])

                def with_tb(ev):
                    def f(bk, c0):
                        ev(bk, c0)
                        emit_tb()
                    return f
                NH_ = 12 if stage >= 3 else (1 if stage >= 1 else 0)
                for j, (dst, b_dst, col0) in ((2, (vT, b_vT, 3072 + h * 128)), (0, (qT, b_qT, h * 128)), (1, (kT, b_kT, 1536 + h * 128))):
                    if h == 0:
                        S.dma("pool", wsl[j][k2][:], w_in_v[:, :, col0:col0 + 128], None, b_wsl[j][k2])
                    if h + 1 < NH_:
                        S.dma("pool", wsl[j][1 - k2][:], w_in_v[:, :, col0 + 128:col0 + 256], None, b_wsl[j][1 - k2])
                    self.proj_slab(wsl[j][k2], b_wsl[j][k2], 128, hnT, b_hnT, pj, b_pj, with_tb(rope_evac(dst, b_dst)) if j < 2 else copy_evac(dst, b_dst))
                while tb_left:
                    emit_tb()
                for bi, (pi_, d, start, pstart) in enumerate(blocks if stage >= 2 else []):
                    k = bcnt % 3; bcnt += 1
                    W = 256 if pstart is not None else 128
                    qs = sl_of(start, d)
                    S.op("pe", lambda e: e.matmul(sc[k][:, 0:W], self.ident[:], mk[:, 0:W], start=True, stop=False), [self.b_ident, b_mk], [b_sc[k]], inc=False)
                    S.op("pe", lambda e: e.matmul(sc[k][:, 0:128], kT[:, qs], qT[:, qs], start=False, stop=(pstart is None)), [b_kT, b_qT], [b_sc[k]], inc=(pstart is None))
                    if pstart is not None:
                        S.op("pe", lambda e: e.matmul(sc[k][:, 128:256], kT[:, sl_of(pstart, d)], qT[:, qs], start=False, stop=True), [b_kT, b_qT], [b_sc[k]])
                    S.op("act", lambda e: e.activation(out=pt[k][:, 0:W], in_=sc[k][:, 0:W], func=AF.Exp, scale=SCALE), [b_sc[k]], [b_pt[k]])

                    def stage2(k=k, bi=bi, pi_=pi_, d=d, pstart=pstart, qs=qs):
                        ku = bi % 2
                        S.op("pe", lambda e: e.matmul(ud[ku][:, 0:128], vS[:, bi, :], pt[k][:, 0:128], start=True, stop=(pstart is None)), [b_vS, b_pt[k]], [b_ud[ku]], inc=False)
                        if pstart is not None:
                            pbi = blk_index[(d, pstart)]
                            S.op("pe", lambda e: e.matmul(ud[ku][:, 0:128], vS[:, pbi, :], pt[k][:, 128:256], start=False, stop=True), [b_vS, b_pt[k]], [b_ud[ku]], inc=False)
                        S.op("pe", lambda e: e.matmul(ud[ku][:, 128:256], self.ones[:], pt[k][:, 0:128], start=True, stop=(pstart is None)), [self.b_ones, b_pt[k]], [b_ud[ku]], inc=(pstart is None))
                        if pstart is not None:
                            S.op("pe", lambda e: e.matmul(ud[ku][:, 128:256], self.ones[:], pt[k][:, 128:256], start=False, stop=True), [self.b_ones, b_pt[k]], [b_ud[ku]])
                        if pi_ == 0:
                            S.op("act", lambda e: e.activation(out=acc[:, qs], in_=ud[ku][:, 0:128], func=AF.Copy), [b_ud[ku]], [b_acc])
                            S.op("act", lambda e: e.activation(out=dacc[:, qs], in_=ud[ku][:, 128:256], func=AF.Copy), [b_ud[ku]], [b_dacc])
                        else:
                            S.op("dve", lambda e: e.tensor_tensor(out=acc[:, qs], in0=acc[:, qs], in1=ud[ku][:, 0:128], op=ALU.add), [b_ud[ku], b_acc], [b_acc])
                            S.op("dve", lambda e: e.tensor_tensor(out=dacc[:, qs], in0=dacc[:, qs], in1=ud[ku][:, 128:256], op=ALU.add), [b_ud[ku], b_dacc], [b_dacc])
                    epipe.append(stage2)
                    while len(epipe) > ELA:
                        epipe.popleft()()
                while epipe:
                    epipe.popleft()()
                if sub_ < 4:
                    continue
                S.op("act", lambda e: e.activation(out=dacc[:], in_=dacc[:], func=AF.Ln), [b_dacc], [b_dacc])
                S.op("act", lambda e: e.activation(out=dacc[:], in_=dacc[:], func=AF.Exp, scale=-1.0), [b_dacc], [b_dacc])
                S.op("dve", lambda e: e.tensor_tensor(out=osb[k2][:], in0=acc[:], in1=dacc[:], op=ALU.mult), [b_acc, b_dacc], [b_osb[k2]])
                S.dma("sp", self.mixT_d[h], osb[k2][:], b_osb[k2], self.b_mixT[h])
            uT = self.sb(st, "m_uT", [128, S_], F32); b_uT = Buf()
            pw = self.sb(st, "m_pw", [128, 4, 128], BF16); b_pw = Buf()
            S.dma("pool", pw[:], self.pool_w[i].rearrange("g c d -> c g d"), None, b_pw)
            psc = self.sb(st, "m_psc", [128, 4], F32); b_psc = Buf()
            S.dma("sp", psc[:], self.pool_scale_t[i], None, b_psc)
            invc = self.sb(st, "m_invc", [128, 64], F32); b_invc = Buf()
            S.dma("sp", invc[:], self.invc_d, None, b_invc)
            for g, w in enumerate((2, 4, 8, 16) if stage >= 4 else ()):
                k2 = g % 2
                col0 = 4608 + g * 128
                S.dma("pool", wsl[0][k2][:], w_in_v[:, :, col0:col0 + 128], None, b_wsl[0][k2])
                self.proj_slab(wsl[0][k2], b_wsl[0][k2], 128, hnT, b_hnT, pj, b_pj, copy_evac(uT, b_uT))
                cur, b_cur = uT, b_uT
                pp2 = [(acc, b_acc), (dacc, b_dacc)]
                step = 1; n = 0
                while step < w:
                    nxt, b_nxt = pp2[n % 2]; n += 1
                    S.op("dve", lambda e: e.tensor_tensor(out=nxt[:, step:], in0=cur[:, step:], in1=cur[:, 0:S_ - step], op=ALU.add), [b_cur], [b_nxt])
                    S.op("act", lambda e: e.activation(out=nxt[:, 0:step], in_=cur[:, 0:step], func=AF.Copy), [b_cur], [b_nxt])
                    cur, b_cur = nxt, b_nxt; step *= 2
                S.op("dve", lambda e: e.scalar_tensor_tensor(out=qT[:, 16:], in0=cur[:, 16:], scalar=1.0 / w, in1=uT[:, 16:], op0=ALU.mult, op1=ALU.subtract), [b_cur, b_uT], [b_qT])
                S.op("dve", lambda e: e.tensor_tensor(out=cur[:, 0:16], in0=cur[:, 0:16], in1=invc[:, g * 16:(g + 1) * 16], op=ALU.mult), [b_cur, b_invc], [b_cur])
                S.op("dve", lambda e: e.tensor_tensor(out=qT[:, 0:16], in0=cur[:, 0:16], in1=uT[:, 0:16], op=ALU.subtract), [b_cur, b_uT], [b_qT])
                for bk4 in range(4):
                    c0 = bk4 * 512; bk = bk4 % 2
                    S.op("pe", lambda e: e.matmul(pj[bk][:], pw[:, g, :], qT[:, c0:c0 + 512], start=True, stop=True), [b_pw, b_qT], [b_pj[bk]])
                    S.op("dve", lambda e: e.tensor_scalar(osb[k2][:, c0:c0 + 512], pj[bk][:], psc[:, g:g + 1], None, op0=ALU.mult), [b_pj[bk], b_psc], [b_osb[k2]])
                S.dma("sp", self.mixT_d[12 + g], osb[k2][:], b_osb[k2], self.b_mixT[12 + g])
            S.barrier()

    def out_proj(self, layer, first):
        nc, S = self.nc, self.S
        i = layer // 2
        wout = (self.ev_w_out if layer % 2 == 0 else self.od_w_out)[i].rearrange("(fc p) d -> p fc d", p=128)
        src, src_bufs = self.stream_src(first)
        with ExitStack() as st:
            mixT = self.sb(st, "o_mixT", [128, 16, S_], BF16); b_mixT = Buf()
            for fc in range(16):
                S.dma("sp", mixT[:, fc, :], self.mixT_d[fc], self.b_mixT[fc], b_mixT)
            if self._wo is not None:
                wo, b_wo = self._wo
            else:
                wo = self.sb(st, "o_wo", [128, 16, D_], BF16); b_wo = Buf()
                for fc4 in range(4):
                    S.dma("pool", wo[:, fc4 * 4:(fc4 + 1) * 4, :], wout[:, fc4 * 4:(fc4 + 1) * 4, :], None, b_wo)
            fsb = [self.sb(st, "o_fsb%d" % k, [128, D_], F32) for k in range(2)]; b_fsb = [Buf(), Buf()]
            hp = [self.sb(st, "o_hp%d" % k, [128, D_], F32) for k in range(2)]; b_hp = [Buf(), Buf()]
            wpost = self.sb(st, "o_wpost", [128, D_], F32); b_wpost = Buf()
            S.dma("sp", wpost[:], self.norm_w[layer * 6 + 3:layer * 6 + 4, :].partition_broadcast(128), None, b_wpost)
            junk = [self.sb(st, "o_junk%d" % k, [128, D_], BF16) for k in range(2)]; b_junk = [Buf(), Buf()]
            ss = [self.sb(st, "o_ss%d" % k, [128, 4], F32) for k in range(2)]; b_ss = [Buf(), Buf()]
            ps = [self.pp(st, "o_ps%d" % k, [128, 512], F32) for k in range(8)]; b_ps = [PB() for _ in range(8)]
            for tt in range(16):
                par = tt % 2
                for db in range(4):
                    bi = par * 4 + db
                    for fc in range(16):
                        S.op("pe", lambda e: e.matmul(ps[bi][:], mixT[:, fc, tt * 128:(tt + 1) * 128], wo[:, fc, db * 512:(db + 1) * 512], start=(fc == 0), stop=(fc == 15)),
                             [b_mixT, b_wo], [b_ps[bi]], inc=(fc == 15))
                    if db % 2 == 0:
                        S.op("act", lambda e: e.activation(out=fsb[par][:, db * 512:(db + 1) * 512], in_=ps[bi][:], func=AF.Copy), [b_ps[bi]], [b_fsb[par]])
                    else:
                        S.op("dve", lambda e: e.tensor_copy(fsb[par][:, db * 512:(db + 1) * 512], ps[bi][:]), [b_ps[bi]], [b_fsb[par]])
                self.post_tile(tt, fsb[par], b_fsb[par], wpost, b_wpost, 1.0, hp[par], b_hp[par], src(tt), src_bufs(tt), junk[par], b_junk[par], ss[par], b_ss[par])
            S.barrier()

    def odd_mixer(self, layer, first):
        nc, S = self.nc, self.S
        i = layer // 2
        src, src_bufs = self.stream_src(first)
        w_in_v = self.od_w_in[i].rearrange("(kc p) c -> p kc c", p=128)
        C_Q, C_KC, C_VC, C_KS, C_VS, C_KW, C_VW, C_GL, C_U, C_CG, C_BG = 0, 1536, 1792, 2048, 2304, 2560, 2816, 3072, 3108, 3620, 4132
        stage = self.cfg.get("od_stage", 9)
        with ExitStack() as st:
            kcmpT = [self.sb(st, "n_kcmpT%d" % g, [128, 128], BF16) for g in range(2)]; b_kcmpT = [Buf(), Buf()]
            vcmp = [self.sb(st, "n_vcmp%d" % g, [128, 128], BF16) for g in range(2)]; b_vcmp = [Buf(), Buf()]
            ksT = [self.sb(st, "n_ksT%d" % g, [128, S_], BF16) for g in range(2)]; b_ksT = [Buf(), Buf()]
            kwT = [self.sb(st, "n_kwT%d" % g, [128, S_], BF16) for g in range(2)]; b_kwT = [Buf(), Buf()]
            vs = [self.sb(st, "n_vs%d" % g, [128, 16, 128], BF16) for g in range(2)]; b_vs = [Buf(), Buf()]
            vw = [self.sb(st, "n_vw%d" % g, [128, 16, 128], BF16) for g in range(2)]; b_vw = [Buf(), Buf()]
            with ExitStack() as s1:
                hnT = self.sb(s1, "n_hnT", [128, 16, S_], BF16); b_hnT = Buf()
                with ExitStack() as st2:
                    self.pre_norm(st2, src, src_bufs, layer * 6 + 2, list(range(16)), hnT, b_hnT)
                    S.barrier()
                cosF = self.sb(s1, "n_cos", [128, S_], F32); sinF = self.sb(s1, "n_sin", [128, S_], F32); b_cs = Buf()
                S.dma("sp", cosF[:], self.rope_d[0], self.b_rope, b_cs)
                S.dma("sp", sinF[:], self.rope_d[1], self.b_rope, b_cs)
                wsl = [self.sb(s1, "n_w%d" % k, [128, 16, 256], BF16) for k in range(2)]; b_wsl = [Buf(), Buf()]
                qsw = [self.sb(s1, "n_qsw%d" % k, [128, 512], F32) for k in range(2)]; b_qsw = [Buf(), Buf()]
                t1 = [self.sb(s1, "n_t1%d" % k, [128, 512], F32) for k in range(2)]; b_t1 = [Buf(), Buf()]
                f32a = self.sb(s1, "n_f32a", [128, S_], F32); b_f32a = Buf()
                f32b = self.sb(s1, "n_f32b", [128, S_], F32); b_f32b = Buf()
                f32c = self.sb(s1, "n_f32c", [128, S_], F32); b_f32c = Buf()
                ob = [self.sb(s1, "n_ob%d" % k, [128, S_], BF16) for k in range(4)]; b_ob = [Buf() for _ in range(4)]
                pj = [self.pp(s1, "n_pj%d" % k, [128, 512], F32) for k in range(2)]; b_pj = [PB(), PB()]
                px = [self.pp(s1, "n_px%d" % k, [128, 512], F32) for k in range(2)]; b_px = [PB(), PB()]
                rcnt = [0]; wcnt = [0]

                def rope_evac(dst, b_dst, udst=None, b_udst=None):
                    def f(bk, c0):
                        k = rcnt[0] % 2; rcnt[0] += 1
                        S.op("act", lambda e: e.activation(out=qsw[k][0:64, :], in_=pj[bk][64:128, :], func=AF.Copy), [b_pj[bk]], [b_qsw[k]])
                        S.op("act", lambda e: e.activation(out=qsw[k][64:128, :], in_=pj[bk][0:64, :], func=AF.Copy), [b_pj[bk]], [b_qsw[k]])
                        if udst is not None:
                            S.op("act", lambda e: e.activation(out=udst[:, c0:c0 + 512], in_=pj[bk][:], func=AF.Copy), [b_pj[bk]], [b_udst])
                        S.op("dve", lambda e: e.tensor_tensor(out=t1[k][:], in0=pj[bk][:], in1=cosF[:, c0:c0 + 512], op=ALU.mult), [b_pj[bk], b_cs], [b_t1[k]])
                        S.op("dve", lambda e: e.tensor_tensor(out=qsw[k][:], in0=qsw[k][:], in1=sinF[:, c0:c0 + 512], op=ALU.mult), [b_qsw[k], b_cs], [b_qsw[k]])
                        S.op("dve", lambda e: e.tensor_tensor(out=dst[:, c0:c0 + 512], in0=t1[k][:], in1=qsw[k][:], op=ALU.add), [b_t1[k], b_qsw[k]], [b_dst])
                    return f

                def copy_evac(dst, b_dst, np_=128, func=AF.Copy):
                    def f(bk, c0):
                        S.op("act", lambda e: e.activation(out=dst[0:np_, c0:c0 + 512], in_=pj[bk][0:np_, :], func=func), [b_pj[bk]], [b_dst])
                    return f

                plan = [(C_GL, 36)]
                for kv_ in range(2):
                    for g_ in range(2):
                        plan.append(((C_KC if kv_ == 0 else C_VC) + g_ * 128, 128))
                for g_ in range(2):
                    plan += [(C_KS + g_ * 128, 128), (C_KW + g_ * 128, 128)]
                plan += [(C_VS, 256), (C_VW, 256)]
                for c_ in range(4):
                    plan += [(C_U + c_ * 128, 128), (C_CG + c_ * 128, 128), (C_BG + c_ * 128, 128)]
                for h_ in range(12):
                    plan.append((C_Q + h_ * 128, 128))
                issued = [0]

                def ensure(i):
                    while issued[0] <= i and issued[0] < len(plan):
                        c0_, n_ = plan[issued[0]]
                        kk = issued[0] % 2
                        S.dma("pool", wsl[kk][:, :, 0:n_], w_in_v[:, :, c0_:c0_ + n_], None, b_wsl[kk])
                        issued[0] += 1

                def take(col0, ncol):
                    i = wcnt[0]; wcnt[0] += 1
                    assert plan[i] == (col0, ncol), (i, plan[i], col0, ncol)
                    ensure(i)
                    return i % 2

                def slab(col0, ncol, evac):
                    k = take(col0, ncol)
                    ensure(wcnt[0])
                    self.proj_slab(wsl[k], b_wsl[k], ncol, hnT, b_hnT, pj, b_pj, evac)

                slab(C_GL, 36, copy_evac(f32a, b_f32a, 36, AF.Sigmoid))
                S.dma("sp", self.gl_d, f32a[0:36, :], b_f32a, self.b_gl)
                w1 = self.sb(s1, "n_w1", [128, 32, 256], BF16); b_w1 = Buf()
                w2 = self.sb(s1, "n_w2", [128, 2, 128], BF16); b_w2 = Buf()
                peT = self.sb(s1, "n_peT", [128, 32], F32); b_peT = Buf()
                X = self.sb(s1, "n_X", [128, 32, 127], BF16); b_X = Buf()
                hidT = self.sb(s1, "n_hidT", [128, 2, 127], BF16); b_hidT = Buf()
                gx = [self.sb(s1, "n_gx%d" % k, [128, 127], F32) for k in range(3)]; b_gx = [Buf() for _ in range(3)]
                for kv in range(2):
                    S.dma("pool", w1[:], self.cmp_w1[kv][i].rearrange("(j p) h -> p j h", p=128), None, b_w1)
                    S.dma("pool", w2[:], self.cmp_w2[kv][i].rearrange("(hc p) d -> p hc d", p=128), None, b_w2)
                    S.dma("sp", peT[:], self.cmp_peT[kv][i], None, b_peT)
                    for g in range(2):
                        slab((C_KC if kv == 0 else C_VC) + g * 128, 128, copy_evac(f32a, b_f32a))
                        for j in range(32):
                            S.op("dve", lambda e: e.tensor_scalar(X[:, j, :], f32a[:, j:j + 16 * 126 + 1:16], peT[:, j:j + 1], None, op0=ALU.add), [b_f32a, b_peT], [b_X])
                        for hc in range(2):
                            for j in range(32):
                                S.op("pe", lambda e: e.matmul(px[hc][:, 0:127], w1[:, j, hc * 128:(hc + 1) * 128], X[:, j, :], start=(j == 0), stop=(j == 31)), [b_w1, b_X], [b_px[hc]], inc=(j == 31))
                            S.op("act", lambda e: e.activation(out=gx[0][:], in_=px[hc][:, 0:127], func=AF.Square), [b_px[hc]], [b_gx[0]])
                            S.op("dve", lambda e: e.tensor_scalar(gx[0][:], gx[0][:], 0.044715, 1.0, op0=ALU.mult, op1=ALU.add), [b_gx[0]], [b_gx[0]])
                            S.op("dve", lambda e: e.tensor_tensor(out=gx[1][:], in0=gx[0][:], in1=px[hc][:, 0:127], op=ALU.mult), [b_gx[0], b_px[hc]], [b_gx[1]])
                            S.op("act", lambda e: e.activation(out=gx[2][:], in_=gx[1][:], func=AF.Tanh, scale=0.7978845608028654), [b_gx[1]], [b_gx[2]])
                            S.op("dve", lambda e: e.scalar_tensor_tensor(out=gx[2][:], in0=gx[2][:], scalar=1.0, in1=px[hc][:, 0:127], op0=ALU.add, op1=ALU.mult), [b_gx[2], b_px[hc]], [b_gx[2]])
                            S.op("dve", lambda e: e.tensor_scalar(hidT[:, hc, :], gx[2][:], 0.5, None, op0=ALU.mult), [b_gx[2]], [b_hidT])
                        if kv == 0:
                            for hc in range(2):
                                S.op("pe", lambda e: e.matmul(px[0][:, 0:127], w2[:, hc, :], hidT[:, hc, :], start=(hc == 0), stop=(hc == 1)), [b_w2, b_hidT], [b_px[0]], inc=(hc == 1))
                            S.op("act", lambda e: e.activation(out=kcmpT[g][:, 0:127], in_=px[0][:, 0:127], func=AF.Copy), [b_px[0]], [b_kcmpT[g]])
                        else:
                            for hc in range(2):
                                S.op("pe", lambda e: e.matmul(px[0][0:127, 0:128], hidT[:, hc, :], w2[:, hc, :], start=(hc == 0), stop=(hc == 1)), [b_w2, b_hidT], [b_px[0]], inc=(hc == 1))
                            S.op("act", lambda e: e.activation(out=vcmp[g][0:127, :], in_=px[0][0:127, 0:128], func=AF.Copy), [b_px[0]], [b_vcmp[g]])
                for g in range(2):
                    slab(C_KS + g * 128, 128, rope_evac(ksT[g], b_ksT[g]))
                    slab(C_KW + g * 128, 128, rope_evac(kwT[g], b_kwT[g]))
                for (c0v, dsts, b_dsts) in ((C_VS, vs, b_vs), (C_VW, vw, b_vw)):
                    k = take(c0v, 256)
                    ensure(wcnt[0])
                    for t2 in range(8):
                        bk = t2 % 2
                        for u in range(2):
                            tt = t2 * 2 + u
                            for kc in range(16):
                                S.op("pe", lambda e: e.matmul(pj[bk][:, u * 256:(u + 1) * 256], hnT[:, kc, tt * 128:(tt + 1) * 128], wsl[k][:, kc, :], start=(kc == 0), stop=(kc == 15)),
                                     [b_hnT, b_wsl[k]], [b_pj[bk]], inc=(kc == 15))
                        for u in range(2):
                            tt = t2 * 2 + u
                            for g in range(2):
                                S.op("act", lambda e: e.activation(out=dsts[g][:, tt, :], in_=pj[bk][:, u * 256 + g * 128:u * 256 + (g + 1) * 128], func=AF.Copy), [b_pj[bk]], [b_dsts[g]])
                cw = self.sb(s1, "n_cw", [128, 12], F32); b_cw = Buf()
                S.dma("sp", cw[:], self.conv_wT[i], None, b_cw)
                for c in range(4):
                    slab(C_U + c * 128, 128, copy_evac(f32a, b_f32a))
                    slab(C_CG + c * 128, 128, copy_evac(f32b, b_f32b))
                    slab(C_BG + c * 128, 128, copy_evac(f32c, b_f32c))
                    S.op("dve", lambda e: e.tensor_tensor(out=f32a[:], in0=f32a[:], in1=f32b[:], op=ALU.mult), [b_f32a, b_f32b], [b_f32a])
                    S.op("dve", lambda e: e.tensor_scalar(f32b[:], f32a[:], cw[:, c * 3 + 2:c * 3 + 3], None, op0=ALU.mult), [b_f32a, b_cw], [b_f32b])
                    S.op("dve", lambda e: e.scalar_tensor_tensor(out=f32b[:, 1:], in0=f32a[:, 0:S_ - 1], scalar=cw[:, c * 3 + 1:c * 3 + 2], in1=f32b[:, 1:], op0=ALU.mult, op1=ALU.add), [b_f32a, b_cw, b_f32b], [b_f32b])
                    S.op("dve", lambda e: e.scalar_tensor_tensor(out=f32b[:, 2:], in0=f32a[:, 0:S_ - 2], scalar=cw[:, c * 3:c * 3 + 1], in1=f32b[:, 2:], op0=ALU.mult, op1=ALU.add), [b_f32a, b_cw, b_f32b], [b_f32b])
                    S.op("dve", lambda e: e.tensor_tensor(out=ob[c % 2][:], in0=f32b[:], in1=f32c[:], op=ALU.mult), [b_f32b, b_f32c], [b_ob[c % 2]])
                    S.dma("sp", self.mixT_d[12 + c], ob[c % 2][:], b_ob[c % 2], self.b_mixT[12 + c])
                for h in range(12):
                    k2 = h % 2
                    slab(C_Q + h * 128, 128, rope_evac(ob[2 + k2], b_ob[2 + k2], ob[k2], b_ob[k2]))
                    S.dma("sp", self.qscr_d[h, 0], ob[k2][:], b_ob[k2], self.b_qscr[h])
                    S.dma("sp", self.qscr_d[h, 1], ob[2 + k2][:], b_ob[2 + k2], self.b_qscr[h])
                S.barrier()
            if stage < 2:
                return
            self._uid += 1
            wo_t = self._mst.enter_context(nc.sbuf_tensor("o_wo_pf_%d" % self._uid, [128, 16, D_], BF16, side="right"))
            b_wo_t = Buf()
            wout_v = self.od_w_out[i].rearrange("(fc p) d -> p fc d", p=128)
            for fc4 in range(4):
                S.dma("pool", wo_t[:, fc4 * 4:(fc4 + 1) * 4, :], wout_v[:, fc4 * 4:(fc4 + 1) * 4, :], None, b_wo_t)
            self._wo = (wo_t, b_wo_t)
            with ExitStack() as s2:
                wbias = self.sb(s2, "a_wbias", [128, 8, 512], BF16); b_wbias = Buf()
                S.dma("sp", wbias[:], self.wbias_d, None, b_wbias)
                cbias = self.sb(s2, "a_cbias", [128, S_], BF16); b_cbias = Buf()
                S.dma("sp", cbias[:], self.cbias_d, None, b_cbias)
                cover = self.sb(s2, "a_cover", [128, 32], F32); b_cover = Buf()
                S.dma("sp", cover[:], self.cover_d, None, b_cover)
                nf01 = self.sb(s2, "a_nf01", [128, 512], F32); addc = self.sb(s2, "a_addc", [128, 512], F32); b_nfa = Buf()
                S.dma("sp", nf01[:], self.nf01_d, None, b_nfa)
                S.dma("sp", addc[:], self.addc_d, None, b_nfa)
                e_all = self.sb(s2, "a_eall", [32, 16, 128], BF16); b_eall = Buf()
                S.dma("sp", e_all[:], self.e_all_d, None, b_eall)
                Pacc = self.sb(s2, "a_Pacc", [128, S_], F32); b_Pacc = Buf()
                qt = [self.sb(s2, "a_q%d" % k, [128, S_], BF16) for k in range(2)]; b_qt = [Buf(), Buf()]
                gate = [self.sb(s2, "a_gate%d" % k, [128, S_], F32) for k in range(4)]; b_gate = [Buf() for _ in range(4)]
                oacc = self.sb(s2, "a_oacc", [128, S_], F32); b_oacc = Buf()
                oacc_b = self.sb(s2, "a_oaccb", [128, S_], F32); b_oacc_b = Buf()
                obf = [self.sb(s2, "a_obf%d" % k, [128, S_], BF16) for k in range(2)]; b_obf = [Buf(), Buf()]
                pt = [self.sb(s2, "a_pt%d" % k, [128, 512], BF16) for k in range(3)]; b_pt = [Buf() for _ in range(3)]
                rz = self.sb(s2, "a_rz", [128, 512], F32); b_rz = Buf()
                tq = self.sb(s2, "a_tq", [128, 512], F32); b_tq = Buf()
                scs = self.sb(s2, "a_scs", [128, 512], F32); b_scs = Buf()
                m8 = self.sb(s2, "a_m8", [128, 128], F32); b_m8 = Buf()
                selb = self.sb(s2, "a_selb", [128, 512], BF16); b_selb = Buf()
                selbT = self.sb(s2, "a_selbT", [32, S_], BF16); b_selbT = Buf()
                sp_ = [self.pp(s2, "a_s%d" % k, [128, 512], F32) for k in range(3)]; b_sp = [PB() for _ in range(3)]
                up = [self.pp(s2, "a_u%d" % k, [128, 512], F32) for k in range(2)]; b_up = [PB(), PB()]
                dp = [self.pp(s2, "a_d%d" % k, [128, 512], F32) for k in range(2)]; b_dp = [PB(), PB()]
                tp = self.pp(s2, "a_tp", [32, 8, 128], BF16); b_tp = PB()
                pcnt = [0]; ucnt = [0]

                from collections import deque
                pipe = deque(); LA = 2

                def push(fn):
                    pipe.append(fn)
                    while len(pipe) > LA:
                        pipe.popleft()()

                def flush():
                    while pipe:
                        pipe.popleft()()

                def attend(qsrc, b_q, Q, pairs, np_, after):
                    ub = ucnt[0] % 2; ucnt[0] += 1
                    n = len(pairs)
                    for pi_, pr in enumerate(pairs):
                        (kap, b_k, biases, vap, b_v) = pr[:5]
                        lo, hi = pr[5] if len(pr) > 5 else (0, 512)
                        k = pcnt[0] % 3; pcnt[0] += 1
                        nb_ = len(biases)
                        S.op("pe", lambda e: e.matmul(sp_[k][0:np_, lo:hi], kap, qsrc[:, Q * 512 + lo:Q * 512 + hi], start=True, stop=(nb_ == 0)), [b_k, b_q], [b_sp[k]], inc=(nb_ == 0))
                        for bi_, (bl, br, bb) in enumerate(biases):
                            S.op("pe", lambda e: e.matmul(sp_[k][0:np_, lo:hi], bl, br[:, lo:hi], start=False, stop=(bi_ == nb_ - 1)), bb, [b_sp[k]], inc=(bi_ == nb_ - 1))
                        S.op("act", lambda e: e.activation(out=pt[k][0:np_, lo:hi], in_=sp_[k][0:np_, lo:hi], func=AF.Exp, scale=SCALE), [b_sp[k]], [b_pt[k]])

                        def pv(k=k, pi_=pi_, vap=vap, b_v=b_v, lo=lo, hi=hi):
                            S.op("pe", lambda e: e.matmul(up[ub][:, lo:hi], vap, pt[k][0:np_, lo:hi], start=(pi_ == 0), stop=(pi_ == n - 1), skip_group_check=True), [b_v, b_pt[k]], [b_up[ub]], inc=False)
                            S.op("pe", lambda e: e.matmul(dp[ub][:, lo:hi], self.ones[0:np_, :], pt[k][0:np_, lo:hi], start=(pi_ == 0), stop=(pi_ == n - 1), skip_group_check=True), [self.b_ones, b_pt[k]], [b_dp[ub]], inc=True)
                            if pi_ == n - 1:
                                after(ub, k)
                        push(pv)

                def finish(ub, gidx, first_branch, Q, oa=None, b_oa=None):
                    cs = slice(Q * 512, (Q + 1) * 512)
                    if oa is None:
                        oa, b_oa = oacc, b_oacc
                    S.op("dve", lambda e: e.tensor_scalar(rz[:], dp[ub][:], 1e-30, None, op0=ALU.max), [b_dp[ub]], [b_rz])
                    S.op("act", lambda e: e.activation(out=rz[:], in_=rz[:], func=AF.Ln), [b_rz], [b_rz])
                    S.op("act", lambda e: e.activation(out=rz[:], in_=rz[:], func=AF.Exp, scale=-1.0), [b_rz], [b_rz])
                    S.op("dve", lambda e: e.tensor_tensor(out=tq[:], in0=up[ub][:], in1=rz[:], op=ALU.mult), [b_up[ub], b_rz], [b_tq])
                    if first_branch:
                        S.op("dve", lambda e: e.tensor_tensor(out=oa[:, cs], in0=tq[:], in1=gate[gidx][:, cs], op=ALU.mult), [b_tq, b_gate[gidx]], [b_oa])
                    else:
                        S.op("dve", lambda e: e.tensor_tensor(out=tq[:], in0=tq[:], in1=gate[gidx][:, cs], op=ALU.mult), [b_tq, b_gate[gidx]], [b_tq])
                        S.op("dve", lambda e: e.tensor_tensor(out=oa[:, cs], in0=oa[:, cs], in1=tq[:], op=ALU.add), [b_tq, b_oa], [b_oa])

                for g in range(2):
                    for hh in range(6):
                        h = g * 6 + hh
                        k2 = h % 2
                        gi = 0 if hh % 2 == 0 else 3
                        oa, b_oa = (oacc, b_oacc) if hh % 2 == 0 else (oacc_b, b_oacc_b)
                        S.dma("sp", qt[k2][:], self.qscr_d[h, 0], self.b_qscr[h], b_qt[k2])
                        S.dma("sp", gate[gi][:], self.gl_d[h * 3:h * 3 + 1, :].partition_broadcast(128), self.b_gl, b_gate[gi])
                        for Q in range(4):
                            cs = slice(Q * 512, (Q + 1) * 512)

                            def after1(ub, pk, Q=Q, cs=cs, hh=hh, gi=gi, oa=oa, b_oa=b_oa, h=h):
                                finish(ub, gi, True, Q, oa, b_oa)
                                if hh == 0:
                                    S.op("dve", lambda e: e.tensor_tensor(out=Pacc[0:127, cs], in0=pt[pk][0:127, :], in1=rz[0:127, :], op=ALU.mult), [b_pt[pk], b_rz], [b_Pacc])
                                else:
                                    S.op("dve", lambda e: e.tensor_tensor(out=tq[0:127, :], in0=pt[pk][0:127, :], in1=rz[0:127, :], op=ALU.mult), [b_pt[pk], b_rz], [b_tq])
                                    S.op("dve", lambda e: e.tensor_tensor(out=Pacc[0:127, cs], in0=Pacc[0:127, cs], in1=tq[0:127, :], op=ALU.add), [b_tq, b_Pacc], [b_Pacc])
                                if Q == 3:
                                    S.dma("sp", self.ocmp_d[h], oa[:], b_oa, self.b_ocmp[h])
                            attend(qt[k2], b_qt[k2], Q,
                                   [(kcmpT[g][:, 0:127], b_kcmpT[g], [(self.ident[0:127, 0:127], cbias[0:127, cs], [self.b_ident, b_cbias])], vcmp[g][0:127, :], b_vcmp[g])], 127, after1)
                    flush()
                    for tt in range(16):
                        S.op("pe", lambda e: e.matmul(sp_[0][:, tt * 32:(tt + 1) * 32], Pacc[0:127, tt * 128:(tt + 1) * 128], cover[0:127, :], start=True, stop=True), [b_Pacc, b_cover], [b_sp[0]], inc=(tt == 15))
                    S.op("dve", lambda e: e.tensor_tensor(out=scs[:], in0=sp_[0][:], in1=nf01[:], op=ALU.mult), [b_sp[0], b_nfa], [b_scs])
                    S.op("dve", lambda e: e.tensor_tensor(out=scs[:], in0=scs[:], in1=addc[:], op=ALU.add), [b_scs, b_nfa], [b_scs])
                    for tt in range(16):
                        S.op("dve", lambda e: e.max(out=m8[:, tt * 8:(tt + 1) * 8], in_=scs[:, tt * 32:(tt + 1) * 32]), [b_scs], [b_m8])
                    for tt in range(16):
                        S.op("dve", lambda e: e.tensor_scalar(selb[:, tt * 32:(tt + 1) * 32], scs[:, tt * 32:(tt + 1) * 32], m8[:, tt * 8 + 7:tt * 8 + 8], -30000.0, op0=ALU.is_lt, op1=ALU.mult), [b_scs, b_m8], [b_selb])
                    for half in range(2):
                        for j in range(8):
                            tt = half * 8 + j
                            S.op("pe", lambda e: e.transpose(tp[:, j, :], selb[:, tt * 32:(tt + 1) * 32], self.ident[:]), [b_selb, self.b_ident], [b_tp], inc=(j == 7))
                        S.op("act", lambda e: e.activation(out=selbT[:, half * 1024:(half + 1) * 1024], in_=tp[:], func=AF.Copy), [b_tp], [b_selbT])
                    for hh in range(6):
                        h = g * 6 + hh
                        k2 = h % 2
                        g1, g2 = (1, 2) if hh % 2 == 0 else (0, 3)
                        oa, b_oa = (oacc, b_oacc) if hh % 2 == 0 else (oacc_b, b_oacc_b)
                        S.dma("sp", qt[k2][:], self.qscr_d[h, 1], self.b_qscr[h], b_qt[k2])
                        S.dma("sp", gate[g1][:], self.gl_d[h * 3 + 1:h * 3 + 2, :].partition_broadcast(128), self.b_gl, b_gate[g1])
                        S.dma("sp", gate[g2][:], self.gl_d[h * 3 + 2:h * 3 + 3, :].partition_broadcast(128), self.b_gl, b_gate[g2])
                        S.dma("sp", oa[:], self.ocmp_d[h], self.b_ocmp[h], b_oa)
                        for Q in range(4):
                            cs = slice(Q * 512, (Q + 1) * 512)
                            pairs = []
                            for kb in range(0, 4 * Q + 4):
                                biases = [(e_all[:, kb, :], selbT[:, cs], [b_eall, b_selbT])]
                                a = kb - 4 * Q
                                if a >= 0:
                                    biases.append((self.ident[:], wbias[:, 4 + a, :], [self.b_ident, b_wbias]))
                                pairs.append((ksT[g][:, kb * 128:(kb + 1) * 128], b_ksT[g], biases, vs[g][:, kb, :], b_vs[g], (128 * max(0, a), 512)))
                            attend(qt[k2], b_qt[k2], Q, pairs, 128, lambda ub, pk, Q=Q, g1=g1, oa=oa, b_oa=b_oa: finish(ub, g1, False, Q, oa, b_oa))
                            pairs = []
                            for kb in range(max(0, 4 * Q - 4), 4 * Q + 4):
                                a = kb - 4 * Q
                                pairs.append((kwT[g][:, kb * 128:(kb + 1) * 128], b_kwT[g], [(self.ident[:], wbias[:, 4 + a, :], [self.b_ident, b_wbias])], vw[g][:, kb, :], b_vw[g],
                                              (128 * max(0, a), 128 * (min(3, a + 4) + 1))))

                            def after2(ub, pk, Q=Q, g2=g2, oa=oa, b_oa=b_oa, h=h, k2=k2):
                                finish(ub, g2, False, Q, oa, b_oa)
                                if Q == 3:
                                    S.op("act", lambda e: e.activation(out=obf[k2][:], in_=oa[:], func=AF.Copy), [b_oa], [b_obf[k2]])
                                    S.dma("sp", self.mixT_d[h], obf[k2][:], b_obf[k2], self.b_mixT[h])
                            attend(qt[k2], b_qt[k2], Q, pairs, 128, after2)
                    flush()
                S.barrier()


def make_in_map(inputs, b, consts):
    m = {
        "x": np.ascontiguousarray(inputs["x"][b]),
        "pos": np.ascontiguousarray(inputs["positions"][b:b + 1]).astype(np.int32),
        "norm_w": np.ascontiguousarray(inputs["norm_w"].reshape(24, D_)),
        "ffn_w_gate": inputs["ffn_w_gate"].reshape(8, D_, FF),
        "ffn_w_up": inputs["ffn_w_up"].reshape(8, D_, FF),
        "ffn_w_down": inputs["ffn_w_down"].reshape(8, FF, D_),
        "ev_w_in": inputs["ev_w_in"], "ev_w_out": inputs["ev_w_out"], "pool_w": inputs["pool_w"],
        "pool_scale_t": np.ascontiguousarray(inputs["pool_scale"].reshape(2, 4, 128).transpose(0, 2, 1)),
        "od_w_in": inputs["od_w_in"], "od_w_out": inputs["od_w_out"],
        "cmp_w1_k": inputs["cmp_w1_k"], "cmp_w1_v": inputs["cmp_w1_v"],
        "cmp_w2_k": inputs["cmp_w2_k"], "cmp_w2_v": inputs["cmp_w2_v"],
        "cmp_peT_k": np.ascontiguousarray(inputs["cmp_pe_k"].transpose(0, 2, 1)),
        "cmp_peT_v": np.ascontiguousarray(inputs["cmp_pe_v"].transpose(0, 2, 1)),
        "conv_wT": np.ascontiguousarray(inputs["conv_w"].reshape(2, 3, 4, 128).transpose(0, 3, 2, 1).reshape(2, 128, 12)),
    }
    m.update(consts)
    return m


def run(inputs, cfg, cores):
    k = K(cfg)
    nc = k.build()
    consts = host_consts()
    inputs = {kk: np.asarray(v) for kk, v in inputs.items()}
    in_maps = [make_in_map(inputs, b, consts) for b in cores]
    import time as _t; _t0 = _t.time()
    try:
        res = run_bass_kernel_spmd(nc, in_maps, core_ids=list(range(len(cores))))
    finally:
        print("device call seconds", _t.time() - _t0)
    return res, k


def kernel(**inputs):
    res, _ = run(inputs, {}, list(range(8)))
    return np.stack([r["y"] for r in res.results], axis=0).astype(np.float32)
```

```python
import numpy as np
import ml_dtypes
from contextlib import ExitStack
import concourse.bass as bass
import concourse.mybir as mybir
from concourse.bass_utils import run_bass_kernel_spmd

F32 = mybir.dt.float32
BF16 = mybir.dt.bfloat16
I32 = mybir.dt.int32
AF = mybir.ActivationFunctionType
ALU = mybir.AluOpType

S_ = 2048
D_ = 2048
FF = 5632
NFC = FF // 128
EPS = 1e-6
SCALE = 128 ** -0.5
TWO_PI = 6.283185307179586
PI = 3.141592653589793


class Sem:
    __slots__ = ("h", "issued", "dma")

    def __init__(self, h, dma):
        self.h = h
        self.issued = 0
        self.dma = dma


class Buf:
    __slots__ = ("name", "w", "r", "dsem", "excl")

    def __init__(self, name="", excl=False):
        self.name = name
        self.w = None
        self.r = {}
        self.dsem = None
        self.excl = excl


def PB():
    return Buf("psum", True)


class Eng:
    def __init__(self, name, h, sem):
        self.name = name
        self.h = h
        self.sem = sem
        self.known = {}
        self.pending = False


class Sched:
    def __init__(self, nc, es, n_dsem=90):
        self.nc = nc
        mk = lambda nm, dma: Sem(es.enter_context(nc.semaphore(nm)), dma)
        self.engs = {
            "pe": Eng("pe", nc.tensor, mk("s_pe", False)),
            "act": Eng("act", nc.scalar, mk("s_act", False)),
            "dve": Eng("dve", nc.vector, mk("s_dve", False)),
            "pool": Eng("pool", nc.gpsimd, mk("s_pool", False)),
            "sp": Eng("sp", nc.sync, mk("s_sp", False)),
        }
        self.dsems = [mk("s_d%d" % i, True) for i in range(n_dsem)]
        self.dnext = 0
        self.n_ins = 0
        self.n_wait = 0

    def _need(self, need, tok):
        if tok is None:
            return
        sem, val = tok
        if sem.dma:
            val = sem.issued
        if need.get(sem, 0) < val:
            need[sem] = val

    def _sync(self, E, reads, writes, skip_w_sem=None):
        need = {}
        for b in reads:
            self._need(need, b.w)
            if b.excl:
                for sem, val in b.r.items():
                    if sem is not E.sem:
                        self._need(need, (sem, val))
        for b in writes:
            if not (skip_w_sem is not None and b.w is not None and b.w[0] is skip_w_sem):
                self._need(need, b.w)
            for sem, val in b.r.items():
                self._need(need, (sem, val))
        for sem, val in need.items():
            if sem is E.sem and E.name == "pe":
                continue
            if E.known.get(sem, 0) < val:
                E.h.wait_ge(sem.h, val)
                E.known[sem] = val
                self.n_wait += 1

    def op(self, eng, fn, reads=(), writes=(), inc=True):
        E = self.engs[eng]
        self._sync(E, reads, writes)
        ins = fn(E.h)
        self.n_ins += 1
        val = E.sem.issued + 1
        if inc:
            ins.then_inc(E.sem.h, 1)
            E.sem.issued = val
            E.pending = False
        else:
            E.pending = True
        for b in reads:
            if b.r.get(E.sem, 0) < val:
                b.r[E.sem] = val
        for b in writes:
            b.w = (E.sem, val)
            b.r = {}
        return ins

    def dma(self, q, out, in_, src, dst, **kw):
        E = self.engs[q]
        if dst.dsem is None:
            dst.dsem = self.dsems[self.dnext % len(self.dsems)]
            self.dnext += 1
        ds = dst.dsem
        self._sync(E, [src] if src is not None else [], [dst], skip_w_sem=ds)
        ins = E.h.dma_start(out=out, in_=in_, **kw)
        ins.then_inc(ds.h, 16)
        ds.issued += 16
        self.n_ins += 1
        dst.w = (ds, ds.issued)
        dst.r = {}
        if src is not None:
            src.r[ds] = ds.issued
        return ins

    def barrier(self):
        sp = self.engs["sp"]
        for E in self.engs.values():
            assert not E.pending, E.name
        allsems = [E.sem for E in self.engs.values() if E is not sp] + self.dsems
        for sem in allsems:
            if sp.known.get(sem, 0) < sem.issued:
                sp.h.wait_ge(sem.h, sem.issued)
                sp.known[sem] = sem.issued
        sp.h.nop().then_inc(sp.sem.h, 1)
        sp.sem.issued += 1
        for E in self.engs.values():
            if E is sp:
                continue
            E.h.wait_ge(sp.sem.h, sp.sem.issued)
            for sem in allsems + [sp.sem]:
                E.known[sem] = sem.issued
        sp.known[sp.sem] = sp.sem.issued


def host_consts():
    c = {}
    c["ident"] = np.eye(128, dtype=np.float32).astype(ml_dtypes.bfloat16)
    inv_freq = (1.0 / (10000.0 ** (np.arange(0, 128, 2, dtype=np.float32) / np.float32(128)))).astype(np.float32)
    c["invfreq"] = np.concatenate([inv_freq, inv_freq])[:, None].astype(np.float32)
    c["sinsign"] = np.concatenate([-np.ones(64), np.ones(64)])[:, None].astype(np.float32)
    kk = np.arange(128)[:, None]; qq = np.arange(128)[None, :]
    m_diag = (kk <= qq); m_prev = (kk >= qq); m_far = (kk > qq)
    bf = ml_dtypes.bfloat16
    c["mk_ev"] = np.concatenate([np.where(m_diag, 0.0, -30000.0), np.where(m_prev, 0.0, -30000.0)], axis=1).astype(np.float32).astype(bf)
    c["ones_bf"] = np.ones((128, 128), np.float32).astype(bf)
    invc = np.zeros((4, 16), np.float32)
    for g, w in enumerate((2, 4, 8, 16)):
        invc[g] = 1.0 / np.minimum(np.arange(1, 17), w)
    c["invc"] = np.broadcast_to(invc.reshape(1, 64), (128, 64)).copy()
    NEG = -30000.0
    b_diag = np.where(m_diag, 0.0, NEG); b_far = np.where(m_far, 0.0, NEG)
    full = np.zeros((128, 128)); none = np.full((128, 128), NEG)
    wb = np.zeros((128, 8, 4, 128), np.float32)
    for ai, a in enumerate(range(-4, 4)):
        for b in range(4):
            rel = b - a
            wb[:, ai, b, :] = none if (rel < 0 or rel > 4) else (b_diag if rel == 0 else (b_far if rel == 4 else full))
    c["wbias"] = wb.reshape(128, 8, 512).astype(bf)
    n_ = np.arange(128)[:, None]; t_ = np.arange(2048)[None, :]
    c["cbias"] = np.where(t_ >= 16 * n_ + 31, 0.0, NEG).astype(np.float32).astype(bf)
    j_ = np.arange(32)[None, :]
    c["cover"] = ((16 * n_ < 64 * j_ + 64) & (16 * n_ + 32 > 64 * j_)).astype(np.float32)
    t3 = (np.arange(16)[None, :, None] * 128 + np.arange(128)[:, None, None])
    cur = t3 // 64; j3 = np.arange(32)[None, None, :]
    vis = j3 <= cur; forced = vis & ((j3 == 0) | (j3 >= cur - 1))
    c["nf01"] = (vis & ~forced).astype(np.float32).reshape(128, 512)
    c["addc"] = np.where(forced, 1e9, np.where(vis, 0.0, -1e9)).astype(np.float32).reshape(128, 512)
    e_all = np.zeros((32, 16, 128), np.float32)
    for kb in range(16):
        e_all[2 * kb, kb, 0:64] = 1.0; e_all[2 * kb + 1, kb, 64:128] = 1.0
    c["e_all"] = e_all.astype(bf)
    return c


class K:
    def __init__(self, cfg):
        self.cfg = cfg
        nc = self.nc = bass.Bass("TRN2", target_bir_lowering=False)
        self.es = ExitStack()
        self.S = Sched(nc, self.es)
        dt = lambda name, shape, dtype=F32, kind="ExternalInput": nc.dram_tensor(name, shape, dtype, kind=kind).ap()
        self.x = dt("x", [S_, D_])
        self.pos = dt("pos", [1, S_], I32)
        self.norm_w = dt("norm_w", [24, D_])
        self.wg = dt("ffn_w_gate", [8, D_, FF])
        self.wu = dt("ffn_w_up", [8, D_, FF])
        self.wd = dt("ffn_w_down", [8, FF, D_])
        self.ident_d = dt("ident", [128, 128], BF16)
        self.invfreq_d = dt("invfreq", [128, 1])
        self.sinsign_d = dt("sinsign", [128, 1])
        self.ev_w_in = dt("ev_w_in", [2, D_, 5120])
        self.ev_w_out = dt("ev_w_out", [2, D_, D_])
        self.pool_w = dt("pool_w", [2, 4, 128, 128])
        self.pool_scale_t = dt("pool_scale_t", [2, 128, 4])
        self.mk_ev_d = dt("mk_ev", [128, 256], BF16)
        self.ones_d = dt("ones_bf", [128, 128], BF16)
        self.invc_d = dt("invc", [128, 64])
        self.rope_d = dt("rope_scr", [2, 128, S_], F32, kind="ExternalOutput")
        self.mixT_d = dt("mixT_scr", [16, 128, S_], BF16, kind="ExternalOutput")
        self.od_w_in = dt("od_w_in", [2, D_, 4644])
        self.od_w_out = dt("od_w_out", [2, D_, D_])
        self.cmp_w1 = [dt("cmp_w1_k", [2, 4096, 256]), dt("cmp_w1_v", [2, 4096, 256])]
        self.cmp_w2 = [dt("cmp_w2_k", [2, 256, 128]), dt("cmp_w2_v", [2, 256, 128])]
        self.cmp_peT = [dt("cmp_peT_k", [2, 128, 32]), dt("cmp_peT_v", [2, 128, 32])]
        self.conv_wT = dt("conv_wT", [2, 128, 12])
        self.wbias_d = dt("wbias", [128, 8, 512], BF16)
        self.cbias_d = dt("cbias", [128, S_], BF16)
        self.cover_d = dt("cover", [128, 32])
        self.nf01_d = dt("nf01", [128, 512])
        self.addc_d = dt("addc", [128, 512])
        self.e_all_d = dt("e_all", [32, 16, 128], BF16)
        self.gl_d = dt("gl_scr", [36, S_], F32, kind="ExternalOutput")
        self.qscr_d = dt("q_scr", [12, 2, 128, S_], BF16, kind="ExternalOutput")
        self.ocmp_d = dt("ocmp_scr", [12, 128, S_], F32, kind="ExternalOutput")
        self.b_gl = Buf("gl"); self.b_qscr = [Buf("qscr%d" % i) for i in range(12)]; self.b_ocmp = [Buf("ocmp%d" % i) for i in range(12)]
        self.b_rope = Buf("rope")
        self.b_mixT = [Buf("mixT%d" % i) for i in range(16)]
        self.y = dt("y", [S_, D_], F32, kind="ExternalOutput")
        self.ybuf = [Buf("y%d" % i) for i in range(16)]
        self.xbuf = Buf("x")

    def sb(self, st, name, shape, dtype):
        self._uid = getattr(self, "_uid", 0) + 1
        return st.enter_context(self.nc.sbuf_tensor("%s_%d" % (name, self._uid), shape, dtype))

    def pp(self, st, name, shape, dtype):
        self._uid = getattr(self, "_uid", 0) + 1
        return st.enter_context(self.nc.psum_tensor("%s_%d" % (name, self._uid), shape, dtype))

    def setup(self):
        nc, S, es = self.nc, self.S, self.es
        self.ident = self.sb(es, "ident_s", [128, 128], BF16)
        self.b_ident = Buf("ident")
        S.dma("sp", self.ident[:], self.ident_d, None, self.b_ident)
        self.ones = self.sb(es, "ones_s", [128, 128], BF16); self.b_ones = Buf("ones")
        S.dma("sp", self.ones[:], self.ones_d, None, self.b_ones)
        with ExitStack() as st:
            pi = self.sb(st, "rp_pi", [128, S_], I32); b_pi = Buf()
            pf = self.sb(st, "rp_pf", [128, S_], F32); b_pf = Buf()
            ang = self.sb(st, "rp_ang", [128, S_], F32); b_ang = Buf()
            kf = self.sb(st, "rp_kf", [128, S_], F32); b_kf = Buf()
            res = self.sb(st, "rp_res", [128, S_], F32); b_res = Buf()
            ivf = self.sb(st, "rp_ivf", [128, 2], F32); b_ivf = Buf()
            S.dma("sp", ivf[:, 0:1], self.invfreq_d, None, b_ivf)
            S.dma("sp", ivf[:, 1:2], self.sinsign_d, None, b_ivf)
            S.dma("sp", pi[:], self.pos.partition_broadcast(128), None, b_pi)
            S.op("dve", lambda e: e.tensor_copy(pf[:], pi[:]), [b_pi], [b_pf])
            for which in range(2):
                if which == 0:
                    S.op("dve", lambda e: e.tensor_scalar(ang[:], pf[:], ivf[:, 0:1], PI / 2, op0=ALU.mult, op1=ALU.add), [b_pf, b_ivf], [b_ang])
                else:
                    S.op("dve", lambda e: e.tensor_scalar(ang[:], pf[:], ivf[:, 0:1], None, op0=ALU.mult), [b_pf, b_ivf], [b_ang])
                S.op("dve", lambda e: e.tensor_scalar(pi[:], ang[:], 1.0 / TWO_PI, None, op0=ALU.mult), [b_ang], [b_pi])
                S.op("dve", lambda e: e.tensor_copy(kf[:], pi[:]), [b_pi], [b_kf])
                S.op("dve", lambda e: e.scalar_tensor_tensor(out=ang[:], in0=kf[:], scalar=-TWO_PI, in1=ang[:], op0=ALU.mult, op1=ALU.add), [b_kf, b_ang], [b_ang])
                S.op("dve", lambda e: e.tensor_scalar(kf[:], ang[:], PI, TWO_PI, op0=ALU.is_gt, op1=ALU.mult), [b_ang], [b_kf])
                S.op("dve", lambda e: e.tensor_sub(ang[:], ang[:], kf[:]), [b_ang, b_kf], [b_ang])
                S.op("dve", lambda e: e.tensor_scalar(kf[:], ang[:], -PI, -TWO_PI, op0=ALU.is_lt, op1=ALU.mult), [b_ang], [b_kf])
                S.op("dve", lambda e: e.tensor_add(ang[:], ang[:], kf[:]), [b_ang, b_kf], [b_ang])
                S.op("act", lambda e: e.activation(out=res[:], in_=ang[:], func=AF.Sin), [b_ang], [b_res])
                if which == 1:
                    S.op("dve", lambda e: e.tensor_scalar(res[:], res[:], ivf[:, 1:2], None, op0=ALU.mult), [b_res, b_ivf], [b_res])
                S.dma("sp", self.rope_d[which], res[:], b_res, self.b_rope)
            S.barrier()

    def pre_norm(self, st, src, src_bufs, nw_idx, tts, hnT, b_hnT):
        nc, S = self.nc, self.S
        wb = self.sb(st, "pn_wb", [128, D_], F32); b_wb = Buf()
        S.dma("pool", wb[:], self.norm_w[nw_idx:nw_idx + 1, :].partition_broadcast(128), None, b_wb)
        NB_ = 3
        hb = [self.sb(st, "pn_h%d" % i, [128, D_], F32) for i in range(NB_)]
        b_hb = [Buf() for _ in range(NB_)]
        hn = [self.sb(st, "pn_hn%d" % i, [128, D_], BF16) for i in range(NB_)]
        b_hn = [Buf() for _ in range(NB_)]
        ssl = [self.sb(st, "pn_ss%d" % i, [128, 4], F32) for i in range(NB_)]; b_ssl = [Buf() for _ in range(NB_)]
        pst = [self.pp(st, "pn_ps%d" % i, [128, 8, 128], BF16) for i in range(2)]
        b_pst = [PB(), PB()]
        def stage_a(i, tt):
            h_t, bh = hb[i % NB_], b_hb[i % NB_]
            hn_t, bn = hn[i % NB_], b_hn[i % NB_]
            ss, b_ss = ssl[i % NB_], b_ssl[i % NB_]
            S.dma("sp" if i % 2 == 0 else "pool", h_t[:], src(tt), src_bufs(tt), bh)
            S.op("dve", lambda e: e.memset(ss[:, 0:1], 0.0), [], [b_ss])
            S.op("act", lambda e: e.activation(out=hn_t[:], in_=h_t[:], func=AF.Square, accum_out=ss[:, 0:1]), [bh], [bn, b_ss])
            S.op("act", lambda e: e.activation(out=ss[:, 1:2], in_=ss[:, 0:1], func=AF.Sqrt, scale=1.0 / D_, bias=self.eps_t[:, 0:1]), [b_ss], [b_ss])
            S.op("dve", lambda e: e.reciprocal(ss[:, 2:3], ss[:, 1:2]), [b_ss], [b_ss])
            S.op("dve", lambda e: e.scalar_tensor_tensor(out=hn_t[:], in0=h_t[:], scalar=ss[:, 2:3], in1=wb[:], op0=ALU.mult, op1=ALU.mult), [bh, b_ss, b_wb], [bn])

        def stage_b(i, tt):
            hn_t, bn = hn[i % NB_], b_hn[i % NB_]
            for half in range(2):
                for j in range(8):
                    c0 = (half * 8 + j) * 128
                    S.op("pe", lambda e: e.transpose(pst[half][:, j, :], hn_t[:, c0:c0 + 128], self.ident[:]), [bn, self.b_ident], [b_pst[half]], inc=(j == 7))
                if half == 0:
                    S.op("act", lambda e: e.activation(out=hnT[:, half * 8:(half + 1) * 8, i * 128:(i + 1) * 128], in_=pst[half][:], func=AF.Copy), [b_pst[half]], [b_hnT])
                else:
                    S.op("dve", lambda e: e.tensor_copy(hnT[:, half * 8:(half + 1) * 8, i * 128:(i + 1) * 128], pst[half][:]), [b_pst[half]], [b_hnT])

        n = len(tts)
        for i in range(n + 1):
            if i < n:
                stage_a(i, tts[i])
            if i >= 1:
                stage_b(i - 1, tts[i - 1])

    def post_tile(self, tt, f_t, b_f, wb, b_wb, coef, h_t, b_h, h_src, h_src_buf, junk, b_junk, ss, b_ss, ldq="sp"):
        S = self.S
        if ldq is not None:
            S.dma(ldq, h_t[:], h_src, h_src_buf, b_h)
        S.op("dve", lambda e: e.memset(ss[:, 0:1], 0.0), [], [b_ss])
        S.op("act", lambda e: e.activation(out=junk[:], in_=f_t[:], func=AF.Square, accum_out=ss[:, 0:1]), [b_f], [b_junk, b_ss])
        S.op("act", lambda e: e.activation(out=ss[:, 1:2], in_=ss[:, 0:1], func=AF.Sqrt, scale=1.0 / D_, bias=self.eps_t[:, 0:1]), [b_ss], [b_ss])
        S.op("dve", lambda e: e.reciprocal(ss[:, 2:3], ss[:, 1:2]), [b_ss], [b_ss])
        if coef != 1.0:
            S.op("dve", lambda e: e.tensor_scalar(ss[:, 2:3], ss[:, 2:3], coef, None, op0=ALU.mult), [b_ss], [b_ss])
        S.op("dve", lambda e: e.scalar_tensor_tensor(out=f_t[:], in0=f_t[:], scalar=ss[:, 2:3], in1=wb[:], op0=ALU.mult, op1=ALU.mult), [b_f, b_ss, b_wb], [b_f])
        S.op("dve", lambda e: e.tensor_add(f_t[:], f_t[:], h_t[:]), [b_f, b_h], [b_f])
        S.dma("sp", self.y[tt * 128:(tt + 1) * 128, :], f_t[:], b_f, self.ybuf[tt])

    def stream_src(self, first):
        if first:
            return (lambda tt: self.x[tt * 128:(tt + 1) * 128, :]), (lambda tt: None)
        return (lambda tt: self.y[tt * 128:(tt + 1) * 128, :]), (lambda tt: self.ybuf[tt])

    def ffn(self, fi, nw_pre, nw_post, first):
        nc, S = self.nc, self.S
        G = 1024
        NT = G // 128
        src, src_bufs = self.stream_src(first)
        wg_v = self.wg[fi].rearrange("(kc p) f -> p kc f", p=128)
        wu_v = self.wu[fi].rearrange("(kc p) f -> p kc f", p=128)
        wd_v = self.wd[fi].rearrange("(fc p) d -> p fc d", p=128)
        BW = 256
        with ExitStack() as st:
            aT = self.sb(st, "f_aT", [128, NFC, G], BF16); b_aT = Buf()
            wpost = self.sb(st, "f_wpost", [128, D_], F32); b_wpost = Buf()
            S.dma("sp", wpost[:], self.norm_w[nw_post:nw_post + 1, :].partition_broadcast(128), None, b_wpost)
            for g in range(S_ // G):
                tts = list(range(g * NT, (g + 1) * NT))
                with ExitStack() as sa:
                    hnT = self.sb(sa, "f_hnT", [128, 16, G], BF16); b_hnT = Buf()
                    with ExitStack() as st2:
                        self.pre_norm(st2, src, src_bufs, nw_pre, tts, hnT, b_hnT)
                        S.barrier()
                    wgb = [self.sb(sa, "f_wg%d" % i, [128, 16, BW], BF16) for i in range(2)]
                    wub = [self.sb(sa, "f_wu%d" % i, [128, 16, BW], BF16) for i in range(2)]
                    b_wgb = [Buf(), Buf()]; b_wub = [Buf(), Buf()]
                    sg = [self.sb(sa, "f_sg%d" % i, [128, 512], F32) for i in range(4)]
                    b_sg = [Buf() for _ in range(4)]
                    ps = [self.pp(sa, "f_ps%d" % i, [128, 512], F32) for i in range(8)]
                    b_ps = [PB() for _ in range(8)]
                    cnt = 0
                    for blk in range(FF // BW):
                        sl = blk % 2
                        S.dma("pool", wgb[sl][:], wg_v[:, :, blk * BW:(blk + 1) * BW], None, b_wgb[sl])
                        S.dma("pool", wub[sl][:], wu_v[:, :, blk * BW:(blk + 1) * BW], None, b_wub[sl])
                        for c in range(BW // 128):
                            fc = blk * (BW // 128) + c
                            for tb in range(G // 512):
                                pi_ = (cnt % 4) * 2
                                pg, pu, bpg, bpu = ps[pi_], ps[pi_ + 1], b_ps[pi_], b_ps[pi_ + 1]
                                for kc in range(16):
                                    S.op("pe", lambda e: e.matmul(pg[:], wgb[sl][:, kc, c * 128:(c + 1) * 128], hnT[:, kc, tb * 512:(tb + 1) * 512], start=(kc == 0), stop=(kc == 15)),
                                         [b_wgb[sl], b_hnT], [bpg], inc=(kc == 15))
                                for kc in range(16):
                                    S.op("pe", lambda e: e.matmul(pu[:], wub[sl][:, kc, c * 128:(c + 1) * 128], hnT[:, kc, tb * 512:(tb + 1) * 512], start=(kc == 0), stop=(kc == 15)),
                                         [b_wub[sl], b_hnT], [bpu], inc=(kc == 15))
                                sgt, bsg = sg[cnt % 4], b_sg[cnt % 4]
                                S.op("act", lambda e: e.activation(out=sgt[:], in_=pg[:], func=AF.Silu), [bpg], [bsg])
                                S.op("dve", lambda e: e.tensor_tensor(out=aT[:, fc, tb * 512:(tb + 1) * 512], in0=sgt[:], in1=pu[:], op=ALU.mult), [bsg, bpu], [b_aT])
                                cnt += 1
                    S.barrier()
                with ExitStack() as sb_:
                    NWD = 6
                    wdb = [self.sb(sb_, "f_wd%d" % i, [128, 512], BF16) for i in range(NWD)]
                    b_wdb = [Buf() for _ in range(NWD)]
                    fsb = [self.sb(sb_, "f_fsb%d" % i, [128, D_], F32) for i in range(NT)]
                    b_fsb = [Buf() for _ in range(NT)]
                    hp = [self.sb(sb_, "f_hp%d" % i, [128, D_], F32) for i in range(3)]
                    b_hp = [Buf() for _ in range(3)]
                    junk = [self.sb(sb_, "f_junk%d" % i, [128, D_], BF16) for i in range(2)]; b_junk = [Buf(), Buf()]
                    ss = [self.sb(sb_, "f_ss%d" % i, [128, 4], F32) for i in range(2)]; b_ss = [Buf(), Buf()]
                    ps = [self.pp(sb_, "f_pb%d" % i, [128, 512], F32) for i in range(8)]
                    b_ps = [PB() for _ in range(8)]
                    for t8 in range(3):
                        tt = g * NT + t8
                        S.dma("sp", hp[t8][:], src(tt), src_bufs(tt), b_hp[t8])
                    wcnt = 0
                    for dq in range(4):
                        for fc in range(NFC):
                            sl = wcnt % NWD; wcnt += 1
                            S.dma("pool", wdb[sl][:], wd_v[:, fc, dq * 512:(dq + 1) * 512], None, b_wdb[sl])
                            for t8 in range(NT):
                                S.op("pe", lambda e: e.matmul(ps[t8][:], aT[:, fc, t8 * 128:(t8 + 1) * 128], wdb[sl][:], start=(fc == 0), stop=(fc == NFC - 1)),
                                     [b_aT, b_wdb[sl]], [b_ps[t8]], inc=(t8 == NT - 1))
                        for t8 in range(NT):
                            if t8 % 2 == 0:
                                S.op("act", lambda e: e.activation(out=fsb[t8][:, dq * 512:(dq + 1) * 512], in_=ps[t8][:], func=AF.Copy), [b_ps[t8]], [b_fsb[t8]])
                            else:
                                S.op("dve", lambda e: e.tensor_copy(fsb[t8][:, dq * 512:(dq + 1) * 512], ps[t8][:]), [b_ps[t8]], [b_fsb[t8]])
                    for t8 in range(NT):
                        tt = g * NT + t8
                        self.post_tile(tt, fsb[t8], b_fsb[t8], wpost, b_wpost, 0.5, hp[t8 % 3], b_hp[t8 % 3], src(tt), src_bufs(tt), junk[t8 % 2], b_junk[t8 % 2], ss[t8 % 2], b_ss[t8 % 2], ldq=(None if t8 < 3 else "pool"))
                    S.barrier()

    def build(self):
        nc, S, es = self.nc, self.S, self.es
        cfg = self.cfg
        with es:
            self.eps_t = self.sb(es, "eps_t", [128, 1], F32)
            b_eps = Buf()
            S.op("dve", lambda e: e.memset(self.eps_t[:], EPS), [], [b_eps])
            self.setup()
            S.barrier()
            first = True
            n_sub = cfg.get("n_sub", 12)
            sub = 0
            for layer in range(4):
                for which in range(3):
                    if sub >= n_sub:
                        break
                    if which == 0:
                        self.ffn(layer * 2 + 0, layer * 6 + 0, layer * 6 + 1, first)
                    elif which == 1:
                        self.mixer(layer, first)
                    else:
                        self.ffn(layer * 2 + 1, layer * 6 + 4, layer * 6 + 5, first)
                    first = False
                    sub += 1
            S.barrier()
        return nc

    def mixer(self, layer, first):
        with ExitStack() as mst:
            self._mst = mst
            self._wo = None
            if layer % 2 == 0:
                self.even_mixer(layer, first)
            else:
                self.odd_mixer(layer, first)
            if self.cfg.get("ev_stage", 9) >= 5 and self.cfg.get("od_stage", 9) >= 3:
                self.out_proj(layer, first)
            self._wo = None

    def proj_slab(self, wslab, b_w, ncol, hnT, b_hnT, pj, b_pj, evac):
        S = self.S
        for half in range(2):
            for bk in range(2):
                c0 = half * 1024 + bk * 512
                for kc in range(16):
                    S.op("pe", lambda e: e.matmul(pj[bk][0:ncol, :], wslab[:, kc, 0:ncol], hnT[:, kc, c0:c0 + 512], start=(kc == 0), stop=(kc == 15)),
                         [b_w, b_hnT], [b_pj[bk]], inc=(kc == 15))
                evac(bk, c0)

    def even_mixer(self, layer, first):
        nc, S = self.nc, self.S
        i = layer // 2
        src, src_bufs = self.stream_src(first)
        w_in_v = self.ev_w_in[i].rearrange("(kc p) c -> p kc c", p=128)
        with ExitStack() as st:
            hnT = self.sb(st, "m_hnT", [128, 16, S_], BF16); b_hnT = Buf()
            with ExitStack() as st2:
                self.pre_norm(st2, src, src_bufs, layer * 6 + 2, list(range(16)), hnT, b_hnT)
                S.barrier()
            cosF = self.sb(st, "m_cos", [128, S_], F32); sinF = self.sb(st, "m_sin", [128, S_], F32); b_cs = Buf()
            S.dma("sp", cosF[:], self.rope_d[0], self.b_rope, b_cs)
            S.dma("sp", sinF[:], self.rope_d[1], self.b_rope, b_cs)
            mk = self.sb(st, "m_mk", [128, 256], BF16); b_mk = Buf()
            S.dma("sp", mk[:], self.mk_ev_d, None, b_mk)
            wsl = [[self.sb(st, "m_w%d_%d" % (j, k), [128, 16, 128], BF16) for k in range(2)] for j in range(3)]
            b_wsl = [[Buf(), Buf()] for j in range(3)]
            qT = self.sb(st, "m_qT", [128, S_], BF16); kT = self.sb(st, "m_kT", [128, S_], BF16); vT = self.sb(st, "m_vT", [128, S_], BF16)
            b_qT, b_kT, b_vT = Buf(), Buf(), Buf()
            vS = self.sb(st, "m_vS", [128, 48, 128], BF16); b_vS = Buf()
            acc = self.sb(st, "m_acc", [128, S_], F32); dacc = self.sb(st, "m_dacc", [128, S_], F32); b_acc, b_dacc = Buf(), Buf()
            qsw = [self.sb(st, "m_qsw%d" % k, [128, 512], F32) for k in range(2)]; b_qsw = [Buf(), Buf()]
            t1 = [self.sb(st, "m_t1%d" % k, [128, 512], F32) for k in range(2)]; b_t1 = [Buf(), Buf()]
            et = [self.sb(st, "m_et%d" % k, [128, 256], F32) for k in range(3)]; b_et = [Buf() for _ in range(3)]
            pt = [self.sb(st, "m_pt%d" % k, [128, 256], BF16) for k in range(3)]; b_pt = [Buf() for _ in range(3)]
            osb = [self.sb(st, "m_osb%d" % k, [128, S_], BF16) for k in range(2)]; b_osb = [Buf(), Buf()]
            pj = [self.pp(st, "m_pj%d" % k, [128, 512], F32) for k in range(2)]; b_pj = [PB(), PB()]
            vtp = self.pp(st, "m_vtp", [128, 8, 128], BF16); b_vtp = PB()
            sc = [self.pp(st, "m_sc%d" % k, [128, 512], F32) for k in range(3)]; b_sc = [PB() for _ in range(3)]
            ud = [self.pp(st, "m_ud%d" % k, [128, 512], F32) for k in range(2)]; b_ud = [PB(), PB()]
            rcnt = [0]

            def rope_evac(dst, b_dst):
                def f(bk, c0):
                    k = rcnt[0] % 2; rcnt[0] += 1
                    S.op("act", lambda e: e.activation(out=qsw[k][0:64, :], in_=pj[bk][64:128, :], func=AF.Copy), [b_pj[bk]], [b_qsw[k], b_pj[bk]])
                    S.op("act", lambda e: e.activation(out=qsw[k][64:128, :], in_=pj[bk][0:64, :], func=AF.Copy), [b_pj[bk]], [b_qsw[k], b_pj[bk]])
                    S.op("dve", lambda e: e.tensor_tensor(out=t1[k][:], in0=pj[bk][:], in1=cosF[:, c0:c0 + 512], op=ALU.mult), [b_pj[bk], b_cs], [b_t1[k], b_pj[bk]])
                    S.op("dve", lambda e: e.tensor_tensor(out=qsw[k][:], in0=qsw[k][:], in1=sinF[:, c0:c0 + 512], op=ALU.mult), [b_qsw[k], b_cs], [b_qsw[k]])
                    S.op("dve", lambda e: e.tensor_tensor(out=dst[:, c0:c0 + 512], in0=t1[k][:], in1=qsw[k][:], op=ALU.add), [b_t1[k], b_qsw[k]], [b_dst])
                return f

            def copy_evac(dst, b_dst):
                def f(bk, c0):
                    S.op("act", lambda e: e.activation(out=dst[:, c0:c0 + 512], in_=pj[bk][:], func=AF.Copy), [b_pj[bk]], [b_dst])
                return f

            blocks = []
            for pi_, d in enumerate((1, 4, 16)):
                nb = 16 // d
                for r in range(d):
                    for n in range(nb):
                        blocks.append((pi_, d, n * 128 * d + r, (n - 1) * 128 * d + r if n > 0 else None))
            blk_index = {(b[1], b[2]): bi for bi, b in enumerate(blocks)}
            sl_of = lambda start, d: slice(start, start + 127 * d + 1, d) if d > 1 else slice(start, start + 128)

            bcnt = 0
            from collections import deque
            epipe = deque(); ELA = 2
            stage = self.cfg.get("ev_stage", 9)
            for h in range(12 if stage >= 3 else (1 if stage >= 1 else 0)):
                k2 = h % 2
                sub_ = self.cfg.get("ev_sub", 9)
                tb_left = list(range(0, 48, 8))

                def emit_tb():
                    if not tb_left:
                        return
                    b0 = tb_left.pop(0)
                    for jj in range(8):
                        _, d, start, _ = blocks[b0 + jj]
                        S.op("pe", lambda e: e.transpose(vtp[:, jj, :], vT[:, sl_of(start, d)], self.ident[:]), [b_vT, self.b_ident], [b_vtp], inc=(jj == 7))
                    S.op("act", lambda e: e.activation(out=vS[:, b0:b0 + 8, :], in_=vtp[:], func=AF.Copy), [b_vtp], [b_vS])

                def with_tb(ev):
                    def f(bk, c0):
                        ev(bk, c0)
                        emit_tb()
                    return f
                NH_ = 12 if stage >= 3 else (1 if stage >= 1 else 0)
                for j, (dst, b_dst, col0) in ((2, (vT, b_vT, 3072 + h * 128)), (0, (qT, b_qT, h * 128)), (1, (kT, b_kT, 1536 + h * 128))):
                    if h == 0:
                        S.dma("pool", wsl[j][k2][:], w_in_v[:, :, col0:col0 + 128], None, b_wsl[j][k2])
                    if h + 1 < NH_:
                        S.dma("pool", wsl[j][1 - k2][:], w_in_v[:, :, col0 + 128:col0 + 256], None, b_wsl[j][1 - k2])
                    self.proj_slab(wsl[j][k2], b_wsl[j][k2], 128, hnT, b_hnT, pj, b_pj, with_tb(rope_evac(dst, b_dst)) if j < 2 else copy_evac(dst, b_dst))
                while tb_left:
                    emit_tb()
                for bi, (pi_, d, start, pstart) in enumerate(blocks if stage >= 2 else []):
                    k = bcnt % 3; bcnt += 1
                    W = 256 if pstart is not None else 128
                    qs = sl_of(start, d)
                    S.op("pe", lambda e: e.matmul(sc[k][:, 0:W], self.ident[:], mk[:, 0:W], start=True, stop=False), [self.b_ident, b_mk], [b_sc[k]], inc=False)
                    S.op("pe", lambda e: e.matmul(sc[k][:, 0:128], kT[:, qs], qT[:, qs], start=False, stop=(pstart is None)), [b_kT, b_qT], [b_sc[k]], inc=(pstart is None))
                    if pstart is not None:
                        S.op("pe", lambda e: e.matmul(sc[k][:, 128:256], kT[:, sl_of(pstart, d)], qT[:, qs], start=False, stop=True), [b_kT, b_qT], [b_sc[k]])
                    S.op("act", lambda e: e.activation(out=pt[k][:, 0:W], in_=sc[k][:, 0:W], func=AF.Exp, scale=SCALE), [b_sc[k]], [b_pt[k]])

                    def stage2(k=k, bi=bi, pi_=pi_, d=d, pstart=pstart, qs=qs):
                        ku = bi % 2
                        S.op("pe", lambda e: e.matmul(ud[ku][:, 0:128], vS[:, bi, :], pt[k][:, 0:128], start=True, stop=(pstart is None)), [b_vS, b_pt[k]], [b_ud[ku]], inc=False)
                        if pstart is not None:
                            pbi = blk_index[(d, pstart)]
                            S.op("pe", lambda e: e.matmul(ud[ku][:, 0:128], vS[:, pbi, :], pt[k][:, 128:256], start=False, stop=True), [b_vS, b_pt[k]], [b_ud[ku]], inc=False)
                        S.op("pe", lambda e: e.matmul(ud[ku][:, 128:256], self.ones[:], pt[k][:, 0:128], start=True, stop=(pstart is None)), [self.b_ones, b_pt[k]], [b_ud[ku]], inc=(pstart is None))
                        if pstart is not None:
                            S.op("pe", lambda e: e.matmul(ud[ku][:, 128:256], self.ones[:], pt[k][:, 128:256], start=False, stop=True), [self.b_ones, b_pt[k]], [b_ud[ku]])
                        if pi_ == 0:
                            S.op("act", lambda e: e.activation(out=acc[:, qs], in_=ud[ku][:, 0:128], func=AF.Copy), [b_ud[ku]], [b_acc])
                            S.op("act", lambda e: e.activation(out=dacc[:, qs], in_=ud[ku][:, 128:256], func=AF.Copy), [b_ud[ku]], [b_dacc])
                        else:
                            S.op("dve", lambda e: e.tensor_tensor(out=acc[:, qs], in0=acc[:, qs], in1=ud[ku][:, 0:128], op=ALU.add), [b_ud[ku], b_acc], [b_acc])
                            S.op("dve", lambda e: e.tensor_tensor(out=dacc[:, qs], in0=dacc[:, qs], in1=ud[ku][:, 128:256], op=ALU.add), [b_ud[ku], b_dacc], [b_dacc])
                    epipe.append(stage2)
                    while len(epipe) > ELA:
                        epipe.popleft()()
                while epipe:
                    epipe.popleft()()
                if sub_ < 4:
                    continue
                S.op("act", lambda e: e.activation(out=dacc[:], in_=dacc[:], func=AF.Ln), [b_dacc], [b_dacc])
                S.op("act", lambda e: e.activation(out=dacc[:], in_=dacc[:], func=AF.Exp, scale=-1.0), [b_dacc], [b_dacc])
                S.op("dve", lambda e: e.tensor_tensor(out=osb[k2][:], in0=acc[:], in1=dacc[:], op=ALU.mult), [b_acc, b_dacc], [b_osb[k2]])
                S.dma("sp", self.mixT_d[h], osb[k2][:], b_osb[k2], self.b_mixT[h])
            uT = self.sb(st, "m_uT", [128, S_], F32); b_uT = Buf()
            pw = self.sb(st, "m_pw", [128, 4, 128], BF16); b_pw = Buf()
            S.dma("pool", pw[:], self.pool_w[i].rearrange("g c d -> c g d"), None, b_pw)
            psc = self.sb(st, "m_psc", [128, 4], F32); b_psc = Buf()
            S.dma("sp", psc[:], self.pool_scale_t[i], None, b_psc)
            invc = self.sb(st, "m_invc", [128, 64], F32); b_invc = Buf()
            S.dma("sp", invc[:], self.invc_d, None, b_invc)
            for g, w in enumerate((2, 4, 8, 16) if stage >= 4 else ()):
                k2 = g % 2
                col0 = 4608 + g * 128
                S.dma("pool", wsl[0][k2][:], w_in_v[:, :, col0:col0 + 128], None, b_wsl[0][k2])
                self.proj_slab(wsl[0][k2], b_wsl[0][k2], 128, hnT, b_hnT, pj, b_pj, copy_evac(uT, b_uT))
                cur, b_cur = uT, b_uT
                pp2 = [(acc, b_acc), (dacc, b_dacc)]
                step = 1; n = 0
                while step < w:
                    nxt, b_nxt = pp2[n % 2]; n += 1
                    S.op("dve", lambda e: e.tensor_tensor(out=nxt[:, step:], in0=cur[:, step:], in1=cur[:, 0:S_ - step], op=ALU.add), [b_cur], [b_nxt])
                    S.op("act", lambda e: e.activation(out=nxt[:, 0:step], in_=cur[:, 0:step], func=AF.Copy), [b_cur], [b_nxt])
                    cur, b_cur = nxt, b_nxt; step *= 2
                S.op("dve", lambda e: e.scalar_tensor_tensor(out=qT[:, 16:], in0=cur[:, 16:], scalar=1.0 / w, in1=uT[:, 16:], op0=ALU.mult, op1=ALU.subtract), [b_cur, b_uT], [b_qT])
                S.op("dve", lambda e: e.tensor_tensor(out=cur[:, 0:16], in0=cur[:, 0:16], in1=invc[:, g * 16:(g + 1) * 16], op=ALU.mult), [b_cur, b_invc], [b_cur])
                S.op("dve", lambda e: e.tensor_tensor(out=qT[:, 0:16], in0=cur[:, 0:16], in1=uT[:, 0:16], op=ALU.subtract), [b_cur, b_uT], [b_qT])
                for bk4 in range(4):
                    c0 = bk4 * 512; bk = bk4 % 2
                    S.op("pe", lambda e: e.matmul(pj[bk][:], pw[:, g, :], qT[:, c0:c0 + 512], start=True, stop=True), [b_pw, b_qT], [b_pj[bk]])
                    S.op("dve", lambda e: e.tensor_scalar(osb[k2][:, c0:c0 + 512], pj[bk][:], psc[:, g:g + 1], None, op0=ALU.mult), [b_pj[bk], b_psc], [b_osb[k2]])
                S.dma("sp", self.mixT_d[12 + g], osb[k2][:], b_osb[k2], self.b_mixT[12 + g])
            S.barrier()

    def out_proj(self, layer, first):
        nc, S = self.nc, self.S
        i = layer // 2
        wout = (self.ev_w_out if layer % 2 == 0 else self.od_w_out)[i].rearrange("(fc p) d -> p fc d", p=128)
        src, src_bufs = self.stream_src(first)
        with ExitStack() as st:
            mixT = self.sb(st, "o_mixT", [128, 16, S_], BF16); b_mixT = Buf()
            for fc in range(16):
                S.dma("sp", mixT[:, fc, :], self.mixT_d[fc], self.b_mixT[fc], b_mixT)
            if self._wo is not None:
                wo, b_wo = self._wo
            else:
                wo = self.sb(st, "o_wo", [128, 16, D_], BF16); b_wo = Buf()
                for fc4 in range(4):
                    S.dma("pool", wo[:, fc4 * 4:(fc4 + 1) * 4, :], wout[:, fc4 * 4:(fc4 + 1) * 4, :], None, b_wo)
            fsb = [self.sb(st, "o_fsb%d" % k, [128, D_], F32) for k in range(2)]; b_fsb = [Buf(), Buf()]
            hp = [self.sb(st, "o_hp%d" % k, [128, D_], F32) for k in range(2)]; b_hp = [Buf(), Buf()]
            wpost = self.sb(st, "o_wpost", [128, D_], F32); b_wpost = Buf()
            S.dma("sp", wpost[:], self.norm_w[layer * 6 + 3:layer * 6 + 4, :].partition_broadcast(128), None, b_wpost)
            junk = [self.sb(st, "o_junk%d" % k, [128, D_], BF16) for k in range(2)]; b_junk = [Buf(), Buf()]
            ss = [self.sb(st, "o_ss%d" % k, [128, 4], F32) for k in range(2)]; b_ss = [Buf(), Buf()]
            ps = [self.pp(st, "o_ps%d" % k, [128, 512], F32) for k in range(8)]; b_ps = [PB() for _ in range(8)]
            for tt in range(16):
                par = tt % 2
                for db in range(4):
                    bi = par * 4 + db
                    for fc in range(16):
                        S.op("pe", lambda e: e.matmul(ps[bi][:], mixT[:, fc, tt * 128:(tt + 1) * 128], wo[:, fc, db * 512:(db + 1) * 512], start=(fc == 0), stop=(fc == 15)),
                             [b_mixT, b_wo], [b_ps[bi]], inc=(fc == 15))
                    if db % 2 == 0:
                        S.op("act", lambda e: e.activation(out=fsb[par][:, db * 512:(db + 1) * 512], in_=ps[bi][:], func=AF.Copy), [b_ps[bi]], [b_fsb[par]])
                    else:
                        S.op("dve", lambda e: e.tensor_copy(fsb[par][:, db * 512:(db + 1) * 512], ps[bi][:]), [b_ps[bi]], [b_fsb[par]])
                self.post_tile(tt, fsb[par], b_fsb[par], wpost, b_wpost, 1.0, hp[par], b_hp[par], src(tt), src_bufs(tt), junk[par], b_junk[par], ss[par], b_ss[par])
            S.barrier()

    def odd_mixer(self, layer, first):
        nc, S = self.nc, self.S
        i = layer // 2
        src, src_bufs = self.stream_src(first)
        w_in_v = self.od_w_in[i].rearrange("(kc p) c -> p kc c", p=128)
        C_Q, C_KC, C_VC, C_KS, C_VS, C_KW, C_VW, C_GL, C_U, C_CG, C_BG = 0, 1536, 1792, 2048, 2304, 2560, 2816, 3072, 3108, 3620, 4132
        stage = self.cfg.get("od_stage", 9)
        with ExitStack() as st:
            kcmpT = [self.sb(st, "n_kcmpT%d" % g, [128, 128], BF16) for g in range(2)]; b_kcmpT = [Buf(), Buf()]
            vcmp = [self.sb(st, "n_vcmp%d" % g, [128, 128], BF16) for g in range(2)]; b_vcmp = [Buf(), Buf()]
            ksT = [self.sb(st, "n_ksT%d" % g, [128, S_], BF16) for g in range(2)]; b_ksT = [Buf(), Buf()]
            kwT = [self.sb(st, "n_kwT%d" % g, [128, S_], BF16) for g in range(2)]; b_kwT = [Buf(), Buf()]
            vs = [self.sb(st, "n_vs%d" % g, [128, 16, 128], BF16) for g in range(2)]; b_vs = [Buf(), Buf()]
            vw = [self.sb(st, "n_vw%d" % g, [128, 16, 128], BF16) for g in range(2)]; b_vw = [Buf(), Buf()]
            with ExitStack() as s1:
                hnT = self.sb(s1, "n_hnT", [128, 16, S_], BF16); b_hnT = Buf()
                with ExitStack() as st2:
                    self.pre_norm(st2, src, src_bufs, layer * 6 + 2, list(range(16)), hnT, b_hnT)
                    S.barrier()
                cosF = self.sb(s1, "n_cos", [128, S_], F32); sinF = self.sb(s1, "n_sin", [128, S_], F32); b_cs = Buf()
                S.dma("sp", cosF[:], self.rope_d[0], self.b_rope, b_cs)
                S.dma("sp", sinF[:], self.rope_d[1], self.b_rope, b_cs)
                wsl = [self.sb(s1, "n_w%d" % k, [128, 16, 256], BF16) for k in range(2)]; b_wsl = [Buf(), Buf()]
                qsw = [self.sb(s1, "n_qsw%d" % k, [128, 512], F32) for k in range(2)]; b_qsw = [Buf(), Buf()]
                t1 = [self.sb(s1, "n_t1%d" % k, [128, 512], F32) for k in range(2)]; b_t1 = [Buf(), Buf()]
                f32a = self.sb(s1, "n_f32a", [128, S_], F32); b_f32a = Buf()
                f32b = self.sb(s1, "n_f32b", [128, S_], F32); b_f32b = Buf()
                f32c = self.sb(s1, "n_f32c", [128, S_], F32); b_f32c = Buf()
                ob = [self.sb(s1, "n_ob%d" % k, [128, S_], BF16) for k in range(4)]; b_ob = [Buf() for _ in range(4)]
                pj = [self.pp(s1, "n_pj%d" % k, [128, 512], F32) for k in range(2)]; b_pj = [PB(), PB()]
                px = [self.pp(s1, "n_px%d" % k, [128, 512], F32) for k in range(2)]; b_px = [PB(), PB()]
                rcnt = [0]; wcnt = [0]

                def rope_evac(dst, b_dst, udst=None, b_udst=None):
                    def f(bk, c0):
                        k = rcnt[0] % 2; rcnt[0] += 1
                        S.op("act", lambda e: e.activation(out=qsw[k][0:64, :], in_=pj[bk][64:128, :], func=AF.Copy), [b_pj[bk]], [b_qsw[k]])
                        S.op("act", lambda e: e.activation(out=qsw[k][64:128, :], in_=pj[bk][0:64, :], func=AF.Copy), [b_pj[bk]], [b_qsw[k]])
                        if udst is not None:
                            S.op("act", lambda e: e.activation(out=udst[:, c0:c0 + 512], in_=pj[bk][:], func=AF.Copy), [b_pj[bk]], [b_udst])
                        S.op("dve", lambda e: e.tensor_tensor(out=t1[k][:], in0=pj[bk][:], in1=cosF[:, c0:c0 + 512], op=ALU.mult), [b_pj[bk], b_cs], [b_t1[k]])
                        S.op("dve", lambda e: e.tensor_tensor(out=qsw[k][:], in0=qsw[k][:], in1=sinF[:, c0:c0 + 512], op=ALU.mult), [b_qsw[k], b_cs], [b_qsw[k]])
                        S.op("dve", lambda e: e.tensor_tensor(out=dst[:, c0:c0 + 512], in0=t1[k][:], in1=qsw[k][:], op=ALU.add), [b_t1[k], b_qsw[k]], [b_dst])
                    return f

                def copy_evac(dst, b_dst, np_=128, func=AF.Copy):
                    def f(bk, c0):
                        S.op("act", lambda e: e.activation(out=dst[0:np_, c0:c0 + 512], in_=pj[bk][0:np_, :], func=func), [b_pj[bk]], [b_dst])
                    return f

                plan = [(C_GL, 36)]
                for kv_ in range(2):
                    for g_ in range(2):
                        plan.append(((C_KC if kv_ == 0 else C_VC) + g_ * 128, 128))
                for g_ in range(2):
                    plan += [(C_KS + g_ * 128, 128), (C_KW + g_ * 128, 128)]
                plan += [(C_VS, 256), (C_VW, 256)]
                for c_ in range(4):
                    plan += [(C_U + c_ * 128, 128), (C_CG + c_ * 128, 128), (C_BG + c_ * 128, 128)]
                for h_ in range(12):
                    plan.append((C_Q + h_ * 128, 128))
                issued = [0]

                def ensure(i):
                    while issued[0] <= i and issued[0] < len(plan):
                        c0_, n_ = plan[issued[0]]
                        kk = issued[0] % 2
                        S.dma("pool", wsl[kk][:, :, 0:n_], w_in_v[:, :, c0_:c0_ + n_], None, b_wsl[kk])
                        issued[0] += 1

                def take(col0, ncol):
                    i = wcnt[0]; wcnt[0] += 1
                    assert plan[i] == (col0, ncol), (i, plan[i], col0, ncol)
                    ensure(i)
                    return i % 2

                def slab(col0, ncol, evac):
                    k = take(col0, ncol)
                    ensure(wcnt[0])
                    self.proj_slab(wsl[k], b_wsl[k], ncol, hnT, b_hnT, pj, b_pj, evac)

                slab(C_GL, 36, copy_evac(f32a, b_f32a, 36, AF.Sigmoid))
                S.dma("sp", self.gl_d, f32a[0:36, :], b_f32a, self.b_gl)
                w1 = self.sb(s1, "n_w1", [128, 32, 256], BF16); b_w1 = Buf()
                w2 = self.sb(s1, "n_w2", [128, 2, 128], BF16); b_w2 = Buf()
                peT = self.sb(s1, "n_peT", [128, 32], F32); b_peT = Buf()
                X = self.sb(s1, "n_X", [128, 32, 127], BF16); b_X = Buf()
                hidT = self.sb(s1, "n_hidT", [128, 2, 127], BF16); b_hidT = Buf()
                gx = [self.sb(s1, "n_gx%d" % k, [128, 127], F32) for k in range(3)]; b_gx = [Buf() for _ in range(3)]
                for kv in range(2):
                    S.dma("pool", w1[:], self.cmp_w1[kv][i].rearrange("(j p) h -> p j h", p=128), None, b_w1)
                    S.dma("pool", w2[:], self.cmp_w2[kv][i].rearrange("(hc p) d -> p hc d", p=128), None, b_w2)
                    S.dma("sp", peT[:], self.cmp_peT[kv][i], None, b_peT)
                    for g in range(2):
                        slab((C_KC if kv == 0 else C_VC) + g * 128, 128, copy_evac(f32a, b_f32a))
                        for j in range(32):
                            S.op("dve", lambda e: e.tensor_scalar(X[:, j, :], f32a[:, j:j + 16 * 126 + 1:16], peT[:, j:j + 1], None, op0=ALU.add), [b_f32a, b_peT], [b_X])
                        for hc in range(2):
                            for j in range(32):
                                S.op("pe", lambda e: e.matmul(px[hc][:, 0:127], w1[:, j, hc * 128:(hc + 1) * 128], X[:, j, :], start=(j == 0), stop=(j == 31)), [b_w1, b_X], [b_px[hc]], inc=(j == 31))
                            S.op("act", lambda e: e.activation(out=gx[0][:], in_=px[hc][:, 0:127], func=AF.Square), [b_px[hc]], [b_gx[0]])
                            S.op("dve", lambda e: e.tensor_scalar(gx[0][:], gx[0][:], 0.044715, 1.0, op0=ALU.mult, op1=ALU.add), [b_gx[0]], [b_gx[0]])
                            S.op("dve", lambda e: e.tensor_tensor(out=gx[1][:], in0=gx[0][:], in1=px[hc][:, 0:127], op=ALU.mult), [b_gx[0], b_px[hc]], [b_gx[1]])
                            S.op("act", lambda e: e.activation(out=gx[2][:], in_=gx[1][:], func=AF.Tanh, scale=0.7978845608028654), [b_gx[1]], [b_gx[2]])
                            S.op("dve", lambda e: e.scalar_tensor_tensor(out=gx[2][:], in0=gx[2][:], scalar=1.0, in1=px[hc][:, 0:127], op0=ALU.add, op1=ALU.mult), [b_gx[2], b_px[hc]], [b_gx[2]])
                            S.op("dve", lambda e: e.tensor_scalar(hidT[:, hc, :], gx[2][:], 0.5, None, op0=ALU.mult), [b_gx[2]], [b_hidT])
                        if kv == 0:
                            for hc in range(2):
                                S.op("pe", lambda e: e.matmul(px[0][:, 0:127], w2[:, hc, :], hidT[:, hc, :], start=(hc == 0), stop=(hc == 1)), [b_w2, b_hidT], [b_px[0]], inc=(hc == 1))
                            S.op("act", lambda e: e.activation(out=kcmpT[g][:, 0:127], in_=px[0][:, 0:127], func=AF.Copy), [b_px[0]], [b_kcmpT[g]])
                        else:
                            for hc in range(2):
                                S.op("pe", lambda e: e.matmul(px[0][0:127, 0:128], hidT[:, hc, :], w2[:, hc, :], start=(hc == 0), stop=(hc == 1)), [b_w2, b_hidT], [b_px[0]], inc=(hc == 1))
                            S.op("act", lambda e: e.activation(out=vcmp[g][0:127, :], in_=px[0][0:127, 0:128], func=AF.Copy), [b_px[0]], [b_vcmp[g]])
                for g in range(2):
                    slab(C_KS + g * 128, 128, rope_evac(ksT[g], b_ksT[g]))
                    slab(C_KW + g * 128, 128, rope_evac(kwT[g], b_kwT[g]))
                for (c0v, dsts, b_dsts) in ((C_VS, vs, b_vs), (C_VW, vw, b_vw)):
                    k = take(c0v, 256)
                    ensure(wcnt[0])
                    for t2 in range(8):
                        bk = t2 % 2
                        for u in range(2):
                            tt = t2 * 2 + u
                            for kc in range(16):
                                S.op("pe", lambda e: e.matmul(pj[bk][:, u * 256:(u + 1) * 256], hnT[:, kc, tt * 128:(tt + 1) * 128], wsl[k][:, kc, :], start=(kc == 0), stop=(kc == 15)),
                                     [b_hnT, b_wsl[k]], [b_pj[bk]], inc=(kc == 15))
                        for u in range(2):
                            tt = t2 * 2 + u
                            for g in range(2):
                                S.op("act", lambda e: e.activation(out=dsts[g][:, tt, :], in_=pj[bk][:, u * 256 + g * 128:u * 256 + (g + 1) * 128], func=AF.Copy), [b_pj[bk]], [b_dsts[g]])
                cw = self.sb(s1, "n_cw", [128, 12], F32); b_cw = Buf()
                S.dma("sp", cw[:], self.conv_wT[i], None, b_cw)
                for c in range(4):
                    slab(C_U + c * 128, 128, copy_evac(f32a, b_f32a))
                    slab(C_CG + c * 128, 128, copy_evac(f32b, b_f32b))
                    slab(C_BG + c * 128, 128, copy_evac(f32c, b_f32c))
                    S.op("dve", lambda e: e.tensor_tensor(out=f32a[:], in0=f32a[:], in1=f32b[:], op=ALU.mult), [b_f32a, b_f32b], [b_f32a])
                    S.op("dve", lambda e: e.tensor_scalar(f32b[:], f32a[:], cw[:, c * 3 + 2:c * 3 + 3], None, op0=ALU.mult), [b_f32a, b_cw], [b_f32b])
                    S.op("dve", lambda e: e.scalar_tensor_tensor(out=f32b[:, 1:], in0=f32a[:, 0:S_ - 1], scalar=cw[:, c * 3 + 1:c * 3 + 2], in1=f32b[:, 1:], op0=ALU.mult, op1=ALU.add), [b_f32a, b_cw, b_f32b], [b_f32b])
                    S.op("dve", lambda e: e.scalar_tensor_tensor(out=f32b[:, 2:], in0=f32a[:, 0:S_ - 2], scalar=cw[:, c * 3:c * 3 + 1], in1=f32b[:, 2:], op0=ALU.mult, op1=ALU.add), [b_f32a, b_cw, b_f32b], [b_f32b])
                    S.op("dve", lambda e: e.tensor_tensor(out=ob[c % 2][:], in0=f32b[:], in1=f32c[:], op=ALU.mult), [b_f32b, b_f32c], [b_ob[c % 2]])
                    S.dma("sp", self.mixT_d[12 + c], ob[c % 2][:], b_ob[c % 2], self.b_mixT[12 + c])
                for h in range(12):
                    k2 = h % 2
                    slab(C_Q + h * 128, 128, rope_evac(ob[2 + k2], b_ob[2 + k2], ob[k2], b_ob[k2]))
                    S.dma("sp", self.qscr_d[h, 0], ob[k2][:], b_ob[k2], self.b_qscr[h])
                    S.dma("sp", self.qscr_d[h, 1], ob[2 + k2][:], b_ob[2 + k2], self.b_qscr[h])
                S.barrier()
            if stage < 2:
                return
            self._uid += 1
            wo_t = self._mst.enter_context(nc.sbuf_tensor("o_wo_pf_%d" % self._uid, [128, 16, D_], BF16, side="right"))
            b_wo_t = Buf()
            wout_v = self.od_w_out[i].rearrange("(fc p) d -> p fc d", p=128)
            for fc4 in range(4):
                S.dma("pool", wo_t[:, fc4 * 4:(fc4 + 1) * 4, :], wout_v[:, fc4 * 4:(fc4 + 1) * 4, :], None, b_wo_t)
            self._wo = (wo_t, b_wo_t)
            with ExitStack() as s2:
                wbias = self.sb(s2, "a_wbias", [128, 8, 512], BF16); b_wbias = Buf()
                S.dma("sp", wbias[:], self.wbias_d, None, b_wbias)
                cbias = self.sb(s2, "a_cbias", [128, S_], BF16); b_cbias = Buf()
                S.dma("sp", cbias[:], self.cbias_d, None, b_cbias)
                cover = self.sb(s2, "a_cover", [128, 32], F32); b_cover = Buf()
                S.dma("sp", cover[:], self.cover_d, None, b_cover)
                nf01 = self.sb(s2, "a_nf01", [128, 512], F32); addc = self.sb(s2, "a_addc", [128, 512], F32); b_nfa = Buf()
                S.dma("sp", nf01[:], self.nf01_d, None, b_nfa)
                S.dma("sp", addc[:], self.addc_d, None, b_nfa)
                e_all = self.sb(s2, "a_eall", [32, 16, 128], BF16); b_eall = Buf()
                S.dma("sp", e_all[:], self.e_all_d, None, b_eall)
                Pacc = self.sb(s2, "a_Pacc", [128, S_], F32); b_Pacc = Buf()
                qt = [self.sb(s2, "a_q%d" % k, [128, S_], BF16) for k in range(2)]; b_qt = [Buf(), Buf()]
                gate = [self.sb(s2, "a_gate%d" % k, [128, S_], F32) for k in range(4)]; b_gate = [Buf() for _ in range(4)]
                oacc = self.sb(s2, "a_oacc", [128, S_], F32); b_oacc = Buf()
                oacc_b = self.sb(s2, "a_oaccb", [128, S_], F32); b_oacc_b = Buf()
                obf = [self.sb(s2, "a_obf%d" % k, [128, S_], BF16) for k in range(2)]; b_obf = [Buf(), Buf()]
                pt = [self.sb(s2, "a_pt%d" % k, [128, 512], BF16) for k in range(3)]; b_pt = [Buf() for _ in range(3)]
                rz = self.sb(s2, "a_rz", [128, 512], F32); b_rz = Buf()
                tq = self.sb(s2, "a_tq", [128, 512], F32); b_tq = Buf()
                scs = self.sb(s2, "a_scs", [128, 512], F32); b_scs = Buf()
                m8 = self.sb(s2, "a_m8", [128, 128], F32); b_m8 = Buf()
                selb = self.sb(s2, "a_selb", [128, 512], BF16); b_selb = Buf()
                selbT = self.sb(s2, "a_selbT", [32, S_], BF16); b_selbT = Buf()
                sp_ = [self.pp(s2, "a_s%d" % k, [128, 512], F32) for k in range(3)]; b_sp = [PB() for _ in range(3)]
                up = [self.pp(s2, "a_u%d" % k, [128, 512], F32) for k in range(2)]; b_up = [PB(), PB()]
                dp = [self.pp(s2, "a_d%d" % k, [128, 512], F32) for k in range(2)]; b_dp = [PB(), PB()]
                tp = self.pp(s2, "a_tp", [32, 8, 128], BF16); b_tp = PB()
                pcnt = [0]; ucnt = [0]

                from collections import deque
                pipe = deque(); LA = 2

                def push(fn):
                    pipe.append(fn)
                    while len(pipe) > LA:
                        pipe.popleft()()

                def flush():
                    while pipe:
                        pipe.popleft()()

                def attend(qsrc, b_q, Q, pairs, np_, after):
                    ub = ucnt[0] % 2; ucnt[0] += 1
                    n = len(pairs)
                    for pi_, pr in enumerate(pairs):
                        (kap, b_k, biases, vap, b_v) = pr[:5]
                        lo, hi = pr[5] if len(pr) > 5 else (0, 512)
                        k = pcnt[0] % 3; pcnt[0] += 1
                        nb_ = len(biases)
                        S.op("pe", lambda e: e.matmul(sp_[k][0:np_, lo:hi], kap, qsrc[:, Q * 512 + lo:Q * 512 + hi], start=True, stop=(nb_ == 0)), [b_k, b_q], [b_sp[k]], inc=(nb_ == 0))
                        for bi_, (bl, br, bb) in enumerate(biases):
                            S.op("pe", lambda e: e.matmul(sp_[k][0:np_, lo:hi], bl, br[:, lo:hi], start=False, stop=(bi_ == nb_ - 1)), bb, [b_sp[k]], inc=(bi_ == nb_ - 1))
                        S.op("act", lambda e: e.activation(out=pt[k][0:np_, lo:hi], in_=sp_[k][0:np_, lo:hi], func=AF.Exp, scale=SCALE), [b_sp[k]], [b_pt[k]])

                        def pv(k=k, pi_=pi_, vap=vap, b_v=b_v, lo=lo, hi=hi):
                            S.op("pe", lambda e: e.matmul(up[ub][:, lo:hi], vap, pt[k][0:np_, lo:hi], start=(pi_ == 0), stop=(pi_ == n - 1), skip_group_check=True), [b_v, b_pt[k]], [b_up[ub]], inc=False)
                            S.op("pe", lambda e: e.matmul(dp[ub][:, lo:hi], self.ones[0:np_, :], pt[k][0:np_, lo:hi], start=(pi_ == 0), stop=(pi_ == n - 1), skip_group_check=True), [self.b_ones, b_pt[k]], [b_dp[ub]], inc=True)
                            if pi_ == n - 1:
                                after(ub, k)
                        push(pv)

                def finish(ub, gidx, first_branch, Q, oa=None, b_oa=None):
                    cs = slice(Q * 512, (Q + 1) * 512)
                    if oa is None:
                        oa, b_oa = oacc, b_oacc
                    S.op("dve", lambda e: e.tensor_scalar(rz[:], dp[ub][:], 1e-30, None, op0=ALU.max), [b_dp[ub]], [b_rz])
                    S.op("act", lambda e: e.activation(out=rz[:], in_=rz[:], func=AF.Ln), [b_rz], [b_rz])
                    S.op("act", lambda e: e.activation(out=rz[:], in_=rz[:], func=AF.Exp, scale=-1.0), [b_rz], [b_rz])
                    S.op("dve", lambda e: e.tensor_tensor(out=tq[:], in0=up[ub][:], in1=rz[:], op=ALU.mult), [b_up[ub], b_rz], [b_tq])
                    if first_branch:
                        S.op("dve", lambda e: e.tensor_tensor(out=oa[:, cs], in0=tq[:], in1=gate[gidx][:, cs], op=ALU.mult), [b_tq, b_gate[gidx]], [b_oa])
                    else:
                        S.op("dve", lambda e: e.tensor_tensor(out=tq[:], in0=tq[:], in1=gate[gidx][:, cs], op=ALU.mult), [b_tq, b_gate[gidx]], [b_tq])
                        S.op("dve", lambda e: e.tensor_tensor(out=oa[:, cs], in0=oa[:, cs], in1=tq[:], op=ALU.add), [b_tq, b_oa], [b_oa])

                for g in range(2):
                    for hh in range(6):
                        h = g * 6 + hh
                        k2 = h % 2
                        gi = 0 if hh % 2 == 0 else 3
                        oa, b_oa = (oacc, b_oacc) if hh % 2 == 0 else (oacc_b, b_oacc_b)
                        S.dma("sp", qt[k2][:], self.qscr_d[h, 0], self.b_qscr[h], b_qt[k2])
                        S.dma("sp", gate[gi][:], self.gl_d[h * 3:h * 3 + 1, :].partition_broadcast(128), self.b_gl, b_gate[gi])
                        for Q in range(4):
                            cs = slice(Q * 512, (Q + 1) * 512)

                            def after1(ub, pk, Q=Q, cs=cs, hh=hh, gi=gi, oa=oa, b_oa=b_oa, h=h):
                                finish(ub, gi, True, Q, oa, b_oa)
                                if hh == 0:
                                    S.op("dve", lambda e: e.tensor_tensor(out=Pacc[0:127, cs], in0=pt[pk][0:127, :], in1=rz[0:127, :], op=ALU.mult), [b_pt[pk], b_rz], [b_Pacc])
                                else:
                                    S.op("dve", lambda e: e.tensor_tensor(out=tq[0:127, :], in0=pt[pk][0:127, :], in1=rz[0:127, :], op=ALU.mult), [b_pt[pk], b_rz], [b_tq])
                                    S.op("dve", lambda e: e.tensor_tensor(out=Pacc[0:127, cs], in0=Pacc[0:127, cs], in1=tq[0:127, :], op=ALU.add), [b_tq, b_Pacc], [b_Pacc])
                                if Q == 3:
                                    S.dma("sp", self.ocmp_d[h], oa[:], b_oa, self.b_ocmp[h])
                            attend(qt[k2], b_qt[k2], Q,
                                   [(kcmpT[g][:, 0:127], b_kcmpT[g], [(self.ident[0:127, 0:127], cbias[0:127, cs], [self.b_ident, b_cbias])], vcmp[g][0:127, :], b_vcmp[g])], 127, after1)
                    flush()
                    for tt in range(16):
                        S.op("pe", lambda e: e.matmul(sp_[0][:, tt * 32:(tt + 1) * 32], Pacc[0:127, tt * 128:(tt + 1) * 128], cover[0:127, :], start=True, stop=True), [b_Pacc, b_cover], [b_sp[0]], inc=(tt == 15))
                    S.op("dve", lambda e: e.tensor_tensor(out=scs[:], in0=sp_[0][:], in1=nf01[:], op=ALU.mult), [b_sp[0], b_nfa], [b_scs])
                    S.op("dve", lambda e: e.tensor_tensor(out=scs[:], in0=scs[:], in1=addc[:], op=ALU.add), [b_scs, b_nfa], [b_scs])
                    for tt in range(16):
                        S.op("dve", lambda e: e.max(out=m8[:, tt * 8:(tt + 1) * 8], in_=scs[:, tt * 32:(tt + 1) * 32]), [b_scs], [b_m8])
                    for tt in range(16):
                        S.op("dve", lambda e: e.tensor_scalar(selb[:, tt * 32:(tt + 1) * 32], scs[:, tt * 32:(tt + 1) * 32], m8[:, tt * 8 + 7:tt * 8 + 8], -30000.0, op0=ALU.is_lt, op1=ALU.mult), [b_scs, b_m8], [b_selb])
                    for half in range(2):
                        for j in range(8):
                            tt = half * 8 + j
                            S.op("pe", lambda e: e.transpose(tp[:, j, :], selb[:, tt * 32:(tt + 1) * 32], self.ident[:]), [b_selb, self.b_ident], [b_tp], inc=(j == 7))
                        S.op("act", lambda e: e.activation(out=selbT[:, half * 1024:(half + 1) * 1024], in_=tp[:], func=AF.Copy), [b_tp], [b_selbT])
                    for hh in range(6):
                        h = g * 6 + hh
                        k2 = h % 2
                        g1, g2 = (1, 2) if hh % 2 == 0 else (0, 3)
                        oa, b_oa = (oacc, b_oacc) if hh % 2 == 0 else (oacc_b, b_oacc_b)
                        S.dma("sp", qt[k2][:], self.qscr_d[h, 1], self.b_qscr[h], b_qt[k2])
                        S.dma("sp", gate[g1][:], self.gl_d[h * 3 + 1:h * 3 + 2, :].partition_broadcast(128), self.b_gl, b_gate[g1])
                        S.dma("sp", gate[g2][:], self.gl_d[h * 3 + 2:h * 3 + 3, :].partition_broadcast(128), self.b_gl, b_gate[g2])
                        S.dma("sp", oa[:], self.ocmp_d[h], self.b_ocmp[h], b_oa)
                        for Q in range(4):
                            cs = slice(Q * 512, (Q + 1) * 512)
                            pairs = []
                            for kb in range(0, 4 * Q + 4):
                                biases = [(e_all[:, kb, :], selbT[:, cs], [b_eall, b_selbT])]
                                a = kb - 4 * Q
                                if a >= 0:
                                    biases.append((self.ident[:], wbias[:, 4 + a, :], [self.b_ident, b_wbias]))
                                pairs.append((ksT[g][:, kb * 128:(kb + 1) * 128], b_ksT[g], biases, vs[g][:, kb, :], b_vs[g], (128 * max(0, a), 512)))
                            attend(qt[k2], b_qt[k2], Q, pairs, 128, lambda ub, pk, Q=Q, g1=g1, oa=oa, b_oa=b_oa: finish(ub, g1, False, Q, oa, b_oa))
                            pairs = []
                            for kb in range(max(0, 4 * Q - 4), 4 * Q + 4):
                                a = kb - 4 * Q
                                pairs.append((kwT[g][:, kb * 128:(kb + 1) * 128], b_kwT[g], [(self.ident[:], wbias[:, 4 + a, :], [self.b_ident, b_wbias])], vw[g][:, kb, :], b_vw[g],
                                              (128 * max(0, a), 128 * (min(3, a + 4) + 1))))

                            def after2(ub, pk, Q=Q, g2=g2, oa=oa, b_oa=b_oa, h=h, k2=k2):
                                finish(ub, g2, False, Q, oa, b_oa)
                                if Q == 3:
                                    S.op("act", lambda e: e.activation(out=obf[k2][:], in_=oa[:], func=AF.Copy), [b_oa], [b_obf[k2]])
                                    S.dma("sp", self.mixT_d[h], obf[k2][:], b_obf[k2], self.b_mixT[h])
                            attend(qt[k2], b_qt[k2], Q, pairs, 128, after2)
                    flush()
                S.barrier()


def make_in_map(inputs, b, consts):
    m = {
        "x": np.ascontiguousarray(inputs["x"][b]),
        "pos": np.ascontiguousarray(inputs["positions"][b:b + 1]).astype(np.int32),
        "norm_w": np.ascontiguousarray(inputs["norm_w"].reshape(24, D_)),
        "ffn_w_gate": inputs["ffn_w_gate"].reshape(8, D_, FF),
        "ffn_w_up": inputs["ffn_w_up"].reshape(8, D_, FF),
        "ffn_w_down": inputs["ffn_w_down"].reshape(8, FF, D_),
        "ev_w_in": inputs["ev_w_in"], "ev_w_out": inputs["ev_w_out"], "pool_w": inputs["pool_w"],
        "pool_scale_t": np.ascontiguousarray(inputs["pool_scale"].reshape(2, 4, 128).transpose(0, 2, 1)),
        "od_w_in": inputs["od_w_in"], "od_w_out": inputs["od_w_out"],
        "cmp_w1_k": inputs["cmp_w1_k"], "cmp_w1_v": inputs["cmp_w1_v"],
        "cmp_w2_k": inputs["cmp_w2_k"], "cmp_w2_v": inputs["cmp_w2_v"],
        "cmp_peT_k": np.ascontiguousarray(inputs["cmp_pe_k"].transpose(0, 2, 1)),
        "cmp_peT_v": np.ascontiguousarray(inputs["cmp_pe_v"].transpose(0, 2, 1)),
        "conv_wT": np.ascontiguousarray(inputs["conv_w"].reshape(2, 3, 4, 128).transpose(0, 3, 2, 1).reshape(2, 128, 12)),
    }
    m.update(consts)
    return m


def run(inputs, cfg, cores):
    k = K(cfg)
    nc = k.build()
    consts = host_consts()
    inputs = {kk: np.asarray(v) for kk, v in inputs.items()}
    in_maps = [make_in_map(inputs, b, consts) for b in cores]
    import time as _t; _t0 = _t.time()
    try:
        res = run_bass_kernel_spmd(nc, in_maps, core_ids=list(range(len(cores))))
    finally:
        print("device call seconds", _t.time() - _t0)
    return res, k


def kernel(**inputs):
    res, _ = run(inputs, {}, list(range(8)))
    return np.stack([r["y"] for r in res.results], axis=0).astype(np.float32)
```

```python
import numpy as np
import ml_dtypes
from contextlib import ExitStack
import concourse.bass as bass
import concourse.mybir as mybir
from concourse.bass_utils import run_bass_kernel_spmd

F32 = mybir.dt.float32
BF16 = mybir.dt.bfloat16
I32 = mybir.dt.int32
AF = mybir.ActivationFunctionType
ALU = mybir.AluOpType

S_ = 2048
D_ = 2048
FF = 5632
NFC = FF // 128
EPS = 1e-6
SCALE = 128 ** -0.5
TWO_PI = 6.283185307179586
PI = 3.141592653589793


class Sem:
    __slots__ = ("h", "issued", "dma")

    def __init__(self, h, dma):
        self.h = h
        self.issued = 0
        self.dma = dma


class Buf:
    __slots__ = ("name", "w", "r", "dsem", "excl")

    def __init__(self, name="", excl=False):
        self.name = name
        self.w = None
        self.r = {}
        self.dsem = None
        self.excl = excl


def PB():
    return Buf("psum", True)


class Eng:
    def __init__(self, name, h, sem):
        self.name = name
        self.h = h
        self.sem = sem
        self.known = {}
        self.pending = False


class Sched:
    def __init__(self, nc, es, n_dsem=90):
        self.nc = nc
        mk = lambda nm, dma: Sem(es.enter_context(nc.semaphore(nm)), dma)
        self.engs = {
            "pe": Eng("pe", nc.tensor, mk("s_pe", False)),
            "act": Eng("act", nc.scalar, mk("s_act", False)),
            "dve": Eng("dve", nc.vector, mk("s_dve", False)),
            "pool": Eng("pool", nc.gpsimd, mk("s_pool", False)),
            "sp": Eng("sp", nc.sync, mk("s_sp", False)),
        }
        self.dsems = [mk("s_d%d" % i, True) for i in range(n_dsem)]
        self.dnext = 0
        self.n_ins = 0
        self.n_wait = 0

    def _need(self, need, tok):
        if tok is None:
            return
        sem, val = tok
        if sem.dma:
            val = sem.issued
        if need.get(sem, 0) < val:
            need[sem] = val

    def _sync(self, E, reads, writes, skip_w_sem=None):
        need = {}
        for b in reads:
            self._need(need, b.w)
            if b.excl:
                for sem, val in b.r.items():
                    if sem is not E.sem:
                        self._need(need, (sem, val))
        for b in writes:
            if not (skip_w_sem is not None and b.w is not None and b.w[0] is skip_w_sem):
                self._need(need, b.w)
            for sem, val in b.r.items():
                self._need(need, (sem, val))
        for sem, val in need.items():
            if sem is E.sem and E.name == "pe":
                continue
            if E.known.get(sem, 0) < val:
                E.h.wait_ge(sem.h, val)
                E.known[sem] = val
                self.n_wait += 1

    def op(self, eng, fn, reads=(), writes=(), inc=True):
        E = self.engs[eng]
        self._sync(E, reads, writes)
        ins = fn(E.h)
        self.n_ins += 1
        val = E.sem.issued + 1
        if inc:
            ins.then_inc(E.sem.h, 1)
            E.sem.issued = val
            E.pending = False
        else:
            E.pending = True
        for b in reads:
            if b.r.get(E.sem, 0) < val:
                b.r[E.sem] = val
        for b in writes:
            b.w = (E.sem, val)
            b.r = {}
        return ins

    def dma(self, q, out, in_, src, dst, **kw):
        E = self.engs[q]
        if dst.dsem is None:
            dst.dsem = self.dsems[self.dnext % len(self.dsems)]
            self.dnext += 1
        ds = dst.dsem
        self._sync(E, [src] if src is not None else [], [dst], skip_w_sem=ds)
        ins = E.h.dma_start(out=out, in_=in_, **kw)
        ins.then_inc(ds.h, 16)
        ds.issued += 16
        self.n_ins += 1
        dst.w = (ds, ds.issued)
        dst.r = {}
        if src is not None:
            src.r[ds] = ds.issued
        return ins

    def barrier(self):
        sp = self.engs["sp"]
        for E in self.engs.values():
            assert not E.pending, E.name
        allsems = [E.sem for E in self.engs.values() if E is not sp] + self.dsems
        for sem in allsems:
            if sp.known.get(sem, 0) < sem.issued:
                sp.h.wait_ge(sem.h, sem.issued)
                sp.known[sem] = sem.issued
        sp.h.nop().then_inc(sp.sem.h, 1)
        sp.sem.issued += 1
        for E in self.engs.values():
            if E is sp:
                continue
            E.h.wait_ge(sp.sem.h, sp.sem.issued)
            for sem in allsems + [sp.sem]:
                E.known[sem] = sem.issued
        sp.known[sp.sem] = sp.sem.issued


def host_consts():
    c = {}
    c["ident"] = np.eye(128, dtype=np.float32).astype(ml_dtypes.bfloat16)
    inv_freq = (1.0 / (10000.0 ** (np.arange(0, 128, 2, dtype=np.float32) / np.float32(128)))).astype(np.float32)
    c["invfreq"] = np.concatenate([inv_freq, inv_freq])[:, None].astype(np.float32)
    c["sinsign"] = np.concatenate([-np.ones(64), np.ones(64)])[:, None].astype(np.float32)
    kk = np.arange(128)[:, None]; qq = np.arange(128)[None, :]
    m_diag = (kk <= qq); m_prev = (kk >= qq); m_far = (kk > qq)
    bf = ml_dtypes.bfloat16
    c["mk_ev"] = np.concatenate([np.where(m_diag, 0.0, -30000.0), np.where(m_prev, 0.0, -30000.0)], axis=1).astype(np.float32).astype(bf)
    c["ones_bf"] = np.ones((128, 128), np.float32).astype(bf)
    invc = np.zeros((4, 16), np.float32)
    for g, w in enumerate((2, 4, 8, 16)):
        invc[g] = 1.0 / np.minimum(np.arange(1, 17), w)
    c["invc"] = np.broadcast_to(invc.reshape(1, 64), (128, 64)).copy()
    NEG = -30000.0
    b_diag = np.where(m_diag, 0.0, NEG); b_far = np.where(m_far, 0.0, NEG)
    full = np.zeros((128, 128)); none = np.full((128, 128), NEG)
    wb = np.zeros((128, 8, 4, 128), np.float32)
    for ai, a in enumerate(range(-4, 4)):
        for b in range(4):
            rel = b - a
            wb[:, ai, b, :] = none if (rel < 0 or rel > 4) else (b_diag if rel == 0 else (b_far if rel == 4 else full))
    c["wbias"] = wb.reshape(128, 8, 512).astype(bf)
    n_ = np.arange(128)[:, None]; t_ = np.arange(2048)[None, :]
    c["cbias"] = np.where(t_ >= 16 * n_ + 31, 0.0, NEG).astype(np.float32).astype(bf)
    j_ = np.arange(32)[None, :]
    c["cover"] = ((16 * n_ < 64 * j_ + 64) & (16 * n_ + 32 > 64 * j_)).astype(np.float32)
    t3 = (np.arange(16)[None, :, None] * 128 + np.arange(128)[:, None, None])
    cur = t3 // 64; j3 = np.arange(32)[None, None, :]
    vis = j3 <= cur; forced = vis & ((j3 == 0) | (j3 >= cur - 1))
    c["nf01"] = (vis & ~forced).astype(np.float32).reshape(128, 512)
    c["addc"] = np.where(forced, 1e9, np.where(vis, 0.0, -1e9)).astype(np.float32).reshape(128, 512)
    e_all = np.zeros((32, 16, 128), np.float32)
    for kb in range(16):
        e_all[2 * kb, kb, 0:64] = 1.0; e_all[2 * kb + 1, kb, 64:128] = 1.0
    c["e_all"] = e_all.astype(bf)
    return c


class K:
    def __init__(self, cfg):
        self.cfg = cfg
        nc = self.nc = bass.Bass("TRN2", target_bir_lowering=False)
        self.es = ExitStack()
        self.S = Sched(nc, self.es)
        dt = lambda name, shape, dtype=F32, kind="ExternalInput": nc.dram_tensor(name, shape, dtype, kind=kind).ap()
        self.x = dt("x", [S_, D_])
        self.pos = dt("pos", [1, S_], I32)
        self.norm_w = dt("norm_w", [24, D_])
        self.wg = dt("ffn_w_gate", [8, D_, FF])
        self.wu = dt("ffn_w_up", [8, D_, FF])
        self.wd = dt("ffn_w_down", [8, FF, D_])
        self.ident_d = dt("ident", [128, 128], BF16)
        self.invfreq_d = dt("invfreq", [128, 1])
        self.sinsign_d = dt("sinsign", [128, 1])
        self.ev_w_in = dt("ev_w_in", [2, D_, 5120])
        self.ev_w_out = dt("ev_w_out", [2, D_, D_])
        self.pool_w = dt("pool_w", [2, 4, 128, 128])
        self.pool_scale_t = dt("pool_scale_t", [2, 128, 4])
        self.mk_ev_d = dt("mk_ev", [128, 256], BF16)
        self.ones_d = dt("ones_bf", [128, 128], BF16)
        self.invc_d = dt("invc", [128, 64])
        self.rope_d = dt("rope_scr", [2, 128, S_], F32, kind="ExternalOutput")
        self.mixT_d = dt("mixT_scr", [16, 128, S_], BF16, kind="ExternalOutput")
        self.od_w_in = dt("od_w_in", [2, D_, 4644])
        self.od_w_out = dt("od_w_out", [2, D_, D_])
        self.cmp_w1 = [dt("cmp_w1_k", [2, 4096, 256]), dt("cmp_w1_v", [2, 4096, 256])]
        self.cmp_w2 = [dt("cmp_w2_k", [2, 256, 128]), dt("cmp_w2_v", [2, 256, 128])]
        self.cmp_peT = [dt("cmp_peT_k", [2, 128, 32]), dt("cmp_peT_v", [2, 128, 32])]
        self.conv_wT = dt("conv_wT", [2, 128, 12])
        self.wbias_d = dt("wbias", [128, 8, 512], BF16)
        self.cbias_d = dt("cbias", [128, S_], BF16)
        self.cover_d = dt("cover", [128, 32])
        self.nf01_d = dt("nf01", [128, 512])
        self.addc_d = dt("addc", [128, 512])
        self.e_all_d = dt("e_all", [32, 16, 128], BF16)
        self.gl_d = dt("gl_scr", [36, S_], F32, kind="ExternalOutput")
        self.qscr_d = dt("q_scr", [12, 2, 128, S_], BF16, kind="ExternalOutput")
        self.ocmp_d = dt("ocmp_scr", [12, 128, S_], F32, kind="ExternalOutput")
        self.b_gl = Buf("gl"); self.b_qscr = [Buf("qscr%d" % i) for i in range(12)]; self.b_ocmp = [Buf("ocmp%d" % i) for i in range(12)]
        self.b_rope = Buf("rope")
        self.b_mixT = [Buf("mixT%d" % i) for i in range(16)]
        self.y = dt("y", [S_, D_], F32, kind="ExternalOutput")
        self.ybuf = [Buf("y%d" % i) for i in range(16)]
        self.xbuf = Buf("x")

    def sb(self, st, name, shape, dtype):
        self._uid = getattr(self, "_uid", 0) + 1
        return st.enter_context(self.nc.sbuf_tensor("%s_%d" % (name, self._uid), shape, dtype))

    def pp(self, st, name, shape, dtype):
        self._uid = getattr(self, "_uid", 0) + 1
        return st.enter_context(self.nc.psum_tensor("%s_%d" % (name, self._uid), shape, dtype))

    def setup(self):
        nc, S, es = self.nc, self.S, self.es
        self.ident = self.sb(es, "ident_s", [128, 128], BF16)
        self.b_ident = Buf("ident")
        S.dma("sp", self.ident[:], self.ident_d, None, self.b_ident)
        self.ones = self.sb(es, "ones_s", [128, 128], BF16); self.b_ones = Buf("ones")
        S.dma("sp", self.ones[:], self.ones_d, None, self.b_ones)
        with ExitStack() as st:
            pi = self.sb(st, "rp_pi", [128, S_], I32); b_pi = Buf()
            pf = self.sb(st, "rp_pf", [128, S_], F32); b_pf = Buf()
            ang = self.sb(st, "rp_ang", [128, S_], F32); b_ang = Buf()
            kf = self.sb(st, "rp_kf", [128, S_], F32); b_kf = Buf()
            res = self.sb(st, "rp_res", [128, S_], F32); b_res = Buf()
            ivf = self.sb(st, "rp_ivf", [128, 2], F32); b_ivf = Buf()
            S.dma("sp", ivf[:, 0:1], self.invfreq_d, None, b_ivf)
            S.dma("sp", ivf[:, 1:2], self.sinsign_d, None, b_ivf)
            S.dma("sp", pi[:], self.pos.partition_broadcast(128), None, b_pi)
            S.op("dve", lambda e: e.tensor_copy(pf[:], pi[:]), [b_pi], [b_pf])
            for which in range(2):
                if which == 0:
                    S.op("dve", lambda e: e.tensor_scalar(ang[:], pf[:], ivf[:, 0:1], PI / 2, op0=ALU.mult, op1=ALU.add), [b_pf, b_ivf], [b_ang])
                else:
                    S.op("dve", lambda e: e.tensor_scalar(ang[:], pf[:], ivf[:, 0:1], None, op0=ALU.mult), [b_pf, b_ivf], [b_ang])
                S.op("dve", lambda e: e.tensor_scalar(pi[:], ang[:], 1.0 / TWO_PI, None, op0=ALU.mult), [b_ang], [b_pi])
                S.op("dve", lambda e: e.tensor_copy(kf[:], pi[:]), [b_pi], [b_kf])
                S.op("dve", lambda e: e.scalar_tensor_tensor(out=ang[:], in0=kf[:], scalar=-TWO_PI, in1=ang[:], op0=ALU.mult, op1=ALU.add), [b_kf, b_ang], [b_ang])
                S.op("dve", lambda e: e.tensor_scalar(kf[:], ang[:], PI, TWO_PI, op0=ALU.is_gt, op1=ALU.mult), [b_ang], [b_kf])
                S.op("dve", lambda e: e.tensor_sub(ang[:], ang[:], kf[:]), [b_ang, b_kf], [b_ang])
                S.op("dve", lambda e: e.tensor_scalar(kf[:], ang[:], -PI, -TWO_PI, op0=ALU.is_lt, op1=ALU.mult), [b_ang], [b_kf])
                S.op("dve", lambda e: e.tensor_add(ang[:], ang[:], kf[:]), [b_ang, b_kf], [b_ang])
                S.op("act", lambda e: e.activation(out=res[:], in_=ang[:], func=AF.Sin), [b_ang], [b_res])
                if which == 1:
                    S.op("dve", lambda e: e.tensor_scalar(res[:], res[:], ivf[:, 1:2], None, op0=ALU.mult), [b_res, b_ivf], [b_res])
                S.dma("sp", self.rope_d[which], res[:], b_res, self.b_rope)
            S.barrier()

    def pre_norm(self, st, src, src_bufs, nw_idx, tts, hnT, b_hnT):
        nc, S = self.nc, self.S
        wb = self.sb(st, "pn_wb", [128, D_], F32); b_wb = Buf()
        S.dma("pool", wb[:], self.norm_w[nw_idx:nw_idx + 1, :].partition_broadcast(128), None, b_wb)
        NB_ = 3
        hb = [self.sb(st, "pn_h%d" % i, [128, D_], F32) for i in range(NB_)]
        b_hb = [Buf() for _ in range(NB_)]
        hn = [self.sb(st, "pn_hn%d" % i, [128, D_], BF16) for i in range(NB_)]
        b_hn = [Buf() for _ in range(NB_)]
        ssl = [self.sb(st, "pn_ss%d" % i, [128, 4], F32) for i in range(NB_)]; b_ssl = [Buf() for _ in range(NB_)]
        pst = [self.pp(st, "pn_ps%d" % i, [128, 8, 128], BF16) for i in range(2)]
        b_pst = [PB(), PB()]
        def stage_a(i, tt):
            h_t, bh = hb[i % NB_], b_hb[i % NB_]
            hn_t, bn = hn[i % NB_], b_hn[i % NB_]
            ss, b_ss = ssl[i % NB_], b_ssl[i % NB_]
            S.dma("sp" if i % 2 == 0 else "pool", h_t[:], src(tt), src_bufs(tt), bh)
            S.op("dve", lambda e: e.memset(ss[:, 0:1], 0.0), [], [b_ss])
            S.op("act", lambda e: e.activation(out=hn_t[:], in_=h_t[:], func=AF.Square, accum_out=ss[:, 0:1]), [bh], [bn, b_ss])
            S.op("act", lambda e: e.activation(out=ss[:, 1:2], in_=ss[:, 0:1], func=AF.Sqrt, scale=1.0 / D_, bias=self.eps_t[:, 0:1]), [b_ss], [b_ss])
            S.op("dve", lambda e: e.reciprocal(ss[:, 2:3], ss[:, 1:2]), [b_ss], [b_ss])
            S.op("dve", lambda e: e.scalar_tensor_tensor(out=hn_t[:], in0=h_t[:], scalar=ss[:, 2:3], in1=wb[:], op0=ALU.mult, op1=ALU.mult), [bh, b_ss, b_wb], [bn])

        def stage_b(i, tt):
            hn_t, bn = hn[i % NB_], b_hn[i % NB_]
            for half in range(2):
                for j in range(8):
                    c0 = (half * 8 + j) * 128
                    S.op("pe", lambda e: e.transpose(pst[half][:, j, :], hn_t[:, c0:c0 + 128], self.ident[:]), [bn, self.b_ident], [b_pst[half]], inc=(j == 7))
                if half == 0:
                    S.op("act", lambda e: e.activation(out=hnT[:, half * 8:(half + 1) * 8, i * 128:(i + 1) * 128], in_=pst[half][:], func=AF.Copy), [b_pst[half]], [b_hnT])
                else:
                    S.op("dve", lambda e: e.tensor_copy(hnT[:, half * 8:(half + 1) * 8, i * 128:(i + 1) * 128], pst[half][:]), [b_pst[half]], [b_hnT])

        n = len(tts)
        for i in range(n + 1):
            if i < n:
                stage_a(i, tts[i])
            if i >= 1:
                stage_b(i - 1, tts[i - 1])

    def post_tile(self, tt, f_t, b_f, wb, b_wb, coef, h_t, b_h, h_src, h_src_buf, junk, b_junk, ss, b_ss, ldq="sp"):
        S = self.S
        S.dma(ldq, h_t[:], h_src, h_src_buf, b_h)
        S.op("dve", lambda e: e.memset(ss[:, 0:1], 0.0), [], [b_ss])
        S.op("act", lambda e: e.activation(out=junk[:], in_=f_t[:], func=AF.Square, accum_out=ss[:, 0:1]), [b_f], [b_junk, b_ss])
        S.op("act", lambda e: e.activation(out=ss[:, 1:2], in_=ss[:, 0:1], func=AF.Sqrt, scale=1.0 / D_, bias=self.eps_t[:, 0:1]), [b_ss], [b_ss])
        S.op("dve", lambda e: e.reciprocal(ss[:, 2:3], ss[:, 1:2]), [b_ss], [b_ss])
        if coef != 1.0:
            S.op("dve", lambda e: e.tensor_scalar(ss[:, 2:3], ss[:, 2:3], coef, None, op0=ALU.mult), [b_ss], [b_ss])
        S.op("dve", lambda e: e.scalar_tensor_tensor(out=f_t[:], in0=f_t[:], scalar=ss[:, 2:3], in1=wb[:], op0=ALU.mult, op1=ALU.mult), [b_f, b_ss, b_wb], [b_f])
        S.op("dve", lambda e: e.tensor_add(f_t[:], f_t[:], h_t[:]), [b_f, b_h], [b_f])
        S.dma("sp", self.y[tt * 128:(tt + 1) * 128, :], f_t[:], b_f, self.ybuf[tt])

    def stream_src(self, first):
        if first:
            return (lambda tt: self.x[tt * 128:(tt + 1) * 128, :]), (lambda tt: None)
        return (lambda tt: self.y[tt * 128:(tt + 1) * 128, :]), (lambda tt: self.ybuf[tt])

    def ffn(self, fi, nw_pre, nw_post, first):
        nc, S = self.nc, self.S
        G = 1024
        NT = G // 128
        src, src_bufs = self.stream_src(first)
        wg_v = self.wg[fi].rearrange("(kc p) f -> p kc f", p=128)
        wu_v = self.wu[fi].rearrange("(kc p) f -> p kc f", p=128)
        wd_v = self.wd[fi].rearrange("(fc p) d -> p fc d", p=128)
        BW = 256
        with ExitStack() as st:
            aT = self.sb(st, "f_aT", [128, NFC, G], BF16); b_aT = Buf()
            wpost = self.sb(st, "f_wpost", [128, D_], F32); b_wpost = Buf()
            S.dma("sp", wpost[:], self.norm_w[nw_post:nw_post + 1, :].partition_broadcast(128), None, b_wpost)
            for g in range(S_ // G):
                tts = list(range(g * NT, (g + 1) * NT))
                with ExitStack() as sa:
                    hnT = self.sb(sa, "f_hnT", [128, 16, G], BF16); b_hnT = Buf()
                    with ExitStack() as st2:
                        self.pre_norm(st2, src, src_bufs, nw_pre, tts, hnT, b_hnT)
                        S.barrier()
                    wgb = [self.sb(sa, "f_wg%d" % i, [128, 16, BW], BF16) for i in range(2)]
                    wub = [self.sb(sa, "f_wu%d" % i, [128, 16, BW], BF16) for i in range(2)]
                    b_wgb = [Buf(), Buf()]; b_wub = [Buf(), Buf()]
                    sg = [self.sb(sa, "f_sg%d" % i, [128, 512], F32) for i in range(4)]
                    b_sg = [Buf() for _ in range(4)]
                    ps = [self.pp(sa, "f_ps%d" % i, [128, 512], F32) for i in range(8)]
                    b_ps = [PB() for _ in range(8)]
                    cnt = 0
                    for blk in range(FF // BW):
                        sl = blk % 2
                        S.dma("pool", wgb[sl][:], wg_v[:, :, blk * BW:(blk + 1) * BW], None, b_wgb[sl])
                        S.dma("pool", wub[sl][:], wu_v[:, :, blk * BW:(blk + 1) * BW], None, b_wub[sl])
                        for c in range(BW // 128):
                            fc = blk * (BW // 128) + c
                            for tb in range(G // 512):
                                pi_ = (cnt % 4) * 2
                                pg, pu, bpg, bpu = ps[pi_], ps[pi_ + 1], b_ps[pi_], b_ps[pi_ + 1]
                                for kc in range(16):
                                    S.op("pe", lambda e: e.matmul(pg[:], wgb[sl][:, kc, c * 128:(c + 1) * 128], hnT[:, kc, tb * 512:(tb + 1) * 512], start=(kc == 0), stop=(kc == 15)),
                                         [b_wgb[sl], b_hnT], [bpg], inc=(kc == 15))
                                for kc in range(16):
                                    S.op("pe", lambda e: e.matmul(pu[:], wub[sl][:, kc, c * 128:(c + 1) * 128], hnT[:, kc, tb * 512:(tb + 1) * 512], start=(kc == 0), stop=(kc == 15)),
                                         [b_wub[sl], b_hnT], [bpu], inc=(kc == 15))
                                sgt, bsg = sg[cnt % 4], b_sg[cnt % 4]
                                S.op("act", lambda e: e.activation(out=sgt[:], in_=pg[:], func=AF.Silu), [bpg], [bsg])
                                S.op("dve", lambda e: e.tensor_tensor(out=aT[:, fc, tb * 512:(tb + 1) * 512], in0=sgt[:], in1=pu[:], op=ALU.mult), [bsg, bpu], [b_aT])
                                cnt += 1
                    S.barrier()
                with ExitStack() as sb_:
                    NWD = 6
                    wdb = [self.sb(sb_, "f_wd%d" % i, [128, 512], BF16) for i in range(NWD)]
                    b_wdb = [Buf() for _ in range(NWD)]
                    fsb = [self.sb(sb_, "f_fsb%d" % i, [128, D_], F32) for i in range(NT)]
                    b_fsb = [Buf() for _ in range(NT)]
                    hp = [self.sb(sb_, "f_hp%d" % i, [128, D_], F32) for i in range(3)]
                    b_hp = [Buf() for _ in range(3)]
                    junk = [self.sb(sb_, "f_junk%d" % i, [128, D_], BF16) for i in range(2)]; b_junk = [Buf(), Buf()]
                    ss = [self.sb(sb_, "f_ss%d" % i, [128, 4], F32) for i in range(2)]; b_ss = [Buf(), Buf()]
                    ps = [self.pp(sb_, "f_pb%d" % i, [128, 512], F32) for i in range(8)]
                    b_ps = [PB() for _ in range(8)]
                    wcnt = 0
                    for dq in range(4):
                        for fc in range(NFC):
                            sl = wcnt % NWD; wcnt += 1
                            S.dma("pool", wdb[sl][:], wd_v[:, fc, dq * 512:(dq + 1) * 512], None, b_wdb[sl])
                            for t8 in range(NT):
                                S.op("pe", lambda e: e.matmul(ps[t8][:], aT[:, fc, t8 * 128:(t8 + 1) * 128], wdb[sl][:], start=(fc == 0), stop=(fc == NFC - 1)),
                                     [b_aT, b_wdb[sl]], [b_ps[t8]], inc=(t8 == NT - 1))
                        for t8 in range(NT):
                            if t8 % 2 == 0:
                                S.op("act", lambda e: e.activation(out=fsb[t8][:, dq * 512:(dq + 1) * 512], in_=ps[t8][:], func=AF.Copy), [b_ps[t8]], [b_fsb[t8]])
                            else:
                                S.op("dve", lambda e: e.tensor_copy(fsb[t8][:, dq * 512:(dq + 1) * 512], ps[t8][:]), [b_ps[t8]], [b_fsb[t8]])
                    for t8 in range(NT):
                        tt = g * NT + t8
                        self.post_tile(tt, fsb[t8], b_fsb[t8], wpost, b_wpost, 0.5, hp[t8 % 3], b_hp[t8 % 3], src(tt), src_bufs(tt), junk[t8 % 2], b_junk[t8 % 2], ss[t8 % 2], b_ss[t8 % 2], ldq="pool")
                    S.barrier()

    def build(self):
        nc, S, es = self.nc, self.S, self.es
        cfg = self.cfg
        with es:
            self.eps_t = self.sb(es, "eps_t", [128, 1], F32)
            b_eps = Buf()
            S.op("dve", lambda e: e.memset(self.eps_t[:], EPS), [], [b_eps])
            self.setup()
            S.barrier()
            first = True
            n_sub = cfg.get("n_sub", 12)
            sub = 0
            for layer in range(4):
                for which in range(3):
                    if sub >= n_sub:
                        break
                    if which == 0:
                        self.ffn(layer * 2 + 0, layer * 6 + 0, layer * 6 + 1, first)
                    elif which == 1:
                        self.mixer(layer, first)
                    else:
                        self.ffn(layer * 2 + 1, layer * 6 + 4, layer * 6 + 5, first)
                    first = False
                    sub += 1
            S.barrier()
        return nc

    def mixer(self, layer, first):
        with ExitStack() as mst:
            self._mst = mst
            self._wo = None
            if layer % 2 == 0:
                self.even_mixer(layer, first)
            else:
                self.odd_mixer(layer, first)
            if self.cfg.get("ev_stage", 9) >= 5 and self.cfg.get("od_stage", 9) >= 3:
                self.out_proj(layer, first)
            self._wo = None

    def proj_slab(self, wslab, b_w, ncol, hnT, b_hnT, pj, b_pj, evac):
        S = self.S
        for half in range(2):
            for bk in range(2):
                c0 = half * 1024 + bk * 512
                for kc in range(16):
                    S.op("pe", lambda e: e.matmul(pj[bk][0:ncol, :], wslab[:, kc, 0:ncol], hnT[:, kc, c0:c0 + 512], start=(kc == 0), stop=(kc == 15)),
                         [b_w, b_hnT], [b_pj[bk]], inc=(kc == 15))
                evac(bk, c0)

    def even_mixer(self, layer, first):
        nc, S = self.nc, self.S
        i = layer // 2
        src, src_bufs = self.stream_src(first)
        w_in_v = self.ev_w_in[i].rearrange("(kc p) c -> p kc c", p=128)
        with ExitStack() as st:
            hnT = self.sb(st, "m_hnT", [128, 16, S_], BF16); b_hnT = Buf()
            with ExitStack() as st2:
                self.pre_norm(st2, src, src_bufs, layer * 6 + 2, list(range(16)), hnT, b_hnT)
                S.barrier()
            cosF = self.sb(st, "m_cos", [128, S_], F32); sinF = self.sb(st, "m_sin", [128, S_], F32); b_cs = Buf()
            S.dma("sp", cosF[:], self.rope_d[0], self.b_rope, b_cs)
            S.dma("sp", sinF[:], self.rope_d[1], self.b_rope, b_cs)
            mk = self.sb(st, "m_mk", [128, 256], BF16); b_mk = Buf()
            S.dma("sp", mk[:], self.mk_ev_d, None, b_mk)
            wsl = [[self.sb(st, "m_w%d_%d" % (j, k), [128, 16, 128], BF16) for k in range(2)] for j in range(3)]
            b_wsl = [[Buf(), Buf()] for j in range(3)]
            qT = self.sb(st, "m_qT", [128, S_], BF16); kT = self.sb(st, "m_kT", [128, S_], BF16); vT = self.sb(st, "m_vT", [128, S_], BF16)
            b_qT, b_kT, b_vT = Buf(), Buf(), Buf()
            vS = self.sb(st, "m_vS", [128, 48, 128], BF16); b_vS = Buf()
            acc = self.sb(st, "m_acc", [128, S_], F32); dacc = self.sb(st, "m_dacc", [128, S_], F32); b_acc, b_dacc = Buf(), Buf()
            qsw = [self.sb(st, "m_qsw%d" % k, [128, 512], F32) for k in range(2)]; b_qsw = [Buf(), Buf()]
            t1 = [self.sb(st, "m_t1%d" % k, [128, 512], F32) for k in range(2)]; b_t1 = [Buf(), Buf()]
            et = [self.sb(st, "m_et%d" % k, [128, 256], F32) for k in range(3)]; b_et = [Buf() for _ in range(3)]
            pt = [self.sb(st, "m_pt%d" % k, [128, 256], BF16) for k in range(3)]; b_pt = [Buf() for _ in range(3)]
            osb = [self.sb(st, "m_osb%d" % k, [128, S_], BF16) for k in range(2)]; b_osb = [Buf(), Buf()]
            pj = [self.pp(st, "m_pj%d" % k, [128, 512], F32) for k in range(2)]; b_pj = [PB(), PB()]
            vtp = self.pp(st, "m_vtp", [128, 8, 128], BF16); b_vtp = PB()
            sc = [self.pp(st, "m_sc%d" % k, [128, 512], F32) for k in range(3)]; b_sc = [PB() for _ in range(3)]
            ud = [self.pp(st, "m_ud%d" % k, [128, 512], F32) for k in range(2)]; b_ud = [PB(), PB()]
            rcnt = [0]

            def rope_evac(dst, b_dst):
                def f(bk, c0):
                    k = rcnt[0] % 2; rcnt[0] += 1
                    S.op("act", lambda e: e.activation(out=qsw[k][0:64, :], in_=pj[bk][64:128, :], func=AF.Copy), [b_pj[bk]], [b_qsw[k], b_pj[bk]])
                    S.op("act", lambda e: e.activation(out=qsw[k][64:128, :], in_=pj[bk][0:64, :], func=AF.Copy), [b_pj[bk]], [b_qsw[k], b_pj[bk]])
                    S.op("dve", lambda e: e.tensor_tensor(out=t1[k][:], in0=pj[bk][:], in1=cosF[:, c0:c0 + 512], op=ALU.mult), [b_pj[bk], b_cs], [b_t1[k], b_pj[bk]])
                    S.op("dve", lambda e: e.tensor_tensor(out=qsw[k][:], in0=qsw[k][:], in1=sinF[:, c0:c0 + 512], op=ALU.mult), [b_qsw[k], b_cs], [b_qsw[k]])
                    S.op("dve", lambda e: e.tensor_tensor(out=dst[:, c0:c0 + 512], in0=t1[k][:], in1=qsw[k][:], op=ALU.add), [b_t1[k], b_qsw[k]], [b_dst])
                return f

            def copy_evac(dst, b_dst):
                def f(bk, c0):
                    S.op("act", lambda e: e.activation(out=dst[:, c0:c0 + 512], in_=pj[bk][:], func=AF.Copy), [b_pj[bk]], [b_dst])
                return f

            blocks = []
            for pi_, d in enumerate((1, 4, 16)):
                nb = 16 // d
                for r in range(d):
                    for n in range(nb):
                        blocks.append((pi_, d, n * 128 * d + r, (n - 1) * 128 * d + r if n > 0 else None))
            blk_index = {(b[1], b[2]): bi for bi, b in enumerate(blocks)}
            sl_of = lambda start, d: slice(start, start + 127 * d + 1, d) if d > 1 else slice(start, start + 128)

            bcnt = 0
            from collections import deque
            epipe = deque(); ELA = 2
            stage = self.cfg.get("ev_stage", 9)
            for h in range(12 if stage >= 3 else (1 if stage >= 1 else 0)):
                k2 = h % 2
                sub_ = self.cfg.get("ev_sub", 9)
                tb_left = list(range(0, 48, 8))

                def emit_tb():
                    if not tb_left:
                        return
                    b0 = tb_left.pop(0)
                    for jj in range(8):
                        _, d, start, _ = blocks[b0 + jj]
                        S.op("pe", lambda e: e.transpose(vtp[:, jj, :], vT[:, sl_of(start, d)], self.ident[:]), [b_vT, self.b_ident], [b_vtp], inc=(jj == 7))
                    S.op("act", lambda e: e.activation(out=vS[:, b0:b0 + 8, :], in_=vtp[:], func=AF.Copy), [b_vtp], [b_vS])

                def with_tb(ev):
                    def f(bk, c0):
                        ev(bk, c0)
                        emit_tb()
                    return f
                NH_ = 12 if stage >= 3 else (1 if stage >= 1 else 0)
                for j, (dst, b_dst, col0) in ((2, (vT, b_vT, 3072 + h * 128)), (0, (qT, b_qT, h * 128)), (1, (kT, b_kT, 1536 + h * 128))):
                    if h == 0:
                        S.dma("pool", wsl[j][k2][:], w_in_v[:, :, col0:col0 + 128], None, b_wsl[j][k2])
                    if h + 1 < NH_:
                        S.dma("pool", wsl[j][1 - k2][:], w_in_v[:, :, col0 + 128:col0 + 256], None, b_wsl[j][1 - k2])
                    self.proj_slab(wsl[j][k2], b_wsl[j][k2], 128, hnT, b_hnT, pj, b_pj, with_tb(rope_evac(dst, b_dst)) if j < 2 else copy_evac(dst, b_dst))
                while tb_left:
                    emit_tb()
                for bi, (pi_, d, start, pstart) in enumerate(blocks if stage >= 2 else []):
                    k = bcnt % 3; bcnt += 1
                    W = 256 if pstart is not None else 128
                    qs = sl_of(start, d)
                    S.op("pe", lambda e: e.matmul(sc[k][:, 0:W], self.ident[:], mk[:, 0:W], start=True, stop=False), [self.b_ident, b_mk], [b_sc[k]], inc=False)
                    S.op("pe", lambda e: e.matmul(sc[k][:, 0:128], kT[:, qs], qT[:, qs], start=False, stop=(pstart is None)), [b_kT, b_qT], [b_sc[k]], inc=(pstart is None))
                    if pstart is not None:
                        S.op("pe", lambda e: e.matmul(sc[k][:, 128:256], kT[:, sl_of(pstart, d)], qT[:, qs], start=False, stop=True), [b_kT, b_qT], [b_sc[k]])
                    S.op("act", lambda e: e.activation(out=pt[k][:, 0:W], in_=sc[k][:, 0:W], func=AF.Exp, scale=SCALE), [b_sc[k]], [b_pt[k]])

                    def stage2(k=k, bi=bi, pi_=pi_, d=d, pstart=pstart, qs=qs):
                        ku = bi % 2
                        S.op("pe", lambda e: e.matmul(ud[ku][:, 0:128], vS[:, bi, :], pt[k][:, 0:128], start=True, stop=(pstart is None)), [b_vS, b_pt[k]], [b_ud[ku]], inc=False)
                        if pstart is not None:
                            pbi = blk_index[(d, pstart)]
                            S.op("pe", lambda e: e.matmul(ud[ku][:, 0:128], vS[:, pbi, :], pt[k][:, 128:256], start=False, stop=True), [b_vS, b_pt[k]], [b_ud[ku]], inc=False)
                        S.op("pe", lambda e: e.matmul(ud[ku][:, 128:256], self.ones[:], pt[k][:, 0:128], start=True, stop=(pstart is None)), [self.b_ones, b_pt[k]], [b_ud[ku]], inc=(pstart is None))
                        if pstart is not None:
                            S.op("pe", lambda e: e.matmul(ud[ku][:, 128:256], self.ones[:], pt[k][:, 128:256], start=False, stop=True), [self.b_ones, b_pt[k]], [b_ud[ku]])
                        if pi_ == 0:
                            S.op("act", lambda e: e.activation(out=acc[:, qs], in_=ud[ku][:, 0:128], func=AF.Copy), [b_ud[ku]], [b_acc])
                            S.op("act", lambda e: e.activation(out=dacc[:, qs], in_=ud[ku][:, 128:256], func=AF.Copy), [b_ud[ku]], [b_dacc])
                        else:
                            S.op("dve", lambda e: e.tensor_tensor(out=acc[:, qs], in0=acc[:, qs], in1=ud[ku][:, 0:128], op=ALU.add), [b_ud[ku], b_acc], [b_acc])
                            S.op("dve", lambda e: e.tensor_tensor(out=dacc[:, qs], in0=dacc[:, qs], in1=ud[ku][:, 128:256], op=ALU.add), [b_ud[ku], b_dacc], [b_dacc])
                    epipe.append(stage2)
                    while len(epipe) > ELA:
                        epipe.popleft()()
                while epipe:
                    epipe.popleft()()
                if sub_ < 4:
                    continue
                S.op("act", lambda e: e.activation(out=dacc[:], in_=dacc[:], func=AF.Ln), [b_dacc], [b_dacc])
                S.op("act", lambda e: e.activation(out=dacc[:], in_=dacc[:], func=AF.Exp, scale=-1.0), [b_dacc], [b_dacc])
                S.op("dve", lambda e: e.tensor_tensor(out=osb[k2][:], in0=acc[:], in1=dacc[:], op=ALU.mult), [b_acc, b_dacc], [b_osb[k2]])
                S.dma("sp", self.mixT_d[h], osb[k2][:], b_osb[k2], self.b_mixT[h])
            uT = self.sb(st, "m_uT", [128, S_], F32); b_uT = Buf()
            pw = self.sb(st, "m_pw", [128, 4, 128], BF16); b_pw = Buf()
            S.dma("pool", pw[:], self.pool_w[i].rearrange("g c d -> c g d"), None, b_pw)
            psc = self.sb(st, "m_psc", [128, 4], F32); b_psc = Buf()
            S.dma("sp", psc[:], self.pool_scale_t[i], None, b_psc)
            invc = self.sb(st, "m_invc", [128, 64], F32); b_invc = Buf()
            S.dma("sp", invc[:], self.invc_d, None, b_invc)
            uT2 = [uT, self.sb(st, "m_uTb", [128, S_], F32)]; b_uT2 = [b_uT, Buf()]
            pgroups = list(enumerate((2, 4, 8, 16))) if stage >= 4 else []

            def pool_dma(g):
                S.dma("pool", wsl[0][g % 2][:], w_in_v[:, :, 4608 + g * 128:4608 + (g + 1) * 128], None, b_wsl[0][g % 2])

            def pool_proj(g):
                self.proj_slab(wsl[0][g % 2], b_wsl[0][g % 2], 128, hnT, b_hnT, pj, b_pj, copy_evac(uT2[g % 2], b_uT2[g % 2]))

            def pool_chain(g, w):
                k2 = g % 2
                uT_, b_uT_ = uT2[g % 2], b_uT2[g % 2]
                cur, b_cur = uT_, b_uT_
                pp2 = [(acc, b_acc), (dacc, b_dacc)]
                step = 1; n = 0
                while step < w:
                    nxt, b_nxt = pp2[n % 2]; n += 1
                    S.op("dve", lambda e: e.tensor_tensor(out=nxt[:, step:], in0=cur[:, step:], in1=cur[:, 0:S_ - step], op=ALU.add), [b_cur], [b_nxt])
                    S.op("act", lambda e: e.activation(out=nxt[:, 0:step], in_=cur[:, 0:step], func=AF.Copy), [b_cur], [b_nxt])
                    cur, b_cur = nxt, b_nxt; step *= 2
                S.op("dve", lambda e: e.scalar_tensor_tensor(out=qT[:, 16:], in0=cur[:, 16:], scalar=1.0 / w, in1=uT_[:, 16:], op0=ALU.mult, op1=ALU.subtract), [b_cur, b_uT_], [b_qT])
                S.op("dve", lambda e: e.tensor_tensor(out=cur[:, 0:16], in0=cur[:, 0:16], in1=invc[:, g * 16:(g + 1) * 16], op=ALU.mult), [b_cur, b_invc], [b_cur])
                S.op("dve", lambda e: e.tensor_tensor(out=qT[:, 0:16], in0=cur[:, 0:16], in1=uT_[:, 0:16], op=ALU.subtract), [b_cur, b_uT_], [b_qT])
                for bk4 in range(4):
                    c0 = bk4 * 512; bk = bk4 % 2
                    S.op("pe", lambda e: e.matmul(pj[bk][:], pw[:, g, :], qT[:, c0:c0 + 512], start=True, stop=True), [b_pw, b_qT], [b_pj[bk]])
                    S.op("dve", lambda e: e.tensor_scalar(osb[k2][:, c0:c0 + 512], pj[bk][:], psc[:, g:g + 1], None, op0=ALU.mult), [b_pj[bk], b_psc], [b_osb[k2]])
                S.dma("sp", self.mixT_d[12 + g], osb[k2][:], b_osb[k2], self.b_mixT[12 + g])

            if pgroups:
                pool_dma(0); pool_dma(1); pool_proj(0)
            for g, w in pgroups:
                if g + 1 < 4:
                    pool_proj(g + 1)
                if g + 2 < 4:
                    pool_dma(g + 2)
                pool_chain(g, w)
            S.barrier()

    def out_proj(self, layer, first):
        nc, S = self.nc, self.S
        i = layer // 2
        wout = (self.ev_w_out if layer % 2 == 0 else self.od_w_out)[i].rearrange("(fc p) d -> p fc d", p=128)
        src, src_bufs = self.stream_src(first)
        with ExitStack() as st:
            mixT = self.sb(st, "o_mixT", [128, 16, S_], BF16); b_mixT = Buf()
            for fc in range(16):
                S.dma("sp", mixT[:, fc, :], self.mixT_d[fc], self.b_mixT[fc], b_mixT)
            if self._wo is not None:
                wo, b_wo = self._wo
            else:
                wo = self.sb(st, "o_wo", [128, 16, D_], BF16); b_wo = Buf()
                for fc4 in range(4):
                    S.dma("pool", wo[:, fc4 * 4:(fc4 + 1) * 4, :], wout[:, fc4 * 4:(fc4 + 1) * 4, :], None, b_wo)
            fsb = [self.sb(st, "o_fsb%d" % k, [128, D_], F32) for k in range(2)]; b_fsb = [Buf(), Buf()]
            hp = [self.sb(st, "o_hp%d" % k, [128, D_], F32) for k in range(2)]; b_hp = [Buf(), Buf()]
            wpost = self.sb(st, "o_wpost", [128, D_], F32); b_wpost = Buf()
            S.dma("sp", wpost[:], self.norm_w[layer * 6 + 3:layer * 6 + 4, :].partition_broadcast(128), None, b_wpost)
            junk = [self.sb(st, "o_junk%d" % k, [128, D_], BF16) for k in range(2)]; b_junk = [Buf(), Buf()]
            ss = [self.sb(st, "o_ss%d" % k, [128, 4], F32) for k in range(2)]; b_ss = [Buf(), Buf()]
            ps = [self.pp(st, "o_ps%d" % k, [128, 512], F32) for k in range(8)]; b_ps = [PB() for _ in range(8)]
            for tt in range(16):
                par = tt % 2
                for db in range(4):
                    bi = par * 4 + db
                    for fc in range(16):
                        S.op("pe", lambda e: e.matmul(ps[bi][:], mixT[:, fc, tt * 128:(tt + 1) * 128], wo[:, fc, db * 512:(db + 1) * 512], start=(fc == 0), stop=(fc == 15)),
                             [b_mixT, b_wo], [b_ps[bi]], inc=(fc == 15))
                    if db % 2 == 0:
                        S.op("act", lambda e: e.activation(out=fsb[par][:, db * 512:(db + 1) * 512], in_=ps[bi][:], func=AF.Copy), [b_ps[bi]], [b_fsb[par]])
                    else:
                        S.op("dve", lambda e: e.tensor_copy(fsb[par][:, db * 512:(db + 1) * 512], ps[bi][:]), [b_ps[bi]], [b_fsb[par]])
                self.post_tile(tt, fsb[par], b_fsb[par], wpost, b_wpost, 1.0, hp[par], b_hp[par], src(tt), src_bufs(tt), junk[par], b_junk[par], ss[par], b_ss[par])
            S.barrier()

    def odd_mixer(self, layer, first):
        nc, S = self.nc, self.S
        i = layer // 2
        src, src_bufs = self.stream_src(first)
        w_in_v = self.od_w_in[i].rearrange("(kc p) c -> p kc c", p=128)
        C_Q, C_KC, C_VC, C_KS, C_VS, C_KW, C_VW, C_GL, C_U, C_CG, C_BG = 0, 1536, 1792, 2048, 2304, 2560, 2816, 3072, 3108, 3620, 4132
        stage = self.cfg.get("od_stage", 9)
        with ExitStack() as st:
            kcmpT = [self.sb(st, "n_kcmpT%d" % g, [128, 128], BF16) for g in range(2)]; b_kcmpT = [Buf(), Buf()]
            vcmp = [self.sb(st, "n_vcmp%d" % g, [128, 128], BF16) for g in range(2)]; b_vcmp = [Buf(), Buf()]
            ksT = [self.sb(st, "n_ksT%d" % g, [128, S_], BF16) for g in range(2)]; b_ksT = [Buf(), Buf()]
            kwT = [self.sb(st, "n_kwT%d" % g, [128, S_], BF16) for g in range(2)]; b_kwT = [Buf(), Buf()]
            vs = [self.sb(st, "n_vs%d" % g, [128, 16, 128], BF16) for g in range(2)]; b_vs = [Buf(), Buf()]
            vw = [self.sb(st, "n_vw%d" % g, [128, 16, 128], BF16) for g in range(2)]; b_vw = [Buf(), Buf()]
            with ExitStack() as s1:
                hnT = self.sb(s1, "n_hnT", [128, 16, S_], BF16); b_hnT = Buf()
                with ExitStack() as st2:
                    self.pre_norm(st2, src, src_bufs, layer * 6 + 2, list(range(16)), hnT, b_hnT)
                    S.barrier()
                cosF = self.sb(s1, "n_cos", [128, S_], F32); sinF = self.sb(s1, "n_sin", [128, S_], F32); b_cs = Buf()
                S.dma("sp", cosF[:], self.rope_d[0], self.b_rope, b_cs)
                S.dma("sp", sinF[:], self.rope_d[1], self.b_rope, b_cs)
                wsl = [self.sb(s1, "n_w%d" % k, [128, 16, 256], BF16) for k in range(2)]; b_wsl = [Buf(), Buf()]
                qsw = [self.sb(s1, "n_qsw%d" % k, [128, 512], F32) for k in range(2)]; b_qsw = [Buf(), Buf()]
                t1 = [self.sb(s1, "n_t1%d" % k, [128, 512], F32) for k in range(2)]; b_t1 = [Buf(), Buf()]
                f32a = self.sb(s1, "n_f32a", [128, S_], F32); b_f32a = Buf()
                f32b = self.sb(s1, "n_f32b", [128, S_], F32); b_f32b = Buf()
                f32c = self.sb(s1, "n_f32c", [128, S_], F32); b_f32c = Buf()
                ob = [self.sb(s1, "n_ob%d" % k, [128, S_], BF16) for k in range(4)]; b_ob = [Buf() for _ in range(4)]
                pj = [self.pp(s1, "n_pj%d" % k, [128, 512], F32) for k in range(2)]; b_pj = [PB(), PB()]
                px = [self.pp(s1, "n_px%d" % k, [128, 512], F32) for k in range(2)]; b_px = [PB(), PB()]
                rcnt = [0]; wcnt = [0]

                def rope_evac(dst, b_dst, udst=None, b_udst=None):
                    def f(bk, c0):
                        k = rcnt[0] % 2; rcnt[0] += 1
                        S.op("act", lambda e: e.activation(out=qsw[k][0:64, :], in_=pj[bk][64:128, :], func=AF.Copy), [b_pj[bk]], [b_qsw[k]])
                        S.op("act", lambda e: e.activation(out=qsw[k][64:128, :], in_=pj[bk][0:64, :], func=AF.Copy), [b_pj[bk]], [b_qsw[k]])
                        if udst is not None:
                            S.op("act", lambda e: e.activation(out=udst[:, c0:c0 + 512], in_=pj[bk][:], func=AF.Copy), [b_pj[bk]], [b_udst])
                        S.op("dve", lambda e: e.tensor_tensor(out=t1[k][:], in0=pj[bk][:], in1=cosF[:, c0:c0 + 512], op=ALU.mult), [b_pj[bk], b_cs], [b_t1[k]])
                        S.op("dve", lambda e: e.tensor_tensor(out=qsw[k][:], in0=qsw[k][:], in1=sinF[:, c0:c0 + 512], op=ALU.mult), [b_qsw[k], b_cs], [b_qsw[k]])
                        S.op("dve", lambda e: e.tensor_tensor(out=dst[:, c0:c0 + 512], in0=t1[k][:], in1=qsw[k][:], op=ALU.add), [b_t1[k], b_qsw[k]], [b_dst])
                    return f

                def copy_evac(dst, b_dst, np_=128, func=AF.Copy):
                    def f(bk, c0):
                        S.op("act", lambda e: e.activation(out=dst[0:np_, c0:c0 + 512], in_=pj[bk][0:np_, :], func=func), [b_pj[bk]], [b_dst])
                    return f

                plan = [(C_GL, 36)]
                for kv_ in range(2):
                    for g_ in range(2):
                        plan.append(((C_KC if kv_ == 0 else C_VC) + g_ * 128, 128))
                for g_ in range(2):
                    plan += [(C_KS + g_ * 128, 128), (C_KW + g_ * 128, 128)]
                plan += [(C_VS, 256), (C_VW, 256)]
                for c_ in range(4):
                    plan += [(C_U + c_ * 128, 128), (C_CG + c_ * 128, 128), (C_BG + c_ * 128, 128)]
                for h_ in range(12):
                    plan.append((C_Q + h_ * 128, 128))
                issued = [0]

                def ensure(i):
                    while issued[0] <= i and issued[0] < len(plan):
                        c0_, n_ = plan[issued[0]]
                        kk = issued[0] % 2
                        S.dma("pool", wsl[kk][:, :, 0:n_], w_in_v[:, :, c0_:c0_ + n_], None, b_wsl[kk])
                        issued[0] += 1

                def take(col0, ncol):
                    i = wcnt[0]; wcnt[0] += 1
                    assert plan[i] == (col0, ncol), (i, plan[i], col0, ncol)
                    ensure(i)
                    return i % 2

                def slab(col0, ncol, evac):
                    k = take(col0, ncol)
                    ensure(wcnt[0])
                    self.proj_slab(wsl[k], b_wsl[k], ncol, hnT, b_hnT, pj, b_pj, evac)

                slab(C_GL, 36, copy_evac(f32a, b_f32a, 36, AF.Sigmoid))
                S.dma("sp", self.gl_d, f32a[0:36, :], b_f32a, self.b_gl)
                w1 = self.sb(s1, "n_w1", [128, 32, 256], BF16); b_w1 = Buf()
                w2 = self.sb(s1, "n_w2", [128, 2, 128], BF16); b_w2 = Buf()
                peT = self.sb(s1, "n_peT", [128, 32], F32); b_peT = Buf()
                X = self.sb(s1, "n_X", [128, 32, 127], BF16); b_X = Buf()
                hidT = self.sb(s1, "n_hidT", [128, 2, 127], BF16); b_hidT = Buf()
                gx = [self.sb(s1, "n_gx%d" % k, [128, 127], F32) for k in range(3)]; b_gx = [Buf() for _ in range(3)]
                for kv in range(2):
                    S.dma("pool", w1[:], self.cmp_w1[kv][i].rearrange("(j p) h -> p j h", p=128), None, b_w1)
                    S.dma("pool", w2[:], self.cmp_w2[kv][i].rearrange("(hc p) d -> p hc d", p=128), None, b_w2)
                    S.dma("sp", peT[:], self.cmp_peT[kv][i], None, b_peT)
                    for g in range(2):
                        slab((C_KC if kv == 0 else C_VC) + g * 128, 128, copy_evac(f32a, b_f32a))
                        for j in range(32):
                            S.op("dve", lambda e: e.tensor_scalar(X[:, j, :], f32a[:, j:j + 16 * 126 + 1:16], peT[:, j:j + 1], None, op0=ALU.add), [b_f32a, b_peT], [b_X])
                        for hc in range(2):
                            for j in range(32):
                                S.op("pe", lambda e: e.matmul(px[hc][:, 0:127], w1[:, j, hc * 128:(hc + 1) * 128], X[:, j, :], start=(j == 0), stop=(j == 31)), [b_w1, b_X], [b_px[hc]], inc=(j == 31))
                            S.op("act", lambda e: e.activation(out=gx[0][:], in_=px[hc][:, 0:127], func=AF.Square), [b_px[hc]], [b_gx[0]])
                            S.op("dve", lambda e: e.tensor_scalar(gx[0][:], gx[0][:], 0.044715, 1.0, op0=ALU.mult, op1=ALU.add), [b_gx[0]], [b_gx[0]])
                            S.op("dve", lambda e: e.tensor_tensor(out=gx[1][:], in0=gx[0][:], in1=px[hc][:, 0:127], op=ALU.mult), [b_gx[0], b_px[hc]], [b_gx[1]])
                            S.op("act", lambda e: e.activation(out=gx[2][:], in_=gx[1][:], func=AF.Tanh, scale=0.7978845608028654), [b_gx[1]], [b_gx[2]])
                            S.op("dve", lambda e: e.scalar_tensor_tensor(out=gx[2][:], in0=gx[2][:], scalar=1.0, in1=px[hc][:, 0:127], op0=ALU.add, op1=ALU.mult), [b_gx[2], b_px[hc]], [b_gx[2]])
                            S.op("dve", lambda e: e.tensor_scalar(hidT[:, hc, :], gx[2][:], 0.5, None, op0=ALU.mult), [b_gx[2]], [b_hidT])
                        if kv == 0:
                            for hc in range(2):
                                S.op("pe", lambda e: e.matmul(px[0][:, 0:127], w2[:, hc, :], hidT[:, hc, :], start=(hc == 0), stop=(hc == 1)), [b_w2, b_hidT], [b_px[0]], inc=(hc == 1))
                            S.op("act", lambda e: e.activation(out=kcmpT[g][:, 0:127], in_=px[0][:, 0:127], func=AF.Copy), [b_px[0]], [b_kcmpT[g]])
                        else:
                            for hc in range(2):
                                S.op("pe", lambda e: e.matmul(px[0][0:127, 0:128], hidT[:, hc, :], w2[:, hc, :], start=(hc == 0), stop=(hc == 1)), [b_w2, b_hidT], [b_px[0]], inc=(hc == 1))
                            S.op("act", lambda e: e.activation(out=vcmp[g][0:127, :], in_=px[0][0:127, 0:128], func=AF.Copy), [b_px[0]], [b_vcmp[g]])
                for g in range(2):
                    slab(C_KS + g * 128, 128, rope_evac(ksT[g], b_ksT[g]))
                    slab(C_KW + g * 128, 128, rope_evac(kwT[g], b_kwT[g]))
                for (c0v, dsts, b_dsts) in ((C_VS, vs, b_vs), (C_VW, vw, b_vw)):
                    k = take(c0v, 256)
                    ensure(wcnt[0])
                    for t2 in range(8):
                        bk = t2 % 2
                        for u in range(2):
                            tt = t2 * 2 + u
                            for kc in range(16):
                                S.op("pe", lambda e: e.matmul(pj[bk][:, u * 256:(u + 1) * 256], hnT[:, kc, tt * 128:(tt + 1) * 128], wsl[k][:, kc, :], start=(kc == 0), stop=(kc == 15)),
                                     [b_hnT, b_wsl[k]], [b_pj[bk]], inc=(kc == 15))
                        for u in range(2):
                            tt = t2 * 2 + u
                            for g in range(2):
                                S.op("act", lambda e: e.activation(out=dsts[g][:, tt, :], in_=pj[bk][:, u * 256 + g * 128:u * 256 + (g + 1) * 128], func=AF.Copy), [b_pj[bk]], [b_dsts[g]])
                cw = self.sb(s1, "n_cw", [128, 12], F32); b_cw = Buf()
                S.dma("sp", cw[:], self.conv_wT[i], None, b_cw)
                for c in range(4):
                    slab(C_U + c * 128, 128, copy_evac(f32a, b_f32a))
                    slab(C_CG + c * 128, 128, copy_evac(f32b, b_f32b))
                    slab(C_BG + c * 128, 128, copy_evac(f32c, b_f32c))
                    S.op("dve", lambda e: e.tensor_tensor(out=f32a[:], in0=f32a[:], in1=f32b[:], op=ALU.mult), [b_f32a, b_f32b], [b_f32a])
                    S.op("dve", lambda e: e.tensor_scalar(f32b[:], f32a[:], cw[:, c * 3 + 2:c * 3 + 3], None, op0=ALU.mult), [b_f32a, b_cw], [b_f32b])
                    S.op("dve", lambda e: e.scalar_tensor_tensor(out=f32b[:, 1:], in0=f32a[:, 0:S_ - 1], scalar=cw[:, c * 3 + 1:c * 3 + 2], in1=f32b[:, 1:], op0=ALU.mult, op1=ALU.add), [b_f32a, b_cw, b_f32b], [b_f32b])
                    S.op("dve", lambda e: e.scalar_tensor_tensor(out=f32b[:, 2:], in0=f32a[:, 0:S_ - 2], scalar=cw[:, c * 3:c * 3 + 1], in1=f32b[:, 2:], op0=ALU.mult, op1=ALU.add), [b_f32a, b_cw, b_f32b], [b_f32b])
                    S.op("dve", lambda e: e.tensor_tensor(out=ob[c % 2][:], in0=f32b[:], in1=f32c[:], op=ALU.mult), [b_f32b, b_f32c], [b_ob[c % 2]])
                    S.dma("sp", self.mixT_d[12 + c], ob[c % 2][:], b_ob[c % 2], self.b_mixT[12 + c])
                for h in range(12):
                    k2 = h % 2
                    slab(C_Q + h * 128, 128, rope_evac(ob[2 + k2], b_ob[2 + k2], ob[k2], b_ob[k2]))
                    S.dma("sp", self.qscr_d[h, 0], ob[k2][:], b_ob[k2], self.b_qscr[h])
                    S.dma("sp", self.qscr_d[h, 1], ob[2 + k2][:], b_ob[2 + k2], self.b_qscr[h])
                S.barrier()
            if stage < 2:
                return
            self._uid += 1
            wo_t = self._mst.enter_context(nc.sbuf_tensor("o_wo_pf_%d" % self._uid, [128, 16, D_], BF16, side="right"))
            b_wo_t = Buf()
            wout_v = self.od_w_out[i].rearrange("(fc p) d -> p fc d", p=128)
            for fc4 in range(4):
                S.dma("pool", wo_t[:, fc4 * 4:(fc4 + 1) * 4, :], wout_v[:, fc4 * 4:(fc4 + 1) * 4, :], None, b_wo_t)
            self._wo = (wo_t, b_wo_t)
            with ExitStack() as s2:
                wbias = self.sb(s2, "a_wbias", [128, 8, 512], BF16); b_wbias = Buf()
                S.dma("sp", wbias[:], self.wbias_d, None, b_wbias)
                cbias = self.sb(s2, "a_cbias", [128, S_], BF16); b_cbias = Buf()
                S.dma("sp", cbias[:], self.cbias_d, None, b_cbias)
                cover = self.sb(s2, "a_cover", [128, 32], F32); b_cover = Buf()
                S.dma("sp", cover[:], self.cover_d, None, b_cover)
                nf01 = self.sb(s2, "a_nf01", [128, 512], F32); addc = self.sb(s2, "a_addc", [128, 512], F32); b_nfa = Buf()
                S.dma("sp", nf01[:], self.nf01_d, None, b_nfa)
                S.dma("sp", addc[:], self.addc_d, None, b_nfa)
                e_all = self.sb(s2, "a_eall", [32, 16, 128], BF16); b_eall = Buf()
                S.dma("sp", e_all[:], self.e_all_d, None, b_eall)
                Pacc = self.sb(s2, "a_Pacc", [128, S_], F32); b_Pacc = Buf()
                qt = [self.sb(s2, "a_q%d" % k, [128, S_], BF16) for k in range(2)]; b_qt = [Buf(), Buf()]
                gate = [self.sb(s2, "a_gate%d" % k, [128, S_], F32) for k in range(4)]; b_gate = [Buf() for _ in range(4)]
                oacc = self.sb(s2, "a_oacc", [128, S_], F32); b_oacc = Buf()
                oacc_b = self.sb(s2, "a_oaccb", [128, S_], F32); b_oacc_b = Buf()
                obf = [self.sb(s2, "a_obf%d" % k, [128, S_], BF16) for k in range(2)]; b_obf = [Buf(), Buf()]
                pt = [self.sb(s2, "a_pt%d" % k, [128, 512], BF16) for k in range(3)]; b_pt = [Buf() for _ in range(3)]
                rz = self.sb(s2, "a_rz", [128, 512], F32); b_rz = Buf()
                tq = self.sb(s2, "a_tq", [128, 512], F32); b_tq = Buf()
                scs = self.sb(s2, "a_scs", [128, 512], F32); b_scs = Buf()
                m8 = self.sb(s2, "a_m8", [128, 128], F32); b_m8 = Buf()
                selb = self.sb(s2, "a_selb", [128, 512], BF16); b_selb = Buf()
                selbT = self.sb(s2, "a_selbT", [32, S_], BF16); b_selbT = Buf()
                sp_ = [self.pp(s2, "a_s%d" % k, [128, 512], F32) for k in range(3)]; b_sp = [PB() for _ in range(3)]
                up = [self.pp(s2, "a_u%d" % k, [128, 512], F32) for k in range(2)]; b_up = [PB(), PB()]
                dp = [self.pp(s2, "a_d%d" % k, [128, 512], F32) for k in range(2)]; b_dp = [PB(), PB()]
                tp = self.pp(s2, "a_tp", [32, 8, 128], BF16); b_tp = PB()
                pcnt = [0]; ucnt = [0]

                from collections import deque
                pipe = deque(); LA = 2

                def push(fn):
                    pipe.append(fn)
                    while len(pipe) > LA:
                        pipe.popleft()()

                def flush():
                    while pipe:
                        pipe.popleft()()

                def attend(qsrc, b_q, Q, pairs, np_, after):
                    ub = ucnt[0] % 2; ucnt[0] += 1
                    n = len(pairs)
                    for pi_, pr in enumerate(pairs):
                        (kap, b_k, biases, vap, b_v) = pr[:5]
                        lo, hi = pr[5] if len(pr) > 5 else (0, 512)
                        k = pcnt[0] % 3; pcnt[0] += 1
                        nb_ = len(biases)
                        S.op("pe", lambda e: e.matmul(sp_[k][0:np_, lo:hi], kap, qsrc[:, Q * 512 + lo:Q * 512 + hi], start=True, stop=(nb_ == 0)), [b_k, b_q], [b_sp[k]], inc=(nb_ == 0))
                        for bi_, (bl, br, bb) in enumerate(biases):
                            S.op("pe", lambda e: e.matmul(sp_[k][0:np_, lo:hi], bl, br[:, lo:hi], start=False, stop=(bi_ == nb_ - 1)), bb, [b_sp[k]], inc=(bi_ == nb_ - 1))
                        S.op("act", lambda e: e.activation(out=pt[k][0:np_, lo:hi], in_=sp_[k][0:np_, lo:hi], func=AF.Exp, scale=SCALE), [b_sp[k]], [b_pt[k]])

                        def pv(k=k, pi_=pi_, vap=vap, b_v=b_v, lo=lo, hi=hi):
                            S.op("pe", lambda e: e.matmul(up[ub][:, lo:hi], vap, pt[k][0:np_, lo:hi], start=(pi_ == 0), stop=(pi_ == n - 1), skip_group_check=True), [b_v, b_pt[k]], [b_up[ub]], inc=False)
                            S.op("pe", lambda e: e.matmul(dp[ub][:, lo:hi], self.ones[0:np_, :], pt[k][0:np_, lo:hi], start=(pi_ == 0), stop=(pi_ == n - 1), skip_group_check=True), [self.b_ones, b_pt[k]], [b_dp[ub]], inc=True)
                            if pi_ == n - 1:
                                after(ub, k)
                        push(pv)

                def finish(ub, gidx, first_branch, Q, oa=None, b_oa=None):
                    cs = slice(Q * 512, (Q + 1) * 512)
                    if oa is None:
                        oa, b_oa = oacc, b_oacc
                    S.op("dve", lambda e: e.tensor_scalar(rz[:], dp[ub][:], 1e-30, None, op0=ALU.max), [b_dp[ub]], [b_rz])
                    S.op("act", lambda e: e.activation(out=rz[:], in_=rz[:], func=AF.Ln), [b_rz], [b_rz])
                    S.op("act", lambda e: e.activation(out=rz[:], in_=rz[:], func=AF.Exp, scale=-1.0), [b_rz], [b_rz])
                    S.op("dve", lambda e: e.tensor_tensor(out=tq[:], in0=up[ub][:], in1=rz[:], op=ALU.mult), [b_up[ub], b_rz], [b_tq])
                    if first_branch:
                        S.op("dve", lambda e: e.tensor_tensor(out=oa[:, cs], in0=tq[:], in1=gate[gidx][:, cs], op=ALU.mult), [b_tq, b_gate[gidx]], [b_oa])
                    else:
                        S.op("dve", lambda e: e.tensor_tensor(out=tq[:], in0=tq[:], in1=gate[gidx][:, cs], op=ALU.mult), [b_tq, b_gate[gidx]], [b_tq])
                        S.op("dve", lambda e: e.tensor_tensor(out=oa[:, cs], in0=oa[:, cs], in1=tq[:], op=ALU.add), [b_tq, b_oa], [b_oa])

                for g in range(2):
                    for hh in range(6):
                        h = g * 6 + hh
                        k2 = h % 2
                        gi = 0 if hh % 2 == 0 else 3
                        oa, b_oa = (oacc, b_oacc) if hh % 2 == 0 else (oacc_b, b_oacc_b)
                        S.dma("sp", qt[k2][:], self.qscr_d[h, 0], self.b_qscr[h], b_qt[k2])
                        S.dma("sp", gate[gi][:], self.gl_d[h * 3:h * 3 + 1, :].partition_broadcast(128), self.b_gl, b_gate[gi])
                        for Q in range(4):
                            cs = slice(Q * 512, (Q + 1) * 512)

                            def after1(ub, pk, Q=Q, cs=cs, hh=hh, gi=gi, oa=oa, b_oa=b_oa, h=h):
                                finish(ub, gi, True, Q, oa, b_oa)
                                if hh == 0:
                                    S.op("dve", lambda e: e.tensor_tensor(out=Pacc[0:127, cs], in0=pt[pk][0:127, :], in1=rz[0:127, :], op=ALU.mult), [b_pt[pk], b_rz], [b_Pacc])
                                else:
                                    S.op("dve", lambda e: e.tensor_tensor(out=tq[0:127, :], in0=pt[pk][0:127, :], in1=rz[0:127, :], op=ALU.mult), [b_pt[pk], b_rz], [b_tq])
                                    S.op("dve", lambda e: e.tensor_tensor(out=Pacc[0:127, cs], in0=Pacc[0:127, cs], in1=tq[0:127, :], op=ALU.add), [b_tq, b_Pacc], [b_Pacc])
                                if Q == 3:
                                    S.dma("sp", self.ocmp_d[h], oa[:], b_oa, self.b_ocmp[h])
                            attend(qt[k2], b_qt[k2], Q,
                                   [(kcmpT[g][:, 0:127], b_kcmpT[g], [(self.ident[0:127, 0:127], cbias[0:127, cs], [self.b_ident, b_cbias])], vcmp[g][0:127, :], b_vcmp[g])], 127, after1)
                    flush()
                    for tt in range(16):
                        S.op("pe", lambda e: e.matmul(sp_[0][:, tt * 32:(tt + 1) * 32], Pacc[0:127, tt * 128:(tt + 1) * 128], cover[0:127, :], start=True, stop=True), [b_Pacc, b_cover], [b_sp[0]], inc=(tt == 15))
                    S.op("dve", lambda e: e.tensor_tensor(out=scs[:], in0=sp_[0][:], in1=nf01[:], op=ALU.mult), [b_sp[0], b_nfa], [b_scs])
                    S.op("dve", lambda e: e.tensor_tensor(out=scs[:], in0=scs[:], in1=addc[:], op=ALU.add), [b_scs, b_nfa], [b_scs])
                    for tt in range(16):
                        S.op("dve", lambda e: e.max(out=m8[:, tt * 8:(tt + 1) * 8], in_=scs[:, tt * 32:(tt + 1) * 32]), [b_scs], [b_m8])
                    for tt in range(16):
                        S.op("dve", lambda e: e.tensor_scalar(selb[:, tt * 32:(tt + 1) * 32], scs[:, tt * 32:(tt + 1) * 32], m8[:, tt * 8 + 7:tt * 8 + 8], -30000.0, op0=ALU.is_lt, op1=ALU.mult), [b_scs, b_m8], [b_selb])
                    for half in range(2):
                        for j in range(8):
                            tt = half * 8 + j
                            S.op("pe", lambda e: e.transpose(tp[:, j, :], selb[:, tt * 32:(tt + 1) * 32], self.ident[:]), [b_selb, self.b_ident], [b_tp], inc=(j == 7))
                        S.op("act", lambda e: e.activation(out=selbT[:, half * 1024:(half + 1) * 1024], in_=tp[:], func=AF.Copy), [b_tp], [b_selbT])
                    for hh in range(6):
                        h = g * 6 + hh
                        k2 = h % 2
                        g1, g2 = (1, 2) if hh % 2 == 0 else (0, 3)
                        oa, b_oa = (oacc, b_oacc) if hh % 2 == 0 else (oacc_b, b_oacc_b)
                        S.dma("sp", qt[k2][:], self.qscr_d[h, 1], self.b_qscr[h], b_qt[k2])
                        S.dma("sp", gate[g1][:], self.gl_d[h * 3 + 1:h * 3 + 2, :].partition_broadcast(128), self.b_gl, b_gate[g1])
                        S.dma("sp", gate[g2][:], self.gl_d[h * 3 + 2:h * 3 + 3, :].partition_broadcast(128), self.b_gl, b_gate[g2])
                        S.dma("sp", oa[:], self.ocmp_d[h], self.b_ocmp[h], b_oa)
                        for Q in range(4):
                            cs = slice(Q * 512, (Q + 1) * 512)
                            pairs = []
                            for kb in range(0, 4 * Q + 4):
                                biases = [(e_all[:, kb, :], selbT[:, cs], [b_eall, b_selbT])]
                                a = kb - 4 * Q
                                if a >= 0:
                                    biases.append((self.ident[:], wbias[:, 4 + a, :], [self.b_ident, b_wbias]))
                                pairs.append((ksT[g][:, kb * 128:(kb + 1) * 128], b_ksT[g], biases, vs[g][:, kb, :], b_vs[g], (128 * max(0, a), 512)))
                            attend(qt[k2], b_qt[k2], Q, pairs, 128, lambda ub, pk, Q=Q, g1=g1, oa=oa, b_oa=b_oa: finish(ub, g1, False, Q, oa, b_oa))
                            pairs = []
                            for kb in range(max(0, 4 * Q - 4), 4 * Q + 4):
                                a = kb - 4 * Q
                                pairs.append((kwT[g][:, kb * 128:(kb + 1) * 128], b_kwT[g], [(self.ident[:], wbias[:, 4 + a, :], [self.b_ident, b_wbias])], vw[g][:, kb, :], b_vw[g],
                                              (128 * max(0, a), 128 * (min(3, a + 4) + 1))))

                            def after2(ub, pk, Q=Q, g2=g2, oa=oa, b_oa=b_oa, h=h, k2=k2):
                                finish(ub, g2, False, Q, oa, b_oa)
                                if Q == 3:
                                    S.op("act", lambda e: e.activation(out=obf[k2][:], in_=oa[:], func=AF.Copy), [b_oa], [b_obf[k2]])
                                    S.dma("sp", self.mixT_d[h], obf[k2][:], b_obf[k2], self.b_mixT[h])
                            attend(qt[k2], b_qt[k2], Q, pairs, 128, after2)
                    flush()
                S.barrier()


def make_in_map(inputs, b, consts):
    m = {
        "x": np.ascontiguousarray(inputs["x"][b]),
        "pos": np.ascontiguousarray(inputs["positions"][b:b + 1]).astype(np.int32),
        "norm_w": np.ascontiguousarray(inputs["norm_w"].reshape(24, D_)),
        "ffn_w_gate": inputs["ffn_w_gate"].reshape(8, D_, FF),
        "ffn_w_up": inputs["ffn_w_up"].reshape(8, D_, FF),
        "ffn_w_down": inputs["ffn_w_down"].reshape(8, FF, D_),
        "ev_w_in": inputs["ev_w_in"], "ev_w_out": inputs["ev_w_out"], "pool_w": inputs["pool_w"],
        "pool_scale_t": np.ascontiguousarray(inputs["pool_scale"].reshape(2, 4, 128).transpose(0, 2, 1)),
        "od_w_in": inputs["od_w_in"], "od_w_out": inputs["od_w_out"],
        "cmp_w1_k": inputs["cmp_w1_k"], "cmp_w1_v": inputs["cmp_w1_v"],
        "cmp_w2_k": inputs["cmp_w2_k"], "cmp_w2_v": inputs["cmp_w2_v"],
        "cmp_peT_k": np.ascontiguousarray(inputs["cmp_pe_k"].transpose(0, 2, 1)),
        "cmp_peT_v": np.ascontiguousarray(inputs["cmp_pe_v"].transpose(0, 2, 1)),
        "conv_wT": np.ascontiguousarray(inputs["conv_w"].reshape(2, 3, 4, 128).transpose(0, 3, 2, 1).reshape(2, 128, 12)),
    }
    m.update(consts)
    return m


def run(inputs, cfg, cores):
    k = K(cfg)
    nc = k.build()
    consts = host_consts()
    inputs = {kk: np.asarray(v) for kk, v in inputs.items()}
    in_maps = [make_in_map(inputs, b, consts) for b in cores]
    import time as _t; _t0 = _t.time()
    try:
        res = run_bass_kernel_spmd(nc, in_maps, core_ids=list(range(len(cores))))
    finally:
        print("device call seconds", _t.time() - _t0)
    return res, k


def kernel(**inputs):
    res, _ = run(inputs, {}, list(range(8)))
    return np.stack([r["y"] for r in res.results], axis=0).astype(np.float32)
```

```python
import numpy as np
import ml_dtypes
from contextlib import ExitStack
import concourse.bass as bass
import concourse.mybir as mybir
from concourse.bass_utils import run_bass_kernel_spmd

F32 = mybir.dt.float32
BF16 = mybir.dt.bfloat16
I32 = mybir.dt.int32
AF = mybir.ActivationFunctionType
ALU = mybir.AluOpType

S_ = 2048
D_ = 2048
FF = 5632
NFC = FF // 128
EPS = 1e-6
SCALE = 128 ** -0.5
TWO_PI = 6.283185307179586
PI = 3.141592653589793


class Sem:
    __slots__ = ("h", "issued", "dma")

    def __init__(self, h, dma):
        self.h = h
        self.issued = 0
        self.dma = dma


class Buf:
    __slots__ = ("name", "w", "r", "dsem", "excl")

    def __init__(self, name="", excl=False):
        self.name = name
        self.w = None
        self.r = {}
        self.dsem = None
        self.excl = excl


def PB():
    return Buf("psum", True)


class Eng:
    def __init__(self, name, h, sem):
        self.name = name
        self.h = h
        self.sem = sem
        self.known = {}
        self.pending = False


class Sched:
    def __init__(self, nc, es, n_dsem=90):
        self.nc = nc
        mk = lambda nm, dma: Sem(es.enter_context(nc.semaphore(nm)), dma)
        self.engs = {
            "pe": Eng("pe", nc.tensor, mk("s_pe", False)),
            "act": Eng("act", nc.scalar, mk("s_act", False)),
            "dve": Eng("dve", nc.vector, mk("s_dve", False)),
            "pool": Eng("pool", nc.gpsimd, mk("s_pool", False)),
            "sp": Eng("sp", nc.sync, mk("s_sp", False)),
        }
        self.dsems = [mk("s_d%d" % i, True) for i in range(n_dsem)]
        self.dnext = 0
        self.n_ins = 0
        self.n_wait = 0

    def _need(self, need, tok):
        if tok is None:
            return
        sem, val = tok
        if sem.dma:
            val = sem.issued
        if need.get(sem, 0) < val:
            need[sem] = val

    def _sync(self, E, reads, writes, skip_w_sem=None):
        need = {}
        for b in reads:
            self._need(need, b.w)
            if b.excl:
                for sem, val in b.r.items():
                    if sem is not E.sem:
                        self._need(need, (sem, val))
        for b in writes:
            if not (skip_w_sem is not None and b.w is not None and b.w[0] is skip_w_sem):
                self._need(need, b.w)
            for sem, val in b.r.items():
                self._need(need, (sem, val))
        for sem, val in need.items():
            if sem is E.sem and E.name == "pe":
                continue
            if E.known.get(sem, 0) < val:
                E.h.wait_ge(sem.h, val)
                E.known[sem] = val
                self.n_wait += 1

    def op(self, eng, fn, reads=(), writes=(), inc=True):
        E = self.engs[eng]
        self._sync(E, reads, writes)
        ins = fn(E.h)
        self.n_ins += 1
        val = E.sem.issued + 1
        if inc:
            ins.then_inc(E.sem.h, 1)
            E.sem.issued = val
            E.pending = False
        else:
            E.pending = True
        for b in reads:
            if b.r.get(E.sem, 0) < val:
                b.r[E.sem] = val
        for b in writes:
            b.w = (E.sem, val)
            b.r = {}
        return ins

    def dma(self, q, out, in_, src, dst, **kw):
        E = self.engs[q]
        if dst.dsem is None:
            dst.dsem = self.dsems[self.dnext % len(self.dsems)]
            self.dnext += 1
        ds = dst.dsem
        self._sync(E, [src] if src is not None else [], [dst], skip_w_sem=ds)
        ins = E.h.dma_start(out=out, in_=in_, **kw)
        ins.then_inc(ds.h, 16)
        ds.issued += 16
        self.n_ins += 1
        dst.w = (ds, ds.issued)
        dst.r = {}
        if src is not None:
            src.r[ds] = ds.issued
        return ins

    def barrier(self):
        sp = self.engs["sp"]
        for E in self.engs.values():
            assert not E.pending, E.name
        allsems = [E.sem for E in self.engs.values() if E is not sp] + self.dsems
        for sem in allsems:
            if sp.known.get(sem, 0) < sem.issued:
                sp.h.wait_ge(sem.h, sem.issued)
                sp.known[sem] = sem.issued
        sp.h.nop().then_inc(sp.sem.h, 1)
        sp.sem.issued += 1
        for E in self.engs.values():
            if E is sp:
                continue
            E.h.wait_ge(sp.sem.h, sp.sem.issued)
            for sem in allsems + [sp.sem]:
                E.known[sem] = sem.issued
        sp.known[sp.sem] = sp.sem.issued


def host_consts():
    c = {}
    c["ident"] = np.eye(128, dtype=np.float32).astype(ml_dtypes.bfloat16)
    inv_freq = (1.0 / (10000.0 ** (np.arange(0, 128, 2, dtype=np.float32) / np.float32(128)))).astype(np.float32)
    c["invfreq"] = np.concatenate([inv_freq, inv_freq])[:, None].astype(np.float32)
    c["sinsign"] = np.concatenate([-np.ones(64), np.ones(64)])[:, None].astype(np.float32)
    kk = np.arange(128)[:, None]; qq = np.arange(128)[None, :]
    m_diag = (kk <= qq); m_prev = (kk >= qq); m_far = (kk > qq)
    bf = ml_dtypes.bfloat16
    c["mk_ev"] = np.concatenate([np.where(m_diag, 0.0, -30000.0), np.where(m_prev, 0.0, -30000.0)], axis=1).astype(np.float32).astype(bf)
    c["ones_bf"] = np.ones((128, 128), np.float32).astype(bf)
    invc = np.zeros((4, 16), np.float32)
    for g, w in enumerate((2, 4, 8, 16)):
        invc[g] = 1.0 / np.minimum(np.arange(1, 17), w)
    c["invc"] = np.broadcast_to(invc.reshape(1, 64), (128, 64)).copy()
    NEG = -30000.0
    b_diag = np.where(m_diag, 0.0, NEG); b_far = np.where(m_far, 0.0, NEG)
    full = np.zeros((128, 128)); none = np.full((128, 128), NEG)
    wb = np.zeros((128, 8, 4, 128), np.float32)
    for ai, a in enumerate(range(-4, 4)):
        for b in range(4):
            rel = b - a
            wb[:, ai, b, :] = none if (rel < 0 or rel > 4) else (b_diag if rel == 0 else (b_far if rel == 4 else full))
    c["wbias"] = wb.reshape(128, 8, 512).astype(bf)
    n_ = np.arange(128)[:, None]; t_ = np.arange(2048)[None, :]
    c["cbias"] = np.where(t_ >= 16 * n_ + 31, 0.0, NEG).astype(np.float32).astype(bf)
    j_ = np.arange(32)[None, :]
    c["cover"] = ((16 * n_ < 64 * j_ + 64) & (16 * n_ + 32 > 64 * j_)).astype(np.float32)
    t3 = (np.arange(16)[None, :, None] * 128 + np.arange(128)[:, None, None])
    cur = t3 // 64; j3 = np.arange(32)[None, None, :]
    vis = j3 <= cur; forced = vis & ((j3 == 0) | (j3 >= cur - 1))
    c["nf01"] = (vis & ~forced).astype(np.float32).reshape(128, 512)
    c["addc"] = np.where(forced, 1e9, np.where(vis, 0.0, -1e9)).astype(np.float32).reshape(128, 512)
    e_all = np.zeros((32, 16, 128), np.float32)
    for kb in range(16):
        e_all[2 * kb, kb, 0:64] = 1.0; e_all[2 * kb + 1, kb, 64:128] = 1.0
    c["e_all"] = e_all.astype(bf)
    return c


class K:
    def __init__(self, cfg):
        self.cfg = cfg
        nc = self.nc = bass.Bass("TRN2", target_bir_lowering=False)
        self.es = ExitStack()
        self.S = Sched(nc, self.es)
        dt = lambda name, shape, dtype=F32, kind="ExternalInput": nc.dram_tensor(name, shape, dtype, kind=kind).ap()
        self.x = dt("x", [S_, D_])
        self.pos = dt("pos", [1, S_], I32)
        self.norm_w = dt("norm_w", [24, D_])
        self.wg = dt("ffn_w_gate", [8, D_, FF])
        self.wu = dt("ffn_w_up", [8, D_, FF])
        self.wd = dt("ffn_w_down", [8, FF, D_])
        self.ident_d = dt("ident", [128, 128], BF16)
        self.invfreq_d = dt("invfreq", [128, 1])
        self.sinsign_d = dt("sinsign", [128, 1])
        self.ev_w_in = dt("ev_w_in", [2, D_, 5120])
        self.ev_w_out = dt("ev_w_out", [2, D_, D_])
        self.pool_w = dt("pool_w", [2, 4, 128, 128])
        self.pool_scale_t = dt("pool_scale_t", [2, 128, 4])
        self.mk_ev_d = dt("mk_ev", [128, 256], BF16)
        self.ones_d = dt("ones_bf", [128, 128], BF16)
        self.invc_d = dt("invc", [128, 64])
        self.rope_d = dt("rope_scr", [2, 128, S_], F32, kind="ExternalOutput")
        self.mixT_d = dt("mixT_scr", [16, 128, S_], BF16, kind="ExternalOutput")
        self.od_w_in = dt("od_w_in", [2, D_, 4644])
        self.od_w_out = dt("od_w_out", [2, D_, D_])
        self.cmp_w1 = [dt("cmp_w1_k", [2, 4096, 256]), dt("cmp_w1_v", [2, 4096, 256])]
        self.cmp_w2 = [dt("cmp_w2_k", [2, 256, 128]), dt("cmp_w2_v", [2, 256, 128])]
        self.cmp_peT = [dt("cmp_peT_k", [2, 128, 32]), dt("cmp_peT_v", [2, 128, 32])]
        self.conv_wT = dt("conv_wT", [2, 128, 12])
        self.wbias_d = dt("wbias", [128, 8, 512], BF16)
        self.cbias_d = dt("cbias", [128, S_], BF16)
        self.cover_d = dt("cover", [128, 32])
        self.nf01_d = dt("nf01", [128, 512])
        self.addc_d = dt("addc", [128, 512])
        self.e_all_d = dt("e_all", [32, 16, 128], BF16)
        self.gl_d = dt("gl_scr", [36, S_], F32, kind="ExternalOutput")
        self.qscr_d = dt("q_scr", [12, 2, 128, S_], BF16, kind="ExternalOutput")
        self.ocmp_d = dt("ocmp_scr", [12, 128, S_], F32, kind="ExternalOutput")
        self.b_gl = Buf("gl"); self.b_qscr = [Buf("qscr%d" % i) for i in range(12)]; self.b_ocmp = [Buf("ocmp%d" % i) for i in range(12)]
        self.b_rope = Buf("rope")
        self.b_mixT = [Buf("mixT%d" % i) for i in range(16)]
        self.y = dt("y", [S_, D_], F32, kind="ExternalOutput")
        self.ybuf = [Buf("y%d" % i) for i in range(16)]
        self.xbuf = Buf("x")

    def sb(self, st, name, shape, dtype):
        self._uid = getattr(self, "_uid", 0) + 1
        return st.enter_context(self.nc.sbuf_tensor("%s_%d" % (name, self._uid), shape, dtype))

    def pp(self, st, name, shape, dtype):
        self._uid = getattr(self, "_uid", 0) + 1
        return st.enter_context(self.nc.psum_tensor("%s_%d" % (name, self._uid), shape, dtype))

    def setup(self):
        nc, S, es = self.nc, self.S, self.es
        self.ident = self.sb(es, "ident_s", [128, 128], BF16)
        self.b_ident = Buf("ident")
        S.dma("sp", self.ident[:], self.ident_d, None, self.b_ident)
        self.ones = self.sb(es, "ones_s", [128, 128], BF16); self.b_ones = Buf("ones")
        S.dma("sp", self.ones[:], self.ones_d, None, self.b_ones)
        with ExitStack() as st:
            pi = self.sb(st, "rp_pi", [128, S_], I32); b_pi = Buf()
            pf = self.sb(st, "rp_pf", [128, S_], F32); b_pf = Buf()
            ang = self.sb(st, "rp_ang", [128, S_], F32); b_ang = Buf()
            kf = self.sb(st, "rp_kf", [128, S_], F32); b_kf = Buf()
            res = self.sb(st, "rp_res", [128, S_], F32); b_res = Buf()
            ivf = self.sb(st, "rp_ivf", [128, 2], F32); b_ivf = Buf()
            S.dma("sp", ivf[:, 0:1], self.invfreq_d, None, b_ivf)
            S.dma("sp", ivf[:, 1:2], self.sinsign_d, None, b_ivf)
            S.dma("sp", pi[:], self.pos.partition_broadcast(128), None, b_pi)
            S.op("dve", lambda e: e.tensor_copy(pf[:], pi[:]), [b_pi], [b_pf])
            for which in range(2):
                if which == 0:
                    S.op("dve", lambda e: e.tensor_scalar(ang[:], pf[:], ivf[:, 0:1], PI / 2, op0=ALU.mult, op1=ALU.add), [b_pf, b_ivf], [b_ang])
                else:
                    S.op("dve", lambda e: e.tensor_scalar(ang[:], pf[:], ivf[:, 0:1], None, op0=ALU.mult), [b_pf, b_ivf], [b_ang])
                S.op("dve", lambda e: e.tensor_scalar(pi[:], ang[:], 1.0 / TWO_PI, None, op0=ALU.mult), [b_ang], [b_pi])
                S.op("dve", lambda e: e.tensor_copy(kf[:], pi[:]), [b_pi], [b_kf])
                S.op("dve", lambda e: e.scalar_tensor_tensor(out=ang[:], in0=kf[:], scalar=-TWO_PI, in1=ang[:], op0=ALU.mult, op1=ALU.add), [b_kf, b_ang], [b_ang])
                S.op("dve", lambda e: e.tensor_scalar(kf[:], ang[:], PI, TWO_PI, op0=ALU.is_gt, op1=ALU.mult), [b_ang], [b_kf])
                S.op("dve", lambda e: e.tensor_sub(ang[:], ang[:], kf[:]), [b_ang, b_kf], [b_ang])
                S.op("dve", lambda e: e.tensor_scalar(kf[:], ang[:], -PI, -TWO_PI, op0=ALU.is_lt, op1=ALU.mult), [b_ang], [b_kf])
                S.op("dve", lambda e: e.tensor_add(ang[:], ang[:], kf[:]), [b_ang, b_kf], [b_ang])
                S.op("act", lambda e: e.activation(out=res[:], in_=ang[:], func=AF.Sin), [b_ang], [b_res])
                if which == 1:
                    S.op("dve", lambda e: e.tensor_scalar(res[:], res[:], ivf[:, 1:2], None, op0=ALU.mult), [b_res, b_ivf], [b_res])
                S.dma("sp", self.rope_d[which], res[:], b_res, self.b_rope)
            S.barrier()

    def pre_norm(self, st, src, src_bufs, nw_idx, tts, hnT, b_hnT):
        nc, S = self.nc, self.S
        wb = self.sb(st, "pn_wb", [128, D_], F32); b_wb = Buf()
        S.dma("pool", wb[:], self.norm_w[nw_idx:nw_idx + 1, :].partition_broadcast(128), None, b_wb)
        NB_ = 3
        hb = [self.sb(st, "pn_h%d" % i, [128, D_], F32) for i in range(NB_)]
        b_hb = [Buf() for _ in range(NB_)]
        hn = [self.sb(st, "pn_hn%d" % i, [128, D_], BF16) for i in range(NB_)]
        b_hn = [Buf() for _ in range(NB_)]
        ssl = [self.sb(st, "pn_ss%d" % i, [128, 4], F32) for i in range(NB_)]; b_ssl = [Buf() for _ in range(NB_)]
        pst = [self.pp(st, "pn_ps%d" % i, [128, 8, 128], BF16) for i in range(2)]
        b_pst = [PB(), PB()]
        def stage_a(i, tt):
            h_t, bh = hb[i % NB_], b_hb[i % NB_]
            hn_t, bn = hn[i % NB_], b_hn[i % NB_]
            ss, b_ss = ssl[i % NB_], b_ssl[i % NB_]
            S.dma("sp" if i % 2 == 0 else "pool", h_t[:], src(tt), src_bufs(tt), bh)
            S.op("dve", lambda e: e.memset(ss[:, 0:1], 0.0), [], [b_ss])
            S.op("act", lambda e: e.activation(out=hn_t[:], in_=h_t[:], func=AF.Square, accum_out=ss[:, 0:1]), [bh], [bn, b_ss])
            S.op("act", lambda e: e.activation(out=ss[:, 1:2], in_=ss[:, 0:1], func=AF.Sqrt, scale=1.0 / D_, bias=self.eps_t[:, 0:1]), [b_ss], [b_ss])
            S.op("dve", lambda e: e.reciprocal(ss[:, 2:3], ss[:, 1:2]), [b_ss], [b_ss])
            S.op("dve", lambda e: e.scalar_tensor_tensor(out=hn_t[:], in0=h_t[:], scalar=ss[:, 2:3], in1=wb[:], op0=ALU.mult, op1=ALU.mult), [bh, b_ss, b_wb], [bn])

        def stage_b(i, tt):
            hn_t, bn = hn[i % NB_], b_hn[i % NB_]
            for half in range(2):
                for j in range(8):
                    c0 = (half * 8 + j) * 128
                    S.op("pe", lambda e: e.transpose(pst[half][:, j, :], hn_t[:, c0:c0 + 128], self.ident[:]), [bn, self.b_ident], [b_pst[half]], inc=(j == 7))
                if half == 0:
                    S.op("act", lambda e: e.activation(out=hnT[:, half * 8:(half + 1) * 8, i * 128:(i + 1) * 128], in_=pst[half][:], func=AF.Copy), [b_pst[half]], [b_hnT])
                else:
                    S.op("dve", lambda e: e.tensor_copy(hnT[:, half * 8:(half + 1) * 8, i * 128:(i + 1) * 128], pst[half][:]), [b_pst[half]], [b_hnT])

        n = len(tts)
        for i in range(n + 1):
            if i < n:
                stage_a(i, tts[i])
            if i >= 1:
                stage_b(i - 1, tts[i - 1])

    def post_tile(self, tt, f_t, b_f, wb, b_wb, coef, h_t, b_h, h_src, h_src_buf, junk, b_junk, ss, b_ss, ldq="sp"):
        S = self.S
        S.dma(ldq, h_t[:], h_src, h_src_buf, b_h)
        S.op("dve", lambda e: e.memset(ss[:, 0:1], 0.0), [], [b_ss])
        S.op("act", lambda e: e.activation(out=junk[:], in_=f_t[:], func=AF.Square, accum_out=ss[:, 0:1]), [b_f], [b_junk, b_ss])
        S.op("act", lambda e: e.activation(out=ss[:, 1:2], in_=ss[:, 0:1], func=AF.Sqrt, scale=1.0 / D_, bias=self.eps_t[:, 0:1]), [b_ss], [b_ss])
        S.op("dve", lambda e: e.reciprocal(ss[:, 2:3], ss[:, 1:2]), [b_ss], [b_ss])
        if coef != 1.0:
            S.op("dve", lambda e: e.tensor_scalar(ss[:, 2:3], ss[:, 2:3], coef, None, op0=ALU.mult), [b_ss], [b_ss])
        S.op("dve", lambda e: e.scalar_tensor_tensor(out=f_t[:], in0=f_t[:], scalar=ss[:, 2:3], in1=wb[:], op0=ALU.mult, op1=ALU.mult), [b_f, b_ss, b_wb], [b_f])
        S.op("dve", lambda e: e.tensor_add(f_t[:], f_t[:], h_t[:]), [b_f, b_h], [b_f])
        S.dma("sp", self.y[tt * 128:(tt + 1) * 128, :], f_t[:], b_f, self.ybuf[tt])

    def stream_src(self, first):
        if first:
            return (lambda tt: self.x[tt * 128:(tt + 1) * 128, :]), (lambda tt: None)
        return (lambda tt: self.y[tt * 128:(tt + 1) * 128, :]), (lambda tt: self.ybuf[tt])

    def ffn(self, fi, nw_pre, nw_post, first):
        nc, S = self.nc, self.S
        G = 1024
        NT = G // 128
        src, src_bufs = self.stream_src(first)
        wg_v = self.wg[fi].rearrange("(kc p) f -> p kc f", p=128)
        wu_v = self.wu[fi].rearrange("(kc p) f -> p kc f", p=128)
        wd_v = self.wd[fi].rearrange("(fc p) d -> p fc d", p=128)
        BW = 256
        with ExitStack() as st:
            aT = self.sb(st, "f_aT", [128, NFC, G], BF16); b_aT = Buf()
            wpost = self.sb(st, "f_wpost", [128, D_], F32); b_wpost = Buf()
            S.dma("sp", wpost[:], self.norm_w[nw_post:nw_post + 1, :].partition_broadcast(128), None, b_wpost)
            for g in range(S_ // G):
                tts = list(range(g * NT, (g + 1) * NT))
                with ExitStack() as sa:
                    hnT = self.sb(sa, "f_hnT", [128, 16, G], BF16); b_hnT = Buf()
                    with ExitStack() as st2:
                        self.pre_norm(st2, src, src_bufs, nw_pre, tts, hnT, b_hnT)
                        S.barrier()
                    wgb = [self.sb(sa, "f_wg%d" % i, [128, 16, BW], BF16) for i in range(2)]
                    wub = [self.sb(sa, "f_wu%d" % i, [128, 16, BW], BF16) for i in range(2)]
                    b_wgb = [Buf(), Buf()]; b_wub = [Buf(), Buf()]
                    sg = [self.sb(sa, "f_sg%d" % i, [128, 512], F32) for i in range(4)]
                    b_sg = [Buf() for _ in range(4)]
                    ps = [self.pp(sa, "f_ps%d" % i, [128, 512], F32) for i in range(8)]
                    b_ps = [PB() for _ in range(8)]
                    cnt = 0
                    for blk in range(FF // BW):
                        sl = blk % 2
                        S.dma("pool", wgb[sl][:], wg_v[:, :, blk * BW:(blk + 1) * BW], None, b_wgb[sl])
                        S.dma("pool", wub[sl][:], wu_v[:, :, blk * BW:(blk + 1) * BW], None, b_wub[sl])
                        for c in range(BW // 128):
                            fc = blk * (BW // 128) + c
                            for tb in range(G // 512):
                                pi_ = (cnt % 4) * 2
                                pg, pu, bpg, bpu = ps[pi_], ps[pi_ + 1], b_ps[pi_], b_ps[pi_ + 1]
                                for kc in range(16):
                                    S.op("pe", lambda e: e.matmul(pg[:], wgb[sl][:, kc, c * 128:(c + 1) * 128], hnT[:, kc, tb * 512:(tb + 1) * 512], start=(kc == 0), stop=(kc == 15)),
                                         [b_wgb[sl], b_hnT], [bpg], inc=(kc == 15))
                                for kc in range(16):
                                    S.op("pe", lambda e: e.matmul(pu[:], wub[sl][:, kc, c * 128:(c + 1) * 128], hnT[:, kc, tb * 512:(tb + 1) * 512], start=(kc == 0), stop=(kc == 15)),
                                         [b_wub[sl], b_hnT], [bpu], inc=(kc == 15))
                                sgt, bsg = sg[cnt % 4], b_sg[cnt % 4]
                                S.op("act", lambda e: e.activation(out=sgt[:], in_=pg[:], func=AF.Silu), [bpg], [bsg])
                                S.op("dve", lambda e: e.tensor_tensor(out=aT[:, fc, tb * 512:(tb + 1) * 512], in0=sgt[:], in1=pu[:], op=ALU.mult), [bsg, bpu], [b_aT])
                                cnt += 1
                    S.barrier()
                with ExitStack() as sb_:
                    NWD = 4
                    wdb = [self.sb(sb_, "f_wd%d" % i, [128, 2, 512], BF16) for i in range(NWD)]
                    b_wdb = [Buf() for _ in range(NWD)]
                    fsb = [self.sb(sb_, "f_fsb%d" % i, [128, D_], F32) for i in range(NT)]
                    b_fsb = [Buf() for _ in range(NT)]
                    hp = [self.sb(sb_, "f_hp%d" % i, [128, D_], F32) for i in range(3)]
                    b_hp = [Buf() for _ in range(3)]
                    junk = [self.sb(sb_, "f_junk%d" % i, [128, D_], BF16) for i in range(2)]; b_junk = [Buf(), Buf()]
                    ss = [self.sb(sb_, "f_ss%d" % i, [128, 4], F32) for i in range(2)]; b_ss = [Buf(), Buf()]
                    ps = [self.pp(sb_, "f_pb%d" % i, [128, 512], F32) for i in range(8)]
                    b_ps = [PB() for _ in range(8)]
                    wcnt = 0
                    for dq in range(4):
                        for fc2 in range(NFC // 2):
                            sl = wcnt % NWD; wcnt += 1
                            S.dma("pool", wdb[sl][:], wd_v[:, 2 * fc2:2 * fc2 + 2, dq * 512:(dq + 1) * 512], None, b_wdb[sl])
                            for f_ in range(2):
                                fc = 2 * fc2 + f_
                                for t8 in range(NT):
                                    S.op("pe", lambda e: e.matmul(ps[t8][:], aT[:, fc, t8 * 128:(t8 + 1) * 128], wdb[sl][:, f_, :], start=(fc == 0), stop=(fc == NFC - 1)),
                                         [b_aT, b_wdb[sl]], [b_ps[t8]], inc=(t8 == NT - 1))
                        for t8 in range(NT):
                            if t8 % 2 == 0:
                                S.op("act", lambda e: e.activation(out=fsb[t8][:, dq * 512:(dq + 1) * 512], in_=ps[t8][:], func=AF.Copy), [b_ps[t8]], [b_fsb[t8]])
                            else:
                                S.op("dve", lambda e: e.tensor_copy(fsb[t8][:, dq * 512:(dq + 1) * 512], ps[t8][:]), [b_ps[t8]], [b_fsb[t8]])
                    for t8 in range(NT):
                        tt = g * NT + t8
                        self.post_tile(tt, fsb[t8], b_fsb[t8], wpost, b_wpost, 0.5, hp[t8 % 3], b_hp[t8 % 3], src(tt), src_bufs(tt), junk[t8 % 2], b_junk[t8 % 2], ss[t8 % 2], b_ss[t8 % 2], ldq="pool")
                    S.barrier()

    def build(self):
        nc, S, es = self.nc, self.S, self.es
        cfg = self.cfg
        with es:
            self.eps_t = self.sb(es, "eps_t", [128, 1], F32)
            b_eps = Buf()
            S.op("dve", lambda e: e.memset(self.eps_t[:], EPS), [], [b_eps])
            self.setup()
            S.barrier()
            first = True
            n_sub = cfg.get("n_sub", 12)
            sub = 0
            for layer in range(4):
                for which in range(3):
                    if sub >= n_sub:
                        break
                    if which == 0:
                        self.ffn(layer * 2 + 0, layer * 6 + 0, layer * 6 + 1, first)
                    elif which == 1:
                        self.mixer(layer, first)
                    else:
                        self.ffn(layer * 2 + 1, layer * 6 + 4, layer * 6 + 5, first)
                    first = False
                    sub += 1
            S.barrier()
        return nc

    def mixer(self, layer, first):
        with ExitStack() as mst:
            self._mst = mst
            self._wo = None
            if layer % 2 == 0:
                self.even_mixer(layer, first)
            else:
                self.odd_mixer(layer, first)
            if self.cfg.get("ev_stage", 9) >= 5 and self.cfg.get("od_stage", 9) >= 3:
                self.out_proj(layer, first)
            self._wo = None

    def proj_slab(self, wslab, b_w, ncol, hnT, b_hnT, pj, b_pj, evac):
        S = self.S
        for half in range(2):
            for bk in range(2):
                c0 = half * 1024 + bk * 512
                for kc in range(16):
                    S.op("pe", lambda e: e.matmul(pj[bk][0:ncol, :], wslab[:, kc, 0:ncol], hnT[:, kc, c0:c0 + 512], start=(kc == 0), stop=(kc == 15)),
                         [b_w, b_hnT], [b_pj[bk]], inc=(kc == 15))
                evac(bk, c0)

    def even_mixer(self, layer, first):
        nc, S = self.nc, self.S
        i = layer // 2
        src, src_bufs = self.stream_src(first)
        w_in_v = self.ev_w_in[i].rearrange("(kc p) c -> p kc c", p=128)
        with ExitStack() as st:
            hnT = self.sb(st, "m_hnT", [128, 16, S_], BF16); b_hnT = Buf()
            with ExitStack() as st2:
                self.pre_norm(st2, src, src_bufs, layer * 6 + 2, list(range(16)), hnT, b_hnT)
                S.barrier()
            cosF = self.sb(st, "m_cos", [128, S_], F32); sinF = self.sb(st, "m_sin", [128, S_], F32); b_cs = Buf()
            S.dma("sp", cosF[:], self.rope_d[0], self.b_rope, b_cs)
            S.dma("sp", sinF[:], self.rope_d[1], self.b_rope, b_cs)
            mk = self.sb(st, "m_mk", [128, 256], BF16); b_mk = Buf()
            S.dma("sp", mk[:], self.mk_ev_d, None, b_mk)
            wsl = [[self.sb(st, "m_w%d_%d" % (j, k), [128, 16, 128], BF16) for k in range(2)] for j in range(3)]
            b_wsl = [[Buf(), Buf()] for j in range(3)]
            qT = self.sb(st, "m_qT", [128, S_], BF16); kT = self.sb(st, "m_kT", [128, S_], BF16); vT = self.sb(st, "m_vT", [128, S_], BF16)
            b_qT, b_kT, b_vT = Buf(), Buf(), Buf()
            vS = self.sb(st, "m_vS", [128, 48, 128], BF16); b_vS = Buf()
            acc = self.sb(st, "m_acc", [128, S_], F32); dacc = self.sb(st, "m_dacc", [128, S_], F32); b_acc, b_dacc = Buf(), Buf()
            qsw = [self.sb(st, "m_qsw%d" % k, [128, 512], F32) for k in range(2)]; b_qsw = [Buf(), Buf()]
            t1 = [self.sb(st, "m_t1%d" % k, [128, 512], F32) for k in range(2)]; b_t1 = [Buf(), Buf()]
            et = [self.sb(st, "m_et%d" % k, [128, 256], F32) for k in range(3)]; b_et = [Buf() for _ in range(3)]
            pt = [self.sb(st, "m_pt%d" % k, [128, 256], BF16) for k in range(3)]; b_pt = [Buf() for _ in range(3)]
            osb = [self.sb(st, "m_osb%d" % k, [128, S_], BF16) for k in range(2)]; b_osb = [Buf(), Buf()]
            pj = [self.pp(st, "m_pj%d" % k, [128, 512], F32) for k in range(2)]; b_pj = [PB(), PB()]
            vtp = self.pp(st, "m_vtp", [128, 8, 128], BF16); b_vtp = PB()
            sc = [self.pp(st, "m_sc%d" % k, [128, 512], F32) for k in range(3)]; b_sc = [PB() for _ in range(3)]
            ud = [self.pp(st, "m_ud%d" % k, [128, 512], F32) for k in range(2)]; b_ud = [PB(), PB()]
            rcnt = [0]

            def rope_evac(dst, b_dst):
                def f(bk, c0):
                    k = rcnt[0] % 2; rcnt[0] += 1
                    S.op("act", lambda e: e.activation(out=qsw[k][0:64, :], in_=pj[bk][64:128, :], func=AF.Copy), [b_pj[bk]], [b_qsw[k], b_pj[bk]])
                    S.op("act", lambda e: e.activation(out=qsw[k][64:128, :], in_=pj[bk][0:64, :], func=AF.Copy), [b_pj[bk]], [b_qsw[k], b_pj[bk]])
                    S.op("dve", lambda e: e.tensor_tensor(out=t1[k][:], in0=pj[bk][:], in1=cosF[:, c0:c0 + 512], op=ALU.mult), [b_pj[bk], b_cs], [b_t1[k], b_pj[bk]])
                    S.op("dve", lambda e: e.tensor_tensor(out=qsw[k][:], in0=qsw[k][:], in1=sinF[:, c0:c0 + 512], op=ALU.mult), [b_qsw[k], b_cs], [b_qsw[k]])
                    S.op("dve", lambda e: e.tensor_tensor(out=dst[:, c0:c0 + 512], in0=t1[k][:], in1=qsw[k][:], op=ALU.add), [b_t1[k], b_qsw[k]], [b_dst])
                return f

            def copy_evac(dst, b_dst):
                def f(bk, c0):
                    S.op("act", lambda e: e.activation(out=dst[:, c0:c0 + 512], in_=pj[bk][:], func=AF.Copy), [b_pj[bk]], [b_dst])
                return f

            blocks = []
            for pi_, d in enumerate((1, 4, 16)):
                nb = 16 // d
                for r in range(d):
                    for n in range(nb):
                        blocks.append((pi_, d, n * 128 * d + r, (n - 1) * 128 * d + r if n > 0 else None))
            blk_index = {(b[1], b[2]): bi for bi, b in enumerate(blocks)}
            sl_of = lambda start, d: slice(start, start + 127 * d + 1, d) if d > 1 else slice(start, start + 128)

            bcnt = 0
            from collections import deque
            epipe = deque(); ELA = 2
            stage = self.cfg.get("ev_stage", 9)
            for h in range(12 if stage >= 3 else (1 if stage >= 1 else 0)):
                k2 = h % 2
                sub_ = self.cfg.get("ev_sub", 9)
                tb_left = list(range(0, 48, 8))

                def emit_tb():
                    if not tb_left:
                        return
                    b0 = tb_left.pop(0)
                    for jj in range(8):
                        _, d, start, _ = blocks[b0 + jj]
                        S.op("pe", lambda e: e.transpose(vtp[:, jj, :], vT[:, sl_of(start, d)], self.ident[:]), [b_vT, self.b_ident], [b_vtp], inc=(jj == 7))
                    S.op("act", lambda e: e.activation(out=vS[:, b0:b0 + 8, :], in_=vtp[:], func=AF.Copy), [b_vtp], [b_vS])

                def with_tb(ev):
                    def f(bk, c0):
                        ev(bk, c0)
                        emit_tb()
                    return f
                NH_ = 12 if stage >= 3 else (1 if stage >= 1 else 0)
                for j, (dst, b_dst, col0) in ((2, (vT, b_vT, 3072 + h * 128)), (0, (qT, b_qT, h * 128)), (1, (kT, b_kT, 1536 + h * 128))):
                    if h == 0:
                        S.dma("pool", wsl[j][k2][:], w_in_v[:, :, col0:col0 + 128], None, b_wsl[j][k2])
                    if h + 1 < NH_:
                        S.dma("pool", wsl[j][1 - k2][:], w_in_v[:, :, col0 + 128:col0 + 256], None, b_wsl[j][1 - k2])
                    self.proj_slab(wsl[j][k2], b_wsl[j][k2], 128, hnT, b_hnT, pj, b_pj, with_tb(rope_evac(dst, b_dst)) if j < 2 else copy_evac(dst, b_dst))
                while tb_left:
                    emit_tb()
                for bi, (pi_, d, start, pstart) in enumerate(blocks if stage >= 2 else []):
                    k = bcnt % 3; bcnt += 1
                    W = 256 if pstart is not None else 128
                    qs = sl_of(start, d)
                    S.op("pe", lambda e: e.matmul(sc[k][:, 0:W], self.ident[:], mk[:, 0:W], start=True, stop=False), [self.b_ident, b_mk], [b_sc[k]], inc=False)
                    S.op("pe", lambda e: e.matmul(sc[k][:, 0:128], kT[:, qs], qT[:, qs], start=False, stop=(pstart is None)), [b_kT, b_qT], [b_sc[k]], inc=(pstart is None))
                    if pstart is not None:
                        S.op("pe", lambda e: e.matmul(sc[k][:, 128:256], kT[:, sl_of(pstart, d)], qT[:, qs], start=False, stop=True), [b_kT, b_qT], [b_sc[k]])
                    S.op("act", lambda e: e.activation(out=pt[k][:, 0:W], in_=sc[k][:, 0:W], func=AF.Exp, scale=SCALE), [b_sc[k]], [b_pt[k]])

                    def stage2(k=k, bi=bi, pi_=pi_, d=d, pstart=pstart, qs=qs):
                        ku = bi % 2
                        S.op("pe", lambda e: e.matmul(ud[ku][:, 0:128], vS[:, bi, :], pt[k][:, 0:128], start=True, stop=(pstart is None)), [b_vS, b_pt[k]], [b_ud[ku]], inc=False)
                        if pstart is not None:
                            pbi = blk_index[(d, pstart)]
                            S.op("pe", lambda e: e.matmul(ud[ku][:, 0:128], vS[:, pbi, :], pt[k][:, 128:256], start=False, stop=True), [b_vS, b_pt[k]], [b_ud[ku]], inc=False)
                        S.op("pe", lambda e: e.matmul(ud[ku][:, 128:256], self.ones[:], pt[k][:, 0:128], start=True, stop=(pstart is None)), [self.b_ones, b_pt[k]], [b_ud[ku]], inc=(pstart is None))
                        if pstart is not None:
                            S.op("pe", lambda e: e.matmul(ud[ku][:, 128:256], self.ones[:], pt[k][:, 128:256], start=False, stop=True), [self.b_ones, b_pt[k]], [b_ud[ku]])
                        if pi_ == 0:
                            S.op("act", lambda e: e.activation(out=acc[:, qs], in_=ud[ku][:, 0:128], func=AF.Copy), [b_ud[ku]], [b_acc])
                            S.op("act", lambda e: e.activation(out=dacc[:, qs], in_=ud[ku][:, 128:256], func=AF.Copy), [b_ud[ku]], [b_dacc])
                        else:
                            S.op("dve", lambda e: e.tensor_tensor(out=acc[:, qs], in0=acc[:, qs], in1=ud[ku][:, 0:128], op=ALU.add), [b_ud[ku], b_acc], [b_acc])
                            S.op("dve", lambda e: e.tensor_tensor(out=dacc[:, qs], in0=dacc[:, qs], in1=ud[ku][:, 128:256], op=ALU.add), [b_ud[ku], b_dacc], [b_dacc])
                    epipe.append(stage2)
                    while len(epipe) > ELA:
                        epipe.popleft()()
                while epipe:
                    epipe.popleft()()
                if sub_ < 4:
                    continue
                S.op("act", lambda e: e.activation(out=dacc[:], in_=dacc[:], func=AF.Ln), [b_dacc], [b_dacc])
                S.op("act", lambda e: e.activation(out=dacc[:], in_=dacc[:], func=AF.Exp, scale=-1.0), [b_dacc], [b_dacc])
                S.op("dve", lambda e: e.tensor_tensor(out=osb[k2][:], in0=acc[:], in1=dacc[:], op=ALU.mult), [b_acc, b_dacc], [b_osb[k2]])
                S.dma("sp", self.mixT_d[h], osb[k2][:], b_osb[k2], self.b_mixT[h])
            uT = self.sb(st, "m_uT", [128, S_], F32); b_uT = Buf()
            pw = self.sb(st, "m_pw", [128, 4, 128], BF16); b_pw = Buf()
            S.dma("pool", pw[:], self.pool_w[i].rearrange("g c d -> c g d"), None, b_pw)
            psc = self.sb(st, "m_psc", [128, 4], F32); b_psc = Buf()
            S.dma("sp", psc[:], self.pool_scale_t[i], None, b_psc)
            invc = self.sb(st, "m_invc", [128, 64], F32); b_invc = Buf()
            S.dma("sp", invc[:], self.invc_d, None, b_invc)
            for g, w in enumerate((2, 4, 8, 16) if stage >= 4 else ()):
                k2 = g % 2
                col0 = 4608 + g * 128
                S.dma("pool", wsl[0][k2][:], w_in_v[:, :, col0:col0 + 128], None, b_wsl[0][k2])
                self.proj_slab(wsl[0][k2], b_wsl[0][k2], 128, hnT, b_hnT, pj, b_pj, copy_evac(uT, b_uT))
                cur, b_cur = uT, b_uT
                pp2 = [(acc, b_acc), (dacc, b_dacc)]
                step = 1; n = 0
                while step < w:
                    nxt, b_nxt = pp2[n % 2]; n += 1
                    S.op("dve", lambda e: e.tensor_tensor(out=nxt[:, step:], in0=cur[:, step:], in1=cur[:, 0:S_ - step], op=ALU.add), [b_cur], [b_nxt])
                    S.op("act", lambda e: e.activation(out=nxt[:, 0:step], in_=cur[:, 0:step], func=AF.Copy), [b_cur], [b_nxt])
                    cur, b_cur = nxt, b_nxt; step *= 2
                S.op("dve", lambda e: e.scalar_tensor_tensor(out=qT[:, 16:], in0=cur[:, 16:], scalar=1.0 / w, in1=uT[:, 16:], op0=ALU.mult, op1=ALU.subtract), [b_cur, b_uT], [b_qT])
                S.op("dve", lambda e: e.tensor_tensor(out=cur[:, 0:16], in0=cur[:, 0:16], in1=invc[:, g * 16:(g + 1) * 16], op=ALU.mult), [b_cur, b_invc], [b_cur])
                S.op("dve", lambda e: e.tensor_tensor(out=qT[:, 0:16], in0=cur[:, 0:16], in1=uT[:, 0:16], op=ALU.subtract), [b_cur, b_uT], [b_qT])
                for bk4 in range(4):
                    c0 = bk4 * 512; bk = bk4 % 2
                    S.op("pe", lambda e: e.matmul(pj[bk][:], pw[:, g, :], qT[:, c0:c0 + 512], start=True, stop=True), [b_pw, b_qT], [b_pj[bk]])
                    S.op("dve", lambda e: e.tensor_scalar(osb[k2][:, c0:c0 + 512], pj[bk][:], psc[:, g:g + 1], None, op0=ALU.mult), [b_pj[bk], b_psc], [b_osb[k2]])
                S.dma("sp", self.mixT_d[12 + g], osb[k2][:], b_osb[k2], self.b_mixT[12 + g])
            S.barrier()

    def out_proj(self, layer, first):
        nc, S = self.nc, self.S
        i = layer // 2
        wout = (self.ev_w_out if layer % 2 == 0 else self.od_w_out)[i].rearrange("(fc p) d -> p fc d", p=128)
        src, src_bufs = self.stream_src(first)
        with ExitStack() as st:
            mixT = self.sb(st, "o_mixT", [128, 16, S_], BF16); b_mixT = Buf()
            for fc in range(16):
                S.dma("sp", mixT[:, fc, :], self.mixT_d[fc], self.b_mixT[fc], b_mixT)
            if self._wo is not None:
                wo, b_wo = self._wo
            else:
                wo = self.sb(st, "o_wo", [128, 16, D_], BF16); b_wo = Buf()
                for fc4 in range(4):
                    S.dma("pool", wo[:, fc4 * 4:(fc4 + 1) * 4, :], wout[:, fc4 * 4:(fc4 + 1) * 4, :], None, b_wo)
            fsb = [self.sb(st, "o_fsb%d" % k, [128, D_], F32) for k in range(2)]; b_fsb = [Buf(), Buf()]
            hp = [self.sb(st, "o_hp%d" % k, [128, D_], F32) for k in range(2)]; b_hp = [Buf(), Buf()]
            wpost = self.sb(st, "o_wpost", [128, D_], F32); b_wpost = Buf()
            S.dma("sp", wpost[:], self.norm_w[layer * 6 + 3:layer * 6 + 4, :].partition_broadcast(128), None, b_wpost)
            junk = [self.sb(st, "o_junk%d" % k, [128, D_], BF16) for k in range(2)]; b_junk = [Buf(), Buf()]
            ss = [self.sb(st, "o_ss%d" % k, [128, 4], F32) for k in range(2)]; b_ss = [Buf(), Buf()]
            ps = [self.pp(st, "o_ps%d" % k, [128, 512], F32) for k in range(8)]; b_ps = [PB() for _ in range(8)]
            for tt in range(16):
                par = tt % 2
                for db in range(4):
                    bi = par * 4 + db
                    for fc in range(16):
                        S.op("pe", lambda e: e.matmul(ps[bi][:], mixT[:, fc, tt * 128:(tt + 1) * 128], wo[:, fc, db * 512:(db + 1) * 512], start=(fc == 0), stop=(fc == 15)),
                             [b_mixT, b_wo], [b_ps[bi]], inc=(fc == 15))
                    if db % 2 == 0:
                        S.op("act", lambda e: e.activation(out=fsb[par][:, db * 512:(db + 1) * 512], in_=ps[bi][:], func=AF.Copy), [b_ps[bi]], [b_fsb[par]])
                    else:
                        S.op("dve", lambda e: e.tensor_copy(fsb[par][:, db * 512:(db + 1) * 512], ps[bi][:]), [b_ps[bi]], [b_fsb[par]])
                self.post_tile(tt, fsb[par], b_fsb[par], wpost, b_wpost, 1.0, hp[par], b_hp[par], src(tt), src_bufs(tt), junk[par], b_junk[par], ss[par], b_ss[par])
            S.barrier()

    def odd_mixer(self, layer, first):
        nc, S = self.nc, self.S
        i = layer // 2
        src, src_bufs = self.stream_src(first)
        w_in_v = self.od_w_in[i].rearrange("(kc p) c -> p kc c", p=128)
        C_Q, C_KC, C_VC, C_KS, C_VS, C_KW, C_VW, C_GL, C_U, C_CG, C_BG = 0, 1536, 1792, 2048, 2304, 2560, 2816, 3072, 3108, 3620, 4132
        stage = self.cfg.get("od_stage", 9)
        with ExitStack() as st:
            kcmpT = [self.sb(st, "n_kcmpT%d" % g, [128, 128], BF16) for g in range(2)]; b_kcmpT = [Buf(), Buf()]
            vcmp = [self.sb(st, "n_vcmp%d" % g, [128, 128], BF16) for g in range(2)]; b_vcmp = [Buf(), Buf()]
            ksT = [self.sb(st, "n_ksT%d" % g, [128, S_], BF16) for g in range(2)]; b_ksT = [Buf(), Buf()]
            kwT = [self.sb(st, "n_kwT%d" % g, [128, S_], BF16) for g in range(2)]; b_kwT = [Buf(), Buf()]
            vs = [self.sb(st, "n_vs%d" % g, [128, 16, 128], BF16) for g in range(2)]; b_vs = [Buf(), Buf()]
            vw = [self.sb(st, "n_vw%d" % g, [128, 16, 128], BF16) for g in range(2)]; b_vw = [Buf(), Buf()]
            with ExitStack() as s1:
                hnT = self.sb(s1, "n_hnT", [128, 16, S_], BF16); b_hnT = Buf()
                with ExitStack() as st2:
                    self.pre_norm(st2, src, src_bufs, layer * 6 + 2, list(range(16)), hnT, b_hnT)
                    S.barrier()
                cosF = self.sb(s1, "n_cos", [128, S_], F32); sinF = self.sb(s1, "n_sin", [128, S_], F32); b_cs = Buf()
                S.dma("sp", cosF[:], self.rope_d[0], self.b_rope, b_cs)
                S.dma("sp", sinF[:], self.rope_d[1], self.b_rope, b_cs)
                wsl = [self.sb(s1, "n_w%d" % k, [128, 16, 256], BF16) for k in range(2)]; b_wsl = [Buf(), Buf()]
                qsw = [self.sb(s1, "n_qsw%d" % k, [128, 512], F32) for k in range(2)]; b_qsw = [Buf(), Buf()]
                t1 = [self.sb(s1, "n_t1%d" % k, [128, 512], F32) for k in range(2)]; b_t1 = [Buf(), Buf()]
                f32a = self.sb(s1, "n_f32a", [128, S_], F32); b_f32a = Buf()
                f32b = self.sb(s1, "n_f32b", [128, S_], F32); b_f32b = Buf()
                f32c = self.sb(s1, "n_f32c", [128, S_], F32); b_f32c = Buf()
                ob = [self.sb(s1, "n_ob%d" % k, [128, S_], BF16) for k in range(4)]; b_ob = [Buf() for _ in range(4)]
                pj = [self.pp(s1, "n_pj%d" % k, [128, 512], F32) for k in range(2)]; b_pj = [PB(), PB()]
                px = [self.pp(s1, "n_px%d" % k, [128, 512], F32) for k in range(2)]; b_px = [PB(), PB()]
                rcnt = [0]; wcnt = [0]

                def rope_evac(dst, b_dst, udst=None, b_udst=None):
                    def f(bk, c0):
                        k = rcnt[0] % 2; rcnt[0] += 1
                        S.op("act", lambda e: e.activation(out=qsw[k][0:64, :], in_=pj[bk][64:128, :], func=AF.Copy), [b_pj[bk]], [b_qsw[k]])
                        S.op("act", lambda e: e.activation(out=qsw[k][64:128, :], in_=pj[bk][0:64, :], func=AF.Copy), [b_pj[bk]], [b_qsw[k]])
                        if udst is not None:
                            S.op("act", lambda e: e.activation(out=udst[:, c0:c0 + 512], in_=pj[bk][:], func=AF.Copy), [b_pj[bk]], [b_udst])
                        S.op("dve", lambda e: e.tensor_tensor(out=t1[k][:], in0=pj[bk][:], in1=cosF[:, c0:c0 + 512], op=ALU.mult), [b_pj[bk], b_cs], [b_t1[k]])
                        S.op("dve", lambda e: e.tensor_tensor(out=qsw[k][:], in0=qsw[k][:], in1=sinF[:, c0:c0 + 512], op=ALU.mult), [b_qsw[k], b_cs], [b_qsw[k]])
                        S.op("dve", lambda e: e.tensor_tensor(out=dst[:, c0:c0 + 512], in0=t1[k][:], in1=qsw[k][:], op=ALU.add), [b_t1[k], b_qsw[k]], [b_dst])
                    return f

                def copy_evac(dst, b_dst, np_=128, func=AF.Copy):
                    def f(bk, c0):
                        S.op("act", lambda e: e.activation(out=dst[0:np_, c0:c0 + 512], in_=pj[bk][0:np_, :], func=func), [b_pj[bk]], [b_dst])
                    return f

                plan = [(C_GL, 36)]
                for kv_ in range(2):
                    for g_ in range(2):
                        plan.append(((C_KC if kv_ == 0 else C_VC) + g_ * 128, 128))
                for g_ in range(2):
                    plan += [(C_KS + g_ * 128, 128), (C_KW + g_ * 128, 128)]
                plan += [(C_VS, 256), (C_VW, 256)]
                for c_ in range(4):
                    plan += [(C_U + c_ * 128, 128), (C_CG + c_ * 128, 128), (C_BG + c_ * 128, 128)]
                for h_ in range(12):
                    plan.append((C_Q + h_ * 128, 128))
                issued = [0]

                def ensure(i):
                    while issued[0] <= i and issued[0] < len(plan):
                        c0_, n_ = plan[issued[0]]
                        kk = issued[0] % 2
                        S.dma("pool", wsl[kk][:, :, 0:n_], w_in_v[:, :, c0_:c0_ + n_], None, b_wsl[kk])
                        issued[0] += 1

                def take(col0, ncol):
                    i = wcnt[0]; wcnt[0] += 1
                    assert plan[i] == (col0, ncol), (i, plan[i], col0, ncol)
                    ensure(i)
                    return i % 2

                def slab(col0, ncol, evac):
                    k = take(col0, ncol)
                    ensure(wcnt[0])
                    self.proj_slab(wsl[k], b_wsl[k], ncol, hnT, b_hnT, pj, b_pj, evac)

                slab(C_GL, 36, copy_evac(f32a, b_f32a, 36, AF.Sigmoid))
                S.dma("sp", self.gl_d, f32a[0:36, :], b_f32a, self.b_gl)
                w1 = self.sb(s1, "n_w1", [128, 32, 256], BF16); b_w1 = Buf()
                w2 = self.sb(s1, "n_w2", [128, 2, 128], BF16); b_w2 = Buf()
                peT = self.sb(s1, "n_peT", [128, 32], F32); b_peT = Buf()
                X = self.sb(s1, "n_X", [128, 32, 127], BF16); b_X = Buf()
                hidT = self.sb(s1, "n_hidT", [128, 2, 127], BF16); b_hidT = Buf()
                gx = [self.sb(s1, "n_gx%d" % k, [128, 127], F32) for k in range(3)]; b_gx = [Buf() for _ in range(3)]
                for kv in range(2):
                    S.dma("pool", w1[:], self.cmp_w1[kv][i].rearrange("(j p) h -> p j h", p=128), None, b_w1)
                    S.dma("pool", w2[:], self.cmp_w2[kv][i].rearrange("(hc p) d -> p hc d", p=128), None, b_w2)
                    S.dma("sp", peT[:], self.cmp_peT[kv][i], None, b_peT)
                    for g in range(2):
                        slab((C_KC if kv == 0 else C_VC) + g * 128, 128, copy_evac(f32a, b_f32a))
                        for j in range(32):
                            S.op("dve", lambda e: e.tensor_scalar(X[:, j, :], f32a[:, j:j + 16 * 126 + 1:16], peT[:, j:j + 1], None, op0=ALU.add), [b_f32a, b_peT], [b_X])
                        for hc in range(2):
                            for j in range(32):
                                S.op("pe", lambda e: e.matmul(px[hc][:, 0:127], w1[:, j, hc * 128:(hc + 1) * 128], X[:, j, :], start=(j == 0), stop=(j == 31)), [b_w1, b_X], [b_px[hc]], inc=(j == 31))
                            S.op("act", lambda e: e.activation(out=gx[0][:], in_=px[hc][:, 0:127], func=AF.Square), [b_px[hc]], [b_gx[0]])
                            S.op("dve", lambda e: e.tensor_scalar(gx[0][:], gx[0][:], 0.044715, 1.0, op0=ALU.mult, op1=ALU.add), [b_gx[0]], [b_gx[0]])
                            S.op("dve", lambda e: e.tensor_tensor(out=gx[1][:], in0=gx[0][:], in1=px[hc][:, 0:127], op=ALU.mult), [b_gx[0], b_px[hc]], [b_gx[1]])
                            S.op("act", lambda e: e.activation(out=gx[2][:], in_=gx[1][:], func=AF.Tanh, scale=0.7978845608028654), [b_gx[1]], [b_gx[2]])
                            S.op("dve", lambda e: e.scalar_tensor_tensor(out=gx[2][:], in0=gx[2][:], scalar=1.0, in1=px[hc][:, 0:127], op0=ALU.add, op1=ALU.mult), [b_gx[2], b_px[hc]], [b_gx[2]])
                            S.op("dve", lambda e: e.tensor_scalar(hidT[:, hc, :], gx[2][:], 0.5, None, op0=ALU.mult), [b_gx[2]], [b_hidT])
                        if kv == 0:
                            for hc in range(2):
                                S.op("pe", lambda e: e.matmul(px[0][:, 0:127], w2[:, hc, :], hidT[:, hc, :], start=(hc == 0), stop=(hc == 1)), [b_w2, b_hidT], [b_px[0]], inc=(hc == 1))
                            S.op("act", lambda e: e.activation(out=kcmpT[g][:, 0:127], in_=px[0][:, 0:127], func=AF.Copy), [b_px[0]], [b_kcmpT[g]])
                        else:
                            for hc in range(2):
                                S.op("pe", lambda e: e.matmul(px[0][0:127, 0:128], hidT[:, hc, :], w2[:, hc, :], start=(hc == 0), stop=(hc == 1)), [b_w2, b_hidT], [b_px[0]], inc=(hc == 1))
                            S.op("act", lambda e: e.activation(out=vcmp[g][0:127, :], in_=px[0][0:127, 0:128], func=AF.Copy), [b_px[0]], [b_vcmp[g]])
                for g in range(2):
                    slab(C_KS + g * 128, 128, rope_evac(ksT[g], b_ksT[g]))
                    slab(C_KW + g * 128, 128, rope_evac(kwT[g], b_kwT[g]))
                for (c0v, dsts, b_dsts) in ((C_VS, vs, b_vs), (C_VW, vw, b_vw)):
                    k = take(c0v, 256)
                    ensure(wcnt[0])
                    for t2 in range(8):
                        bk = t2 % 2
                        for u in range(2):
                            tt = t2 * 2 + u
                            for kc in range(16):
                                S.op("pe", lambda e: e.matmul(pj[bk][:, u * 256:(u + 1) * 256], hnT[:, kc, tt * 128:(tt + 1) * 128], wsl[k][:, kc, :], start=(kc == 0), stop=(kc == 15)),
                                     [b_hnT, b_wsl[k]], [b_pj[bk]], inc=(kc == 15))
                        for u in range(2):
                            tt = t2 * 2 + u
                            for g in range(2):
                                S.op("act", lambda e: e.activation(out=dsts[g][:, tt, :], in_=pj[bk][:, u * 256 + g * 128:u * 256 + (g + 1) * 128], func=AF.Copy), [b_pj[bk]], [b_dsts[g]])
                cw = self.sb(s1, "n_cw", [128, 12], F32); b_cw = Buf()
                S.dma("sp", cw[:], self.conv_wT[i], None, b_cw)
                for c in range(4):
                    slab(C_U + c * 128, 128, copy_evac(f32a, b_f32a))
                    slab(C_CG + c * 128, 128, copy_evac(f32b, b_f32b))
                    slab(C_BG + c * 128, 128, copy_evac(f32c, b_f32c))
                    S.op("dve", lambda e: e.tensor_tensor(out=f32a[:], in0=f32a[:], in1=f32b[:], op=ALU.mult), [b_f32a, b_f32b], [b_f32a])
                    S.op("dve", lambda e: e.tensor_scalar(f32b[:], f32a[:], cw[:, c * 3 + 2:c * 3 + 3], None, op0=ALU.mult), [b_f32a, b_cw], [b_f32b])
                    S.op("dve", lambda e: e.scalar_tensor_tensor(out=f32b[:, 1:], in0=f32a[:, 0:S_ - 1], scalar=cw[:, c * 3 + 1:c * 3 + 2], in1=f32b[:, 1:], op0=ALU.mult, op1=ALU.add), [b_f32a, b_cw, b_f32b], [b_f32b])
                    S.op("dve", lambda e: e.scalar_tensor_tensor(out=f32b[:, 2:], in0=f32a[:, 0:S_ - 2], scalar=cw[:, c * 3:c * 3 + 1], in1=f32b[:, 2:], op0=ALU.mult, op1=ALU.add), [b_f32a, b_cw, b_f32b], [b_f32b])
                    S.op("dve", lambda e: e.tensor_tensor(out=ob[c % 2][:], in0=f32b[:], in1=f32c[:], op=ALU.mult), [b_f32b, b_f32c], [b_ob[c % 2]])
                    S.dma("sp", self.mixT_d[12 + c], ob[c % 2][:], b_ob[c % 2], self.b_mixT[12 + c])
                for h in range(12):
                    k2 = h % 2
                    slab(C_Q + h * 128, 128, rope_evac(ob[2 + k2], b_ob[2 + k2], ob[k2], b_ob[k2]))
                    S.dma("sp", self.qscr_d[h, 0], ob[k2][:], b_ob[k2], self.b_qscr[h])
                    S.dma("sp", self.qscr_d[h, 1], ob[2 + k2][:], b_ob[2 + k2], self.b_qscr[h])
                S.barrier()
            if stage < 2:
                return
            self._uid += 1
            wo_t = self._mst.enter_context(nc.sbuf_tensor("o_wo_pf_%d" % self._uid, [128, 16, D_], BF16, side="right"))
            b_wo_t = Buf()
            wout_v = self.od_w_out[i].rearrange("(fc p) d -> p fc d", p=128)
            for fc4 in range(4):
                S.dma("pool", wo_t[:, fc4 * 4:(fc4 + 1) * 4, :], wout_v[:, fc4 * 4:(fc4 + 1) * 4, :], None, b_wo_t)
            self._wo = (wo_t, b_wo_t)
            with ExitStack() as s2:
                wbias = self.sb(s2, "a_wbias", [128, 8, 512], BF16); b_wbias = Buf()
                S.dma("sp", wbias[:], self.wbias_d, None, b_wbias)
                cbias = self.sb(s2, "a_cbias", [128, S_], BF16); b_cbias = Buf()
                S.dma("sp", cbias[:], self.cbias_d, None, b_cbias)
                cover = self.sb(s2, "a_cover", [128, 32], F32); b_cover = Buf()
                S.dma("sp", cover[:], self.cover_d, None, b_cover)
                nf01 = self.sb(s2, "a_nf01", [128, 512], F32); addc = self.sb(s2, "a_addc", [128, 512], F32); b_nfa = Buf()
                S.dma("sp", nf01[:], self.nf01_d, None, b_nfa)
                S.dma("sp", addc[:], self.addc_d, None, b_nfa)
                e_all = self.sb(s2, "a_eall", [32, 16, 128], BF16); b_eall = Buf()
                S.dma("sp", e_all[:], self.e_all_d, None, b_eall)
                Pacc = self.sb(s2, "a_Pacc", [128, S_], F32); b_Pacc = Buf()
                qt = [self.sb(s2, "a_q%d" % k, [128, S_], BF16) for k in range(2)]; b_qt = [Buf(), Buf()]
                gate = [self.sb(s2, "a_gate%d" % k, [128, S_], F32) for k in range(4)]; b_gate = [Buf() for _ in range(4)]
                oacc = self.sb(s2, "a_oacc", [128, S_], F32); b_oacc = Buf()
                oacc_b = self.sb(s2, "a_oaccb", [128, S_], F32); b_oacc_b = Buf()
                obf = [self.sb(s2, "a_obf%d" % k, [128, S_], BF16) for k in range(2)]; b_obf = [Buf(), Buf()]
                pt = [self.sb(s2, "a_pt%d" % k, [128, 512], BF16) for k in range(3)]; b_pt = [Buf() for _ in range(3)]
                rz = self.sb(s2, "a_rz", [128, 512], F32); b_rz = Buf()
                tq = self.sb(s2, "a_tq", [128, 512], F32); b_tq = Buf()
                scs = self.sb(s2, "a_scs", [128, 512], F32); b_scs = Buf()
                m8 = self.sb(s2, "a_m8", [128, 128], F32); b_m8 = Buf()
                selb = self.sb(s2, "a_selb", [128, 512], BF16); b_selb = Buf()
                selbT = self.sb(s2, "a_selbT", [32, S_], BF16); b_selbT = Buf()
                sp_ = [self.pp(s2, "a_s%d" % k, [128, 512], F32) for k in range(3)]; b_sp = [PB() for _ in range(3)]
                up = [self.pp(s2, "a_u%d" % k, [128, 512], F32) for k in range(2)]; b_up = [PB(), PB()]
                dp = [self.pp(s2, "a_d%d" % k, [128, 512], F32) for k in range(2)]; b_dp = [PB(), PB()]
                tp = self.pp(s2, "a_tp", [32, 8, 128], BF16); b_tp = PB()
                pcnt = [0]; ucnt = [0]

                from collections import deque
                pipe = deque(); LA = 2

                def push(fn):
                    pipe.append(fn)
                    while len(pipe) > LA:
                        pipe.popleft()()

                def flush():
                    while pipe:
                        pipe.popleft()()

                def attend(qsrc, b_q, Q, pairs, np_, after):
                    ub = ucnt[0] % 2; ucnt[0] += 1
                    n = len(pairs)
                    for pi_, pr in enumerate(pairs):
                        (kap, b_k, biases, vap, b_v) = pr[:5]
                        lo, hi = pr[5] if len(pr) > 5 else (0, 512)
                        k = pcnt[0] % 3; pcnt[0] += 1
                        nb_ = len(biases)
                        S.op("pe", lambda e: e.matmul(sp_[k][0:np_, lo:hi], kap, qsrc[:, Q * 512 + lo:Q * 512 + hi], start=True, stop=(nb_ == 0)), [b_k, b_q], [b_sp[k]], inc=(nb_ == 0))
                        for bi_, (bl, br, bb) in enumerate(biases):
                            S.op("pe", lambda e: e.matmul(sp_[k][0:np_, lo:hi], bl, br[:, lo:hi], start=False, stop=(bi_ == nb_ - 1)), bb, [b_sp[k]], inc=(bi_ == nb_ - 1))
                        S.op("act", lambda e: e.activation(out=pt[k][0:np_, lo:hi], in_=sp_[k][0:np_, lo:hi], func=AF.Exp, scale=SCALE), [b_sp[k]], [b_pt[k]])

                        def pv(k=k, pi_=pi_, vap=vap, b_v=b_v, lo=lo, hi=hi):
                            S.op("pe", lambda e: e.matmul(up[ub][:, lo:hi], vap, pt[k][0:np_, lo:hi], start=(pi_ == 0), stop=(pi_ == n - 1), skip_group_check=True), [b_v, b_pt[k]], [b_up[ub]], inc=False)
                            S.op("pe", lambda e: e.matmul(dp[ub][:, lo:hi], self.ones[0:np_, :], pt[k][0:np_, lo:hi], start=(pi_ == 0), stop=(pi_ == n - 1), skip_group_check=True), [self.b_ones, b_pt[k]], [b_dp[ub]], inc=True)
                            if pi_ == n - 1:
                                after(ub, k)
                        push(pv)

                def finish(ub, gidx, first_branch, Q, oa=None, b_oa=None):
                    cs = slice(Q * 512, (Q + 1) * 512)
                    if oa is None:
                        oa, b_oa = oacc, b_oacc
                    S.op("dve", lambda e: e.tensor_scalar(rz[:], dp[ub][:], 1e-30, None, op0=ALU.max), [b_dp[ub]], [b_rz])
                    S.op("act", lambda e: e.activation(out=rz[:], in_=rz[:], func=AF.Ln), [b_rz], [b_rz])
                    S.op("act", lambda e: e.activation(out=rz[:], in_=rz[:], func=AF.Exp, scale=-1.0), [b_rz], [b_rz])
                    S.op("dve", lambda e: e.tensor_tensor(out=tq[:], in0=up[ub][:], in1=rz[:], op=ALU.mult), [b_up[ub], b_rz], [b_tq])
                    if first_branch:
                        S.op("dve", lambda e: e.tensor_tensor(out=oa[:, cs], in0=tq[:], in1=gate[gidx][:, cs], op=ALU.mult), [b_tq, b_gate[gidx]], [b_oa])
                    else:
                        S.op("dve", lambda e: e.tensor_tensor(out=tq[:], in0=tq[:], in1=gate[gidx][:, cs], op=ALU.mult), [b_tq, b_gate[gidx]], [b_tq])
                        S.op("dve", lambda e: e.tensor_tensor(out=oa[:, cs], in0=oa[:, cs], in1=tq[:], op=ALU.add), [b_tq, b_oa], [b_oa])

                for g in range(2):
                    for hh in range(6):
                        h = g * 6 + hh
                        k2 = h % 2
                        gi = 0 if hh % 2 == 0 else 3
                        oa, b_oa = (oacc, b_oacc) if hh % 2 == 0 else (oacc_b, b_oacc_b)
                        S.dma("sp", qt[k2][:], self.qscr_d[h, 0], self.b_qscr[h], b_qt[k2])
                        S.dma("sp", gate[gi][:], self.gl_d[h * 3:h * 3 + 1, :].partition_broadcast(128), self.b_gl, b_gate[gi])
                        for Q in range(4):
                            cs = slice(Q * 512, (Q + 1) * 512)

                            def after1(ub, pk, Q=Q, cs=cs, hh=hh, gi=gi, oa=oa, b_oa=b_oa, h=h):
                                finish(ub, gi, True, Q, oa, b_oa)
                                if hh == 0:
                                    S.op("dve", lambda e: e.tensor_tensor(out=Pacc[0:127, cs], in0=pt[pk][0:127, :], in1=rz[0:127, :], op=ALU.mult), [b_pt[pk], b_rz], [b_Pacc])
                                else:
                                    S.op("dve", lambda e: e.tensor_tensor(out=tq[0:127, :], in0=pt[pk][0:127, :], in1=rz[0:127, :], op=ALU.mult), [b_pt[pk], b_rz], [b_tq])
                                    S.op("dve", lambda e: e.tensor_tensor(out=Pacc[0:127, cs], in0=Pacc[0:127, cs], in1=tq[0:127, :], op=ALU.add), [b_tq, b_Pacc], [b_Pacc])
                                if Q == 3:
                                    S.dma("sp", self.ocmp_d[h], oa[:], b_oa, self.b_ocmp[h])
                            attend(qt[k2], b_qt[k2], Q,
                                   [(kcmpT[g][:, 0:127], b_kcmpT[g], [(self.ident[0:127, 0:127], cbias[0:127, cs], [self.b_ident, b_cbias])], vcmp[g][0:127, :], b_vcmp[g])], 127, after1)
                    flush()
                    for tt in range(16):
                        S.op("pe", lambda e: e.matmul(sp_[0][:, tt * 32:(tt + 1) * 32], Pacc[0:127, tt * 128:(tt + 1) * 128], cover[0:127, :], start=True, stop=True), [b_Pacc, b_cover], [b_sp[0]], inc=(tt == 15))
                    S.op("dve", lambda e: e.tensor_tensor(out=scs[:], in0=sp_[0][:], in1=nf01[:], op=ALU.mult), [b_sp[0], b_nfa], [b_scs])
                    S.op("dve", lambda e: e.tensor_tensor(out=scs[:], in0=scs[:], in1=addc[:], op=ALU.add), [b_scs, b_nfa], [b_scs])
                    for tt in range(16):
                        S.op("dve", lambda e: e.max(out=m8[:, tt * 8:(tt + 1) * 8], in_=scs[:, tt * 32:(tt + 1) * 32]), [b_scs], [b_m8])
                    for tt in range(16):
                        S.op("dve", lambda e: e.tensor_scalar(selb[:, tt * 32:(tt + 1) * 32], scs[:, tt * 32:(tt + 1) * 32], m8[:, tt * 8 + 7:tt * 8 + 8], -30000.0, op0=ALU.is_lt, op1=ALU.mult), [b_scs, b_m8], [b_selb])
                    for half in range(2):
                        for j in range(8):
                            tt = half * 8 + j
                            S.op("pe", lambda e: e.transpose(tp[:, j, :], selb[:, tt * 32:(tt + 1) * 32], self.ident[:]), [b_selb, self.b_ident], [b_tp], inc=(j == 7))
                        S.op("act", lambda e: e.activation(out=selbT[:, half * 1024:(half + 1) * 1024], in_=tp[:], func=AF.Copy), [b_tp], [b_selbT])
                    for hh in range(6):
                        h = g * 6 + hh
                        k2 = h % 2
                        g1, g2 = (1, 2) if hh % 2 == 0 else (0, 3)
                        oa, b_oa = (oacc, b_oacc) if hh % 2 == 0 else (oacc_b, b_oacc_b)
                        S.dma("sp", qt[k2][:], self.qscr_d[h, 1], self.b_qscr[h], b_qt[k2])
                        S.dma("sp", gate[g1][:], self.gl_d[h * 3 + 1:h * 3 + 2, :].partition_broadcast(128), self.b_gl, b_gate[g1])
                        S.dma("sp", gate[g2][:], self.gl_d[h * 3 + 2:h * 3 + 3, :].partition_broadcast(128), self.b_gl, b_gate[g2])
                        S.dma("sp", oa[:], self.ocmp_d[h], self.b_ocmp[h], b_oa)
                        for Q in range(4):
                            cs = slice(Q * 512, (Q + 1) * 512)
                            pairs = []
                            for kb in range(0, 4 * Q + 4):
                                biases = [(e_all[:, kb, :], selbT[:, cs], [b_eall, b_selbT])]
                                a = kb - 4 * Q
                                if a >= 0:
                                    biases.append((self.ident[:], wbias[:, 4 + a, :], [self.b_ident, b_wbias]))
                                pairs.append((ksT[g][:, kb * 128:(kb + 1) * 128], b_ksT[g], biases, vs[g][:, kb, :], b_vs[g], (128 * max(0, a), 512)))
                            attend(qt[k2], b_qt[k2], Q, pairs, 128, lambda ub, pk, Q=Q, g1=g1, oa=oa, b_oa=b_oa: finish(ub, g1, False, Q, oa, b_oa))
                            pairs = []
                            for kb in range(max(0, 4 * Q - 4), 4 * Q + 4):
                                a = kb - 4 * Q
                                pairs.append((kwT[g][:, kb * 128:(kb + 1) * 128], b_kwT[g], [(self.ident[:], wbias[:, 4 + a, :], [self.b_ident, b_wbias])], vw[g][:, kb, :], b_vw[g],
                                              (128 * max(0, a), 128 * (min(3, a + 4) + 1))))

                            def after2(ub, pk, Q=Q, g2=g2, oa=oa, b_oa=b_oa, h=h, k2=k2):
                                finish(ub, g2, False, Q, oa, b_oa)
                                if Q == 3:
                                    S.op("act", lambda e: e.activation(out=obf[k2][:], in_=oa[:], func=AF.Copy), [b_oa], [b_obf[k2]])
                                    S.dma("sp", self.mixT_d[h], obf[k2][:], b_obf[k2], self.b_mixT[h])
                            attend(qt[k2], b_qt[k2], Q, pairs, 128, after2)
                    flush()
                S.barrier()


def make_in_map(inputs, b, consts):
    m = {
        "x": np.ascontiguousarray(inputs["x"][b]),
        "pos": np.ascontiguousarray(inputs["positions"][b:b + 1]).astype(np.int32),
        "norm_w": np.ascontiguousarray(inputs["norm_w"].reshape(24, D_)),
        "ffn_w_gate": inputs["ffn_w_gate"].reshape(8, D_, FF),
        "ffn_w_up": inputs["ffn_w_up"].reshape(8, D_, FF),
        "ffn_w_down": inputs["ffn_w_down"].reshape(8, FF, D_),
        "ev_w_in": inputs["ev_w_in"], "ev_w_out": inputs["ev_w_out"], "pool_w": inputs["pool_w"],
        "pool_scale_t": np.ascontiguousarray(inputs["pool_scale"].reshape(2, 4, 128).transpose(0, 2, 1)),
        "od_w_in": inputs["od_w_in"], "od_w_out": inputs["od_w_out"],
        "cmp_w1_k": inputs["cmp_w1_k"], "cmp_w1_v": inputs["cmp_w1_v"],
        "cmp_w2_k": inputs["cmp_w2_k"], "cmp_w2_v": inputs["cmp_w2_v"],
        "cmp_peT_k": np.ascontiguousarray(inputs["cmp_pe_k"].transpose(0, 2, 1)),
        "cmp_peT_v": np.ascontiguousarray(inputs["cmp_pe_v"].transpose(0, 2, 1)),
        "conv_wT": np.ascontiguousarray(inputs["conv_w"].reshape(2, 3, 4, 128).transpose(0, 3, 2, 1).reshape(2, 128, 12)),
    }
    m.update(consts)
    return m


def run(inputs, cfg, cores):
    k = K(cfg)
    nc = k.build()
    consts = host_consts()
    inputs = {kk: np.asarray(v) for kk, v in inputs.items()}
    in_maps = [make_in_map(inputs, b, consts) for b in cores]
    import time as _t; _t0 = _t.time()
    try:
        res = run_bass_kernel_spmd(nc, in_maps, core_ids=list(range(len(cores))))
    finally:
        print("device call seconds", _t.time() - _t0)
    return res, k


def kernel(**inputs):
    res, _ = run(inputs, {}, list(range(8)))
    return np.stack([r["y"] for r in res.results], axis=0).astype(np.float32)
```
